# Optimizing a Trainium2 kernel written in Bass

```python
import math
import jax, jax.numpy as jnp
from jax import lax
import numpy as np

D_MODEL = 1024
BATCH = 8
SEQ = 4096
DEPTH = 1

CHUNK = 64
Q_BLOCK = 128
SB_HEAD_DIM = 64
SB_HEADS = D_MODEL // 128
SB_WIDTH = SB_HEADS * SB_HEAD_DIM
SSM_WIDTH = D_MODEL // 2
SSM_GROUP = 16
SSM_GROUPS = SSM_WIDTH // SSM_GROUP
SSM_STATE = 64
N_MEM = 256
XA_HEADS = 4
XA_HEAD_DIM = D_MODEL // XA_HEADS
D_FF = 11 * D_MODEL // 4
CONV_WIDTH = 3
RMS_EPS = 1e-6
IN_WIDTH = 3 * SB_WIDTH + SSM_WIDTH + 2 * D_MODEL

kernel_name = "hybrid_stickbreak_s5_memxattn_convffn"


def rms_norm(x, gain):
    xf = x.astype(jnp.float32)
    y = xf * lax.rsqrt(jnp.mean(xf * xf, axis=-1, keepdims=True) + RMS_EPS)
    return (y * gain.astype(jnp.float32)).astype(x.dtype)


def stick_breaking_attention(q, k, v):
    b, s, _ = q.shape
    n_blk = s // Q_BLOCK
    qh = q.reshape(b, n_blk, Q_BLOCK, SB_HEADS, SB_HEAD_DIM).transpose(1, 0, 3, 2, 4)
    kh = k.reshape(b, s, SB_HEADS, SB_HEAD_DIM).transpose(0, 2, 1, 3)
    vh = v.reshape(b, s, SB_HEADS, SB_HEAD_DIM).transpose(0, 2, 1, 3)
    key_pos = jnp.arange(s, dtype=jnp.int32)
    scale = SB_HEAD_DIM ** -0.5

    def one_block(args):
        q_blk, blk = args
        z = jnp.einsum('bhqd,bhkd->bhqk', q_blk, kh).astype(jnp.float32) * scale
        q_pos = blk * Q_BLOCK + jnp.arange(Q_BLOCK, dtype=jnp.int32)
        causal = key_pos[None, :] < q_pos[:, None]
        log_fail = jnp.where(causal, jax.nn.log_sigmoid(-z), 0.0)
        log_remain = lax.cumsum(log_fail, axis=3, reverse=True) - log_fail
        w = jnp.where(causal, jnp.exp(jax.nn.log_sigmoid(z) + log_remain), 0.0)
        return jnp.einsum('bhqk,bhkd->bhqd', w.astype(vh.dtype), vh)

    out = lax.map(one_block, (qh, jnp.arange(n_blk, dtype=jnp.int32)))
    return out.transpose(1, 0, 3, 2, 4).reshape(b, s, SB_WIDTH)


def _linear_recurrence_combine(e_i, e_j):
    a_i, b_i = e_i
    a_j, b_j = e_j
    return a_j * a_i, a_j * b_i + b_j


def s5_branch(u, a_re, a_im, log_dt, b_re, b_im, c_re, c_im, d_skip, w_glu, b_glu):
    b, s, _ = u.shape
    f32 = jnp.float32
    ug = u.astype(f32).reshape(b, s, SSM_GROUPS, SSM_GROUP)
    lam = lax.complex(a_re.astype(f32), a_im.astype(f32))
    dt = jnp.exp(log_dt.astype(f32))[:, None]
    lam_bar = jnp.exp(lam * dt)
    b_mat = lax.complex(b_re.astype(f32), b_im.astype(f32))
    b_bar = ((lam_bar - 1.0) / lam)[:, :, None] * b_mat
    bu = jnp.einsum('gpc,bsgc->bsgp', b_bar, ug.astype(jnp.complex64))
    decay = jnp.broadcast_to(lam_bar, (1, s, SSM_GROUPS, SSM_STATE))
    _, states = lax.associative_scan(_linear_recurrence_combine, (decay, bu), axis=1)
    c_mat = lax.complex(c_re.astype(f32), c_im.astype(f32))
    y = jnp.einsum('gcp,bsgp->bsgc', c_mat, states).real
    y = y + d_skip.astype(f32).reshape(SSM_GROUPS, SSM_GROUP) * ug
    y = jax.nn.gelu(y.reshape(b, s, SSM_WIDTH)).astype(u.dtype)
    return y * jax.nn.sigmoid(y @ w_glu + b_glu)


def memory_cross_attention(h, mem_n, wq, wk, wv, wo):
    b, s, _ = h.shape
    m = mem_n.shape[1]
    q = (h @ wq).reshape(b, s, XA_HEADS, XA_HEAD_DIM)
    k = (mem_n @ wk).reshape(b, m, XA_HEADS, XA_HEAD_DIM)
    v = (mem_n @ wv).reshape(b, m, XA_HEADS, XA_HEAD_DIM)
    scores = jnp.einsum('bqhd,bmhd->bhqm', q, k).astype(jnp.float32) * (XA_HEAD_DIM ** -0.5)
    p = jax.nn.softmax(scores, axis=-1).astype(v.dtype)
    o = jnp.einsum('bhqm,bmhd->bqhd', p, v).reshape(b, s, D_MODEL)
    return o @ wo


def conv_ffn(h, w_up, conv_w, conv_b, w_down):
    up = h @ w_up
    up = lax.conv_general_dilated(
        up, conv_w[:, None, :].astype(up.dtype), window_strides=(1,),
        padding=[(CONV_WIDTH - 1, 0)], dimension_numbers=('NWC', 'WIO', 'NWC'),
        feature_group_count=2 * D_FF) + conv_b
    gate, val = jnp.split(up, 2, axis=-1)
    return (jax.nn.gelu(gate) * val) @ w_down


def _normal(key, shape, scale):
    return jax.random.normal(key, shape, jnp.float32) * scale


def setup_inputs(seed: int = 0) -> dict:
    key = jax.random.key(seed)
    ks = jax.random.split(key, 32)
    L, D, G, P, C = DEPTH, D_MODEL, SSM_GROUPS, SSM_STATE, SSM_GROUP
    gain = lambda k: 1.0 + _normal(k, (L, D), 0.05)
    return {
        "x": _normal(ks[0], (BATCH, SEQ, D), 1.0),
        "mem": _normal(ks[1], (BATCH, N_MEM, D), 1.0),
        "norm_mix_pre": gain(ks[2]),
        "norm_mix_post": gain(ks[3]),
        "w_in": _normal(ks[4], (L, D, IN_WIDTH), D ** -0.5),
        "b_gate": _normal(ks[5], (L, 2 * D), 0.01),
        "ssm_a_re": -0.5 + _normal(ks[6], (L, G, P), 0.01),
        "ssm_a_im": jnp.pi * jnp.arange(P, dtype=jnp.float32)[None, None, :] + _normal(ks[7], (L, G, P), 0.01),
        "ssm_log_dt": jax.random.uniform(ks[8], (L, G), jnp.float32, math.log(1e-3), math.log(1e-1)),
        "ssm_b_re": _normal(ks[9], (L, G, P, C), (2 * C) ** -0.5),
        "ssm_b_im": _normal(ks[10], (L, G, P, C), (2 * C) ** -0.5),
        "ssm_c_re": _normal(ks[11], (L, G, C, P), (2 * P) ** -0.5),
        "ssm_c_im": _normal(ks[12], (L, G, C, P), (2 * P) ** -0.5),
        "ssm_d": _normal(ks[13], (L, SSM_WIDTH), 1.0),
        "ssm_w_glu": _normal(ks[14], (L, SSM_WIDTH, SSM_WIDTH), SSM_WIDTH ** -0.5),
        "ssm_b_glu": _normal(ks[15], (L, SSM_WIDTH), 0.01),
        "w_branch_attn": _normal(ks[16], (L, SB_WIDTH, D), SB_WIDTH ** -0.5),
        "w_branch_ssm": _normal(ks[17], (L, SSM_WIDTH, D), SSM_WIDTH ** -0.5),
        "w_out": _normal(ks[18], (L, D, D), D ** -0.5),
        "norm_xa_pre": gain(ks[19]),
        "norm_xa_post": gain(ks[20]),
        "norm_mem": gain(ks[21]),
        "xa_wq": _normal(ks[22], (L, D, D), D ** -0.5),
        "xa_wk": _normal(ks[23], (L, D, D), D ** -0.5),
        "xa_wv": _normal(ks[24], (L, D, D), D ** -0.5),
        "xa_wo": _normal(ks[25], (L, D, D), D ** -0.5),
        "norm_ffn_pre": gain(ks[26]),
        "norm_ffn_post": gain(ks[27]),
        "ffn_w_up": _normal(ks[28], (L, D, 2 * D_FF), D ** -0.5),
        "ffn_conv_w": _normal(ks[29], (L, CONV_WIDTH, 2 * D_FF), CONV_WIDTH ** -0.5),
        "ffn_conv_b": _normal(ks[30], (L, 2 * D_FF), 0.01),
        "ffn_w_down": _normal(ks[31], (L, D_FF, D), D_FF ** -0.5),
    }


def reference(x, mem, norm_mix_pre, norm_mix_post, w_in, b_gate,
              ssm_a_re, ssm_a_im, ssm_log_dt, ssm_b_re, ssm_b_im, ssm_c_re, ssm_c_im,
              ssm_d, ssm_w_glu, ssm_b_glu, w_branch_attn, w_branch_ssm, w_out,
              norm_xa_pre, norm_xa_post, norm_mem, xa_wq, xa_wk, xa_wv, xa_wo,
              norm_ffn_pre, norm_ffn_post, ffn_w_up, ffn_conv_w, ffn_conv_b, ffn_w_down):
    splits = (SB_WIDTH, 2 * SB_WIDTH, 3 * SB_WIDTH, 3 * SB_WIDTH + SSM_WIDTH)
    for l in range(DEPTH):
        h = rms_norm(x, norm_mix_pre[l])
        proj = h @ w_in[l]
        q, k, v, u, gate_logits = jnp.split(proj, splits, axis=-1)
        gate_attn, gate_ssm = jnp.split(jax.nn.sigmoid(gate_logits + b_gate[l]), 2, axis=-1)
        o_attn = stick_breaking_attention(q, k, v)
        o_ssm = s5_branch(u, ssm_a_re[l], ssm_a_im[l], ssm_log_dt[l], ssm_b_re[l], ssm_b_im[l],
                          ssm_c_re[l], ssm_c_im[l], ssm_d[l], ssm_w_glu[l], ssm_b_glu[l])
        merged = gate_attn * (o_attn @ w_branch_attn[l]) + gate_ssm * (o_ssm @ w_branch_ssm[l])
        x = x + rms_norm(merged @ w_out[l], norm_mix_post[l])
        h = rms_norm(x, norm_xa_pre[l])
        mem_n = rms_norm(mem, norm_mem[l])
        xa = memory_cross_attention(h, mem_n, xa_wq[l], xa_wk[l], xa_wv[l], xa_wo[l])
        x = x + rms_norm(xa, norm_xa_post[l])
        h = rms_norm(x, norm_ffn_pre[l])
        f = conv_ffn(h, ffn_w_up[l], ffn_conv_w[l], ffn_conv_b[l], ffn_w_down[l])
        x = x + rms_norm(f, norm_ffn_post[l])
    return x
```

```python
import contextlib
import numpy as np
import concourse.bass as bass
import concourse.mybir as mybir
from concourse.bass_utils import run_bass_kernel_spmd

F32 = mybir.dt.float32
BF16 = mybir.dt.bfloat16
I32 = mybir.dt.int32
AF = mybir.ActivationFunctionType
ALU = mybir.AluOpType

S = 4096
D = 1024
TT = 512
NTILES = S // TT
DFF = 2816
NMEM = 256
EPS = 1e-6
TWO_PI = 6.283185307179586
SEM_LIMIT = 30000
ARENA_BYTES = 43520


class Buf:
    __slots__ = ("name", "w", "r", "psum")

    def __init__(self, name):
        self.name = name
        self.w = None
        self.r = []
        self.psum = name.startswith("ps")


class _Rec:
    def __init__(self):
        self.call = None

    def __getattr__(self, name):
        def f(*a, **k):
            self.call = (name, a, k)
            return self
        return f


def _record(fn):
    r = _Rec()
    fn(r)
    assert r.call is not None
    return r.call


class ArenaScope:
    def __init__(self, arena):
        self.arena = arena

    def __enter__(self):
        self.mark = self.arena.top
        return self

    def __exit__(self, *a):
        self.arena.top = self.mark
        return False


class Arena:
    def __init__(self, tensor, nbytes):
        self.t = tensor
        self.nbytes = nbytes
        self.top = 0
        self.peak = 0

    def scope(self):
        return ArenaScope(self)

    def alloc(self, name, shape, dt):
        esz = 4 if dt in (F32, I32) else 2
        n = 1
        for s_ in shape[1:]:
            n *= s_
        nb_ = (n * esz + 31) // 32 * 32
        off = self.top
        self.top += nb_
        self.peak = max(self.peak, self.top)
        assert self.top <= self.nbytes, "arena overflow %s: %d > %d" % (name, self.top, self.nbytes)
        ap = self.t[0:shape[0], off // 2:(off + n * esz) // 2]
        if esz == 4:
            ap = ap.bitcast(dt)
        if len(shape) == 3:
            ap = ap.rearrange("p (a b) -> p a b", a=shape[1])
        elif len(shape) == 4:
            ap = ap.rearrange("p (a b c) -> p a b c", a=shape[1], b=shape[2])
        return ap


class FW:
    ENG = ["pe", "act", "dve", "pool", "sp"]

    def __init__(self, nc, stack):
        self.nc = nc
        self.stack = stack
        self.ops = {e: [] for e in self.ENG}
        self.sem = {e: stack.enter_context(nc.semaphore("c_" + e + "0")) for e in self.ENG}
        self.semn = {e: 0 for e in self.ENG}
        self.cnt = {e: 0 for e in self.ENG}
        self.last = {e: None for e in self.ENG}
        self.waited = {e: {} for e in self.ENG}
        self.pend = {e: ([], []) for e in self.ENG}
        self.dsem = {}
        self.nd = 0

    def new_dsem(self):
        self.nd += 1
        return self.stack.enter_context(self.nc.semaphore("d%d" % self.nd))

    def _wait(self, e, ev):
        if ev is None or ev == "PEND":
            return
        sem, val = ev
        key = id(sem)
        if key in self.dsem:
            val = max(val, self.dsem[key])
        if self.waited[e].get(key, 0) >= val:
            return
        self.waited[e][key] = val
        self.ops[e].append(("wait", sem, val))

    def deps(self, e, reads, writes):
        for b in reads:
            if b.w == "PEND":
                assert b in self.pend[e][1], "read of pending write: " + b.name
            else:
                self._wait(e, b.w)
            if b.psum:
                for ev in b.r:
                    self._wait(e, ev)
        for b in writes:
            if b.w == "PEND":
                assert b in self.pend[e][1], "write of pending write: " + b.name
            else:
                self._wait(e, b.w)
            for ev in b.r:
                self._wait(e, ev)
            for e2 in self.ENG:
                if e2 != e:
                    assert b not in self.pend[e2][0], "WAR on pending read: " + b.name

    def op(self, e, fn, reads=(), writes=(), signal=True):
        call = _record(fn)
        self.deps(e, reads, writes)
        pr, pw = self.pend[e]
        pr.extend(reads)
        pw.extend(writes)
        for b in writes:
            b.w = "PEND"
            b.r = []
        if signal:
            if self.cnt[e] >= SEM_LIMIT:
                self.semn[e] += 1
                self.sem[e] = self.stack.enter_context(self.nc.semaphore("c_%s%d" % (e, self.semn[e])))
                self.cnt[e] = 0
            self.cnt[e] += 1
            sem, val = self.sem[e], self.cnt[e]
            self.ops[e].append(("op", call, sem, 1))
            ev = (sem, val)
            self.last[e] = ev
            for b in pw:
                b.w = ev
                b.r = []
            for b in pr:
                if b.w != ev:
                    b.r.append(ev)
            self.pend[e] = ([], [])
        else:
            self.ops[e].append(("op", call, None, 0))

    def flush(self, e):
        pr, pw = self.pend[e]
        if not pr and not pw:
            return
        item = self.ops[e][-1]
        assert item[0] == "op" and item[2] is None
        if self.cnt[e] >= SEM_LIMIT:
            self.semn[e] += 1
            self.sem[e] = self.stack.enter_context(self.nc.semaphore("c_%s%d" % (e, self.semn[e])))
            self.cnt[e] = 0
        self.cnt[e] += 1
        sem, val = self.sem[e], self.cnt[e]
        self.ops[e][-1] = ("op", item[1], sem, 1)
        ev = (sem, val)
        self.last[e] = ev
        for b in pw:
            b.w = ev
            b.r = []
        for b in pr:
            if b.w != ev:
                b.r.append(ev)
        self.pend[e] = ([], [])

    def dma(self, e, fn, reads=(), writes=(), dsem=None):
        call = _record(fn)
        self.deps(e, reads, writes)
        c = self.dsem.get(id(dsem), 0) + 16
        self.dsem[id(dsem)] = c
        self.ops[e].append(("op", call, dsem, 16))
        ev = (dsem, c)
        for b in writes:
            b.w = ev
            b.r = []
        for b in reads:
            b.r.append(ev)
        return ev

    def barrier(self, engs=("pe", "act", "dve", "pool"), with_sp=False):
        for e in engs:
            assert not self.pend[e][0] and not self.pend[e][1], "pending at barrier " + e
        for e in engs:
            if e == "pe":
                continue
            for e2 in engs:
                if e2 != e:
                    self._wait(e, self.last[e2])
        if with_sp:
            for e2 in engs:
                self._wait("sp", self.last[e2])

    def replay(self, e, eng):
        for item in self.ops[e]:
            if item[0] == "wait":
                eng.wait_ge(item[1], item[2])
            else:
                name, a, k = item[1]
                inst = getattr(eng, name)(*a, **k)
                if item[2] is not None:
                    inst.then_inc(item[2], item[3])


def build(ntiles=NTILES, dbg=None):
    nc = bass.Bass("TRN2", target_bir_lowering=False)
    dbg_specs = {}

    def din(name, shape):
        return nc.dram_tensor(name, list(shape), F32, kind="ExternalInput").ap()

    x_d = din("x", [S, D])
    mem_d = din("mem", [NMEM, D])
    g_mix_pre = din("norm_mix_pre", [1, D]); g_mix_post = din("norm_mix_post", [1, D])
    w_in_d = din("w_in", [D, 4096]); b_gate_d = din("b_gate", [1, 2048])
    a_re_d = din("ssm_a_re", [32, 64]); a_im_d = din("ssm_a_im", [32, 64]); ldt_d = din("ssm_log_dt", [1, 32])
    b_re_d = din("ssm_b_re", [32, 64, 16]); b_im_d = din("ssm_b_im", [32, 64, 16])
    c_re_d = din("ssm_c_re", [32, 16, 64]); c_im_d = din("ssm_c_im", [32, 16, 64])
    d_d = din("ssm_d", [1, 512]); w_glu_d = din("ssm_w_glu", [512, 512]); b_glu_d = din("ssm_b_glu", [1, 512])
    w_ba_d = din("w_branch_attn", [512, D]); w_bs_d = din("w_branch_ssm", [512, D]); w_out_d = din("w_out", [D, D])
    g_xa_pre = din("norm_xa_pre", [1, D]); g_xa_post = din("norm_xa_post", [1, D]); g_mem = din("norm_mem", [1, D])
    wq_d = din("xa_wq", [D, D]); wk_d = din("xa_wk", [D, D]); wv_d = din("xa_wv", [D, D]); wo_d = din("xa_wo", [D, D])
    g_ffn_pre = din("norm_ffn_pre", [1, D]); g_ffn_post = din("norm_ffn_post", [1, D])
    up_d = din("ffn_w_up", [D, 2 * DFF]); cw_d = din("ffn_conv_w", [3, 2 * DFF]); cb_d = din("ffn_conv_b", [1, 2 * DFF])
    down_d = din("ffn_w_down", [DFF, D])
    out_d = nc.dram_tensor("out", [S, D], F32, kind="ExternalOutput").ap()

    def scratch(name, shape):
        return nc.dram_tensor(name, list(shape), BF16, kind="Internal").ap()

    wsrc = {"wk": wk_d, "wv": wv_d, "w_in": w_in_d, "w_glu": w_glu_d, "w_ba": w_ba_d, "w_bs": w_bs_d,
            "w_out": w_out_d, "wq": wq_d, "wo": wo_d, "up": up_d, "down": down_d}
    wbf = {k: scratch(k + "_bf", v.shape) for k, v in wsrc.items()}

    with contextlib.ExitStack() as st:
        E = st.enter_context
        fw = FW(nc, st)
        bufs = {}

        def B(name):
            if name not in bufs:
                bufs[name] = Buf(name)
            return bufs[name]

        def sb(stack, name, shape, dt):
            if isinstance(stack, ArenaScope):
                return stack.arena.alloc(name, list(shape), dt)
            return stack.enter_context(nc.sbuf_tensor(name, list(shape), dt))

        kT = sb(st, "kT", [128, 4, S], BF16)
        vtok = sb(st, "vtok", [128, 32, 512], BF16)
        memKT = sb(st, "memKT", [128, 8, NMEM], BF16)
        memV = sb(st, "memV", [128, 2, D], BF16)
        ident_bf = sb(st, "ident_bf", [128, 128], BF16)
        ident_f = sb(st, "ident_f", [128, 128], F32)
        ntri = sb(st, "ntri", [128, 128], BF16)
        nones = sb(st, "nones", [128, 128], BF16)
        maskb = sb(st, "maskb", [128, 512], BF16)
        nbglu = sb(st, "nbglu", [128, 4], F32)
        pcols = sb(st, "pcols", [128, 96], F32)
        gcols = pcols[:, 0:32].rearrange("p (a b) -> p a b", a=4)
        bgate = pcols[:, 32:48]
        bglu = pcols[:, 48:52]
        convb = pcols[:, 52:96]
        convw = sb(st, "convw", [128, 3, 44], F32)
        halo = sb(st, "halo", [128, 44, 2], F32)
        Gm = sb(st, "Gm", [128, 16, 128], BF16)
        BcRe = sb(st, "BcRe", [128, 16, 128], BF16)
        BcIm = sb(st, "BcIm", [128, 16, 128], BF16)
        CmRe = sb(st, "CmRe", [128, 16, 128], BF16)
        CmIm = sb(st, "CmIm", [128, 16, 128], BF16)
        Ctab = sb(st, "Ctab", [128, 16, 128], F32)
        Stab = sb(st, "Stab", [128, 16, 128], F32)
        rho4 = sb(st, "rho4", [128, 16], F32)
        dvec = sb(st, "dvec", [128, 16], F32)
        XcRe = sb(st, "XcRe", [128, 16], F32)
        XcIm = sb(st, "XcIm", [128, 16], F32)
        maskG = sb(st, "maskG", [128, 128], F32)
        kk = sb(st, "kk", [128, 128], F32)
        gbuf = sb(st, "gbuf", [128, D], F32)
        xres = sb(st, "xres", [128, 4, D], F32)
        hT = sb(st, "hT", [128, 8, TT], BF16)
        ring = [sb(st, "ring%d" % i, [128, 4096], BF16) for i in range(3)]
        ss = sb(st, "ss", [128, 8], F32)
        ssum = sb(st, "ssum", [128, 4], F32)
        ssN = sb(st, "ssN", [128, 4], F32)
        lnN = sb(st, "lnN", [128, 4], F32)
        rstdN = sb(st, "rstdN", [128, 4], F32)
        lnv = sb(st, "lnv", [128, 4], F32)
        rstd = sb(st, "rstd", [128, 4], F32)
        arena_t = sb(st, "arena", [128, ARENA_BYTES // 2], BF16)
        arena = Arena(arena_t, ARENA_BYTES)
        pp = [E(nc.psum_tensor("pp%d" % i, [128, 1024], F32)) for i in range(4)]
        ps = [pp[i // 2][:, 512 * (i % 2):512 * (i % 2) + 512] for i in range(8)]
        PSB = [B("ps%d" % i) for i in range(8)]

        XN = ["xres%d" % tb for tb in range(4)]
        ds_xt = [fw.new_dsem() for _ in range(4)]
        ds_ot = [fw.new_dsem() for _ in range(4)]
        ds_x = fw.new_dsem(); ds_out = fw.new_dsem(); ds_g = fw.new_dsem(); ds_misc = fw.new_dsem()
        ds_ring = [fw.new_dsem() for _ in range(3)]
        ds_dbg = fw.new_dsem()

        out_events = []

        def dump(name, ap, bname, shape, dt=F32):
            if dbg is None or name not in dbg:
                return
            o = nc.dram_tensor("dbg_" + name, list(shape), dt, kind="ExternalOutput").ap()
            dbg_specs[name] = (list(shape), dt)
            bl = bname if isinstance(bname, list) else [bname]
            ev = fw.dma("sp", lambda e: e.dma_start(out=o, in_=ap), reads=[B(b_) for b_ in bl], writes=[B("dbg_" + name)], dsem=fw.new_dsem())
            out_events.append(ev)
            for e_ in ("pe", "act", "dve", "pool"):
                fw._wait(e_, ev)

        def rows512(ap2d):
            return ap2d.rearrange("r (a c) -> (r a) c", c=512)

        pieces = []
        for c in range(4):
            pieces.append(("w_in_%d" % c, [(w_in_d[:, 512 * c:512 * c + 512], wbf["w_in"][:, 512 * c:512 * c + 512])]))
        for k in ("w_glu", "w_ba", "w_bs"):
            pieces.append((k, [(rows512(wsrc[k]), rows512(wbf[k]))]))
        for c in (4, 6, 5, 7):
            pieces.append(("w_in_%d" % c, [(w_in_d[:, 512 * c:512 * c + 512], wbf["w_in"][:, 512 * c:512 * c + 512])]))
        for k in ("w_out", "wk", "wv", "wq", "wo"):
            pieces.append((k, [(rows512(wsrc[k]), rows512(wbf[k]))]))
        for p_ in range(4):
            nj = min(3, 11 - 3 * p_)
            pieces.append(("up_g%d" % p_, [(up_d[:, 768 * p_:768 * p_ + 256 * nj], wbf["up"][:, 768 * p_:768 * p_ + 256 * nj])]))
            pieces.append(("up_v%d" % p_, [(up_d[:, DFF + 768 * p_:DFF + 768 * p_ + 256 * nj], wbf["up"][:, DFF + 768 * p_:DFF + 768 * p_ + 256 * nj])]))
        for g_ in range(3):
            r0, r1 = 1024 * g_, min(1024 * (g_ + 1), DFF)
            pieces.append(("down_%d" % g_, [(rows512(down_d[r0:r1, :]), rows512(wbf["down"][r0:r1, :]))]))
        def piece_name(key, k0, c0):
            if key == "w_in":
                return "w_in_%d" % (c0 // 512)
            if key == "up":
                return ("up_g%d" if c0 < DFF else "up_v%d") % (((c0 % DFF) // 256) // 3)
            if key == "down":
                return "down_%d" % (k0 // 8)
            return key

        ring_state = {"n": 0}
        RB = [B("ring%d" % i) for i in range(3)]

        def wview(key, k0, k1, c0, c1):
            return wbf[key].rearrange("(kc p) n -> p kc n", p=128)[:, k0:k1, c0:c1]

        def ring_load(parts):
            n = ring_state["n"]
            ring_state["n"] += 1
            si = n % 3
            views = []
            off = 0
            for (key, k0, k1, c0, c1) in parts:
                nk, ncol = k1 - k0, c1 - c0
                dstv = ring[si][:, off:off + nk * ncol].rearrange("p (k n) -> p k n", k=nk)
                srcv = wview(key, k0, k1, c0, c1)
                fw.dma("sp", lambda e, dstv=dstv, srcv=srcv: e.dma_start(out=dstv, in_=srcv),
                       reads=[B("wbf_" + piece_name(key, k0, c0))], writes=[RB[si]], dsem=ds_ring[si])
                views.append(dstv)
                off += nk * ncol
            assert off <= 4096
            return si, views

        def tile_groups(ti_):
            g = []
            for c in range(4):
                g.append([("w_in", 0, 8, 512 * c, 512 * c + 512)])
            g.append([("w_glu", 0, 4, 0, 512)])
            for pr in range(4):
                g.append([("w_in", 0, 8, 2048 + 256 * pr, 2048 + 256 * pr + 256), ("w_in", 0, 8, 3072 + 256 * pr, 3072 + 256 * pr + 256)])
                g.append([("w_ba", 0, 4, 256 * pr, 256 * pr + 256), ("w_bs", 0, 4, 256 * pr, 256 * pr + 256)])
            for half in range(2):
                g.append([("w_out", 0, 8, 512 * half, 512 * half + 512)])
            if ti_ == 0:
                for half in range(2):
                    g.append([("wk", 0, 8, 512 * half, 512 * half + 512)])
                for half in range(2):
                    g.append([("wv", 0, 8, 512 * half, 512 * half + 512)])
            for half in range(2):
                g.append([("wq", 0, 8, 512 * half, 512 * half + 512)])
            for half in range(2):
                g.append([("wo", 0, 8, 512 * half, 512 * half + 512)])
            for j in range(11):
                g.append([("up", 0, 8, 256 * j, 256 * j + 256), ("up", 0, 8, DFF + 256 * j, DFF + 256 * j + 256)])
            for (k0, k1) in ((0, 8), (8, 16), (16, 22)):
                for half in range(2):
                    g.append([("down", k0, k1, 512 * half, 512 * half + 512)])
            return g

        groups = []
        for ti_ in range(ntiles):
            groups.extend(tile_groups(ti_))
        gstate = {"issued": 0, "used": 0, "loaded": {}}

        def issue_loads(upto):
            while gstate["issued"] < min(upto, len(groups)):
                gi = gstate["issued"]
                gstate["loaded"][gi] = ring_load(groups[gi])
                gstate["issued"] += 1

        def next_group():
            gi = gstate["used"]
            issue_loads(gi + 1)
            si, views = gstate["loaded"].pop(gi)
            gstate["used"] += 1
            return si, views

        def group_done():
            issue_loads(gstate["used"] + 2)

        bank_rr = {"i": 0}

        def nb():
            i = bank_rr["i"]
            bank_rr["i"] = (i + 1) % 8
            return i

        evac_rr = {"i": 0}

        def evac_copy(out_ap, in_ap, reads, writes, scale=None, eng=None):
            if eng is None:
                eng = "act" if evac_rr["i"] % 2 == 0 else "dve"
                evac_rr["i"] += 1
            if eng == "act":
                if scale is None:
                    fw.op("act", lambda e: e.activation(out=out_ap, in_=in_ap, func=AF.Copy), reads=reads, writes=writes)
                else:
                    fw.op("act", lambda e: e.activation(out=out_ap, in_=in_ap, func=AF.Copy, scale=scale), reads=reads, writes=writes)
            else:
                if scale is None:
                    fw.op("dve", lambda e: e.tensor_copy(out=out_ap, in_=in_ap), reads=reads, writes=writes)
                else:
                    fw.op("dve", lambda e: e.tensor_scalar(out=out_ap, in0=in_ap, scalar1=float(scale), scalar2=None, op0=ALU.mult),
                          reads=reads, writes=writes)

        sig = fw.flush

        def tt(eng, out_ap, a, b, op, reads, writes):
            fw.op(eng, lambda e: e.tensor_tensor(out=out_ap, in0=a, in1=b, op=op), reads=reads, writes=writes)

        with nc.allow_non_contiguous_dma(reason="small param layouts"):
            fw.op("pool", lambda e: e.memset(ident_f[:], 1.0), writes=[B("ident_f")])
            fw.op("pool", lambda e: e.affine_select(out=ident_f[:], in_=ident_f[:], pattern=[[-1, 128]], compare_op=ALU.is_equal,
                                                    fill=0.0, base=0, channel_multiplier=1), reads=[B("ident_f")], writes=[B("ident_f")])
            fw.op("dve", lambda e: e.tensor_copy(out=ident_bf[:], in_=ident_f[:]), reads=[B("ident_f")], writes=[B("ident_bf")])
            fw.op("pool", lambda e: e.memset(nones[:], -1.0), writes=[B("nones")])
            with arena.scope() as ph:
                tmpf = sb(ph, "c_tmpf", [128, 4, 512], F32)
                fw.op("pool", lambda e: e.memset(tmpf[:, 0, 0:128], -1.0), writes=[B("c_tmpf")])
                fw.op("pool", lambda e: e.affine_select(out=tmpf[:, 0, 0:128], in_=tmpf[:, 0, 0:128], pattern=[[-1, 128]], compare_op=ALU.is_ge,
                                                        fill=0.0, base=0, channel_multiplier=1), reads=[B("c_tmpf")], writes=[B("c_tmpf")])
                fw.op("dve", lambda e: e.tensor_copy(out=ntri[:], in_=tmpf[:, 0, 0:128]), reads=[B("c_tmpf")], writes=[B("ntri")])
                fw.op("pool", lambda e: e.memset(tmpf[:, 1, :], 0.0), writes=[B("c_tmpf1")])
                fw.op("pool", lambda e: e.affine_select(out=tmpf[:, 1, :], in_=tmpf[:, 1, :], pattern=[[1, 512]], compare_op=ALU.is_gt,
                                                        fill=-30000.0, base=0, channel_multiplier=-1), reads=[B("c_tmpf1")], writes=[B("c_tmpf1")])
                fw.op("dve", lambda e: e.tensor_copy(out=maskb[:], in_=tmpf[:, 1, :]), reads=[B("c_tmpf1")], writes=[B("maskb")])
                fw.barrier(with_sp=True)
            fw.op("pool", lambda e: e.memset(maskG[:], 1.0), writes=[B("maskG")])
            fw.op("pool", lambda e: e.affine_select(out=maskG[:].rearrange("p (i c) -> p i c", i=4), in_=maskG[:].rearrange("p (i c) -> p i c", i=4), pattern=[[32, 4], [0, 32]],
                                                    compare_op=ALU.is_ge, fill=0.0, base=31, channel_multiplier=-1), reads=[B("maskG")], writes=[B("maskG")])
            with arena.scope() as phk_:
                kki = sb(phk_, "c_kki", [128, 128], I32)
                fw.op("pool", lambda e: e.iota(kki[:], pattern=[[4, 128]], base=4, channel_multiplier=0), writes=[B("c_kki")])
                fw.op("dve", lambda e: e.tensor_copy(out=kk[:], in_=kki[:]), reads=[B("c_kki")], writes=[B("kk")])
                fw.barrier(with_sp=True)
            fw.op("pool", lambda e: e.memset(halo[:], 0.0), writes=[B("halo")])
            fw.op("pool", lambda e: e.memset(XcRe[:], 0.0), writes=[B("XcRe")])
            fw.op("pool", lambda e: e.memset(XcIm[:], 0.0), writes=[B("XcIm")])
            cast_evs = []
            for (pname, parts) in pieces:
                assert len(parts) == 1
                sv, dv = parts[0]
                assert sv.shape[0] <= 2048
                if len(cast_evs) >= 2:
                    fw._wait("pool", cast_evs[-2])
                ev_ = fw.dma("pool", lambda e, sv=sv, dv=dv: e.dma_start(out=dv, in_=sv), writes=[B("wbf_" + pname)], dsem=fw.new_dsem())
                cast_evs.append(ev_)

            if True:
                r2f = ring[2][:].bitcast(F32)
                S1 = r2f[0:96, 0:128]; S2 = r2f[0:88, 128:256]; S3 = r2f[0:44, 256:384]; S4 = r2f[0:16, 384:512]
                dsp = fw.new_dsem()
                for i, g in enumerate([g_mix_pre, g_xa_pre, g_ffn_pre, g_mem]):
                    fw.dma("sp", lambda e, i=i, g=g: e.dma_start(out=S1[8 * i:8 * i + 8, :], in_=g.rearrange("o (k p) -> (o k) p", p=128)), writes=[B("pS1_%d" % i)], dsem=dsp)
                fw.dma("sp", lambda e: e.dma_start(out=S1[32:48, :], in_=b_gate_d.rearrange("o (k p) -> (o k) p", p=128)), writes=[B("pS1_4")], dsem=dsp)
                fw.dma("sp", lambda e: e.dma_start(out=S1[48:52, :], in_=b_glu_d.rearrange("o (k p) -> (o k) p", p=128)), writes=[B("pS1_5")], dsem=dsp)
                fw.dma("sp", lambda e: e.dma_start(out=S1[52:96, :], in_=cb_d.rearrange("o (c p) -> (o c) p", p=128)), writes=[B("pS1_6")], dsem=dsp)
                cwv = cw_d.rearrange("t (c p) -> t c p", p=128)
                for t in range(2):
                    fw.dma("sp", lambda e, t=t: e.dma_start(out=S2[44 * t:44 * t + 44, :], in_=cwv[t]), writes=[B("pS2_%d" % t)], dsem=dsp)
                fw.dma("sp", lambda e: e.dma_start(out=S3[:, :], in_=cwv[2]), writes=[B("pS3")], dsem=dsp)
                for i in range(4):
                    fw.dma("sp", lambda e, i=i: e.dma_start(out=S4[:, 32 * i:32 * i + 32], in_=d_d.rearrange("o (q c) -> (o q) c", c=32)), writes=[B("pS4_%d" % i)], dsem=dsp)
                bp = 7
                fw.op("pe", lambda e: e.transpose(out=ps[bp][:, 0:96], in_=S1[0:96, :], identity=ident_f[0:96, 0:96]),
                      reads=[B("pS1_%d" % i) for i in range(7)] + [B("ident_f")], writes=[PSB[bp]], signal=False)
                fw.op("pe", lambda e: e.transpose(out=ps[bp][:, 96:184], in_=S2[0:88, :], identity=ident_f[0:88, 0:88]),
                      reads=[B("pS2_0"), B("pS2_1"), B("ident_f")], writes=[PSB[bp]], signal=False)
                fw.op("pe", lambda e: e.transpose(out=ps[bp][:, 184:228], in_=S3[0:44, :], identity=ident_f[0:44, 0:44]),
                      reads=[B("pS3"), B("ident_f")], writes=[PSB[bp]], signal=False)
                fw.op("pe", lambda e: e.transpose(out=ps[bp][:, 228:244], in_=S4[0:16, :], identity=ident_f[0:16, 0:16]),
                      reads=[B("pS4_%d" % i) for i in range(4)] + [B("ident_f"), RB[2]], writes=[PSB[bp]], signal=True)
                fw.op("act", lambda e: e.activation(out=pcols[:], in_=ps[bp][:, 0:96], func=AF.Copy), reads=[PSB[bp]],
                      writes=[B("gcols"), B("bgate"), B("bglu"), B("convb")])
                fw.op("act", lambda e: e.activation(out=convw[:].rearrange("p t c -> p (t c)"), in_=ps[bp][:, 96:228], func=AF.Copy), reads=[PSB[bp]], writes=[B("convw")])
                fw.op("act", lambda e: e.activation(out=dvec[:], in_=ps[bp][:, 228:244], func=AF.Copy), reads=[PSB[bp]], writes=[B("dvec")])
                fw.op("dve", lambda e: e.tensor_scalar(out=nbglu[:], in0=bglu[:], scalar1=-1.0, scalar2=None, op0=ALU.mult), reads=[B("bglu")], writes=[B("nbglu")])
            ssm_setup(nc, fw, B, sb, ps, PSB, arena, locals())

        def rms_to_hT(ph, src, srcbuf, ntb, gi, dst, dstbuf):
            xn = sb(ph, "xn", [128, ntb, D], BF16)
            junk = sb(ph, "junk", [128, D], BF16)
            sbn = srcbuf if isinstance(srcbuf, list) else [srcbuf] * ntb
            for tb in range(ntb):
                fw.op("act", lambda e, tb=tb: e.activation(out=junk[:], in_=src[:, tb, :], func=AF.Square, accum_out=ss[:, tb:tb + 1]),
                      reads=[B(sbn[tb])], writes=[B("junk"), B("ss")])
            fw.op("act", lambda e: e.activation(out=lnv[:, 0:ntb], in_=ss[:, 0:ntb], func=AF.Ln, scale=1.0 / D, bias=EPS),
                  reads=[B("ss")], writes=[B("lnv")])
            fw.op("act", lambda e: e.activation(out=rstd[:, 0:ntb], in_=lnv[:, 0:ntb], func=AF.Exp, scale=-0.5),
                  reads=[B("lnv")], writes=[B("rstd")])
            for tb in range(ntb):
                fw.op("dve", lambda e, tb=tb: e.tensor_scalar(out=xn[:, tb, :], in0=src[:, tb, :], scalar1=rstd[:, tb:tb + 1], scalar2=None, op0=ALU.mult),
                      reads=[B(sbn[tb]), B("rstd")], writes=[B("xn")])
            for kc in range(8):
                bi = nb()
                pv = ps[bi][:].bitcast(BF16)
                for tb in range(ntb):
                    fw.op("pe", lambda e, tb=tb, kc=kc, pv=pv: e.transpose(out=pv[:, tb * 128:(tb + 1) * 128], in_=xn[:, tb, kc * 128:(kc + 1) * 128], identity=ident_bf[:]),
                          reads=[B("xn"), B("ident_bf")], writes=[PSB[bi]], signal=(tb == ntb - 1))
                if kc % 2 == 0:
                    fw.op("act", lambda e, kc=kc, pv=pv: e.activation(out=dst[:, kc, 0:ntb * 128], in_=pv[:, 0:ntb * 128], func=AF.Copy, scale=gcols[:, gi, kc:kc + 1]),
                          reads=[PSB[bi], B("gcols")], writes=[B(dstbuf)])
                else:
                    fw.op("dve", lambda e, kc=kc, pv=pv: e.tensor_scalar(out=dst[:, kc, 0:ntb * 128], in0=pv[:, 0:ntb * 128], scalar1=gcols[:, gi, kc:kc + 1], scalar2=None, op0=ALU.mult),
                          reads=[PSB[bi], B("gcols")], writes=[B(dstbuf)])

        def post_norm_residual(ph, gain_d):
            tmpa = sb(ph, "pn_a", [128, 512], F32)
            tmpb = sb(ph, "pn_b", [128, 512], F32)
            junk = sb(ph, "pn_junk", [128, 512], BF16)
            fw.dma("sp", lambda e: e.dma_start(out=gbuf[:], in_=gain_d[0:1, :].to_broadcast([128, D])), writes=[B("gbuf")], dsem=ds_g)
            for tb in range(4):
                for half in range(2):
                    bi = 2 * tb + half
                    fw.op("act", lambda e, bi=bi: e.activation(out=junk[:], in_=ps[bi][:], func=AF.Square, accum_out=ss[:, bi:bi + 1]),
                          reads=[PSB[bi]], writes=[B("pn_junk"), B("ss")])
            ssv = ss[:].rearrange("p (t h) -> p t h", h=2)
            tt("dve", ssum[:], ssv[:, :, 0], ssv[:, :, 1], ALU.add, [B("ss")], [B("ssum")])
            fw.op("act", lambda e: e.activation(out=lnv[:], in_=ssum[:], func=AF.Ln, scale=1.0 / D, bias=EPS), reads=[B("ssum")], writes=[B("lnv")])
            fw.op("act", lambda e: e.activation(out=rstd[:], in_=lnv[:], func=AF.Exp, scale=-0.5), reads=[B("lnv")], writes=[B("rstd")])
            k = 0
            for tb in range(4):
                for half in range(2):
                    bi = 2 * tb + half
                    tmp, tn = (tmpa, "pn_a") if k % 2 == 0 else (tmpb, "pn_b")
                    k += 1
                    tt("dve", tmp[:], ps[bi][:], gbuf[:, half * 512:(half + 1) * 512], ALU.mult, [PSB[bi], B("gbuf")], [B(tn)])
                    xs = xres[:, tb, half * 512:(half + 1) * 512]
                    fw.op("dve", lambda e, tmp=tmp, xs=xs, tb=tb: e.scalar_tensor_tensor(out=xs, in0=tmp[:], scalar=rstd[:, tb:tb + 1], in1=xs, op0=ALU.mult, op1=ALU.add),
                          reads=[B(tn), B("rstd"), B(XN[tb])], writes=[B(XN[tb])])

        def normA_tb(tb, xn):
            c1 = slice(tb, tb + 1)
            fw.op("act", lambda e: e.activation(out=xn[:, tb, :], in_=xres[:, tb, :], func=AF.Square, accum_out=ssN[:, c1]),
                  reads=[B(XN[tb])], writes=[B("xn%d" % tb), B("ssN%d" % tb)])
            fw.op("act", lambda e: e.activation(out=lnN[:, c1], in_=ssN[:, c1], func=AF.Ln, scale=1.0 / D, bias=EPS), reads=[B("ssN%d" % tb)], writes=[B("lnN%d" % tb)])
            fw.op("act", lambda e: e.activation(out=rstdN[:, c1], in_=lnN[:, c1], func=AF.Exp, scale=-0.5), reads=[B("lnN%d" % tb)], writes=[B("rstdN%d" % tb)])
            fw.op("act", lambda e: e.activation(out=xn[:, tb, :], in_=xres[:, tb, :], func=AF.Copy, scale=rstdN[:, c1]),
                  reads=[B(XN[tb]), B("rstdN%d" % tb)], writes=[B("xn%d" % tb)])

        def normB_tb(tb, gi, xn):
            bi = 2 * tb
            pv = ps[bi][:].bitcast(BF16)
            for kc in range(8):
                fw.op("pe", lambda e: e.transpose(out=pv[:, kc * 128:(kc + 1) * 128], in_=xn[:, tb, kc * 128:(kc + 1) * 128], identity=ident_bf[:]),
                      reads=[B("xn%d" % tb), B("ident_bf")], writes=[PSB[bi]], signal=(kc == 7))
            fw.op("dve", lambda e: e.tensor_tensor(out=hT[:, :, tb * 128:(tb + 1) * 128], in0=pv[:, 0:1024].rearrange("p (k t) -> p k t", k=8),
                                                   in1=gcols[:, gi, :].unsqueeze(2).to_broadcast([128, 8, 128]), op=ALU.mult),
                  reads=[PSB[bi], B("gcols")], writes=[B("hT")])

        def norm_tb(tb, gi, xn, junkN):
            normA_tb(tb, xn)
            normB_tb(tb, gi, xn)

        def norm_pipelined(gi, xn):
            for s in range(5):
                if s < 4:
                    normA_tb(s, xn)
                if s >= 1:
                    normB_tb(s - 1, gi, xn)

        def post_tb(tb, tmps, junkP):
            c1 = slice(tb, tb + 1)
            for half in range(2):
                bi = 2 * tb + half
                jt, jn = tmps[half]
                fw.op("act", lambda e: e.activation(out=jt[:].bitcast(BF16)[:, 0:512], in_=ps[bi][:], func=AF.Square, accum_out=ss[:, bi:bi + 1]),
                      reads=[PSB[bi]], writes=[B(jn), B("ssP%d" % tb)])
            tt("dve", ssum[:, c1], ss[:, 2 * tb:2 * tb + 1], ss[:, 2 * tb + 1:2 * tb + 2], ALU.add, [B("ssP%d" % tb)], [B("ssumP%d" % tb)])
            fw.op("act", lambda e: e.activation(out=lnv[:, c1], in_=ssum[:, c1], func=AF.Ln, scale=1.0 / D, bias=EPS), reads=[B("ssumP%d" % tb)], writes=[B("lnP%d" % tb)])
            fw.op("act", lambda e: e.activation(out=rstd[:, c1], in_=lnv[:, c1], func=AF.Exp, scale=-0.5), reads=[B("lnP%d" % tb)], writes=[B("rstdP%d" % tb)])
            for half in range(2):
                bi = 2 * tb + half
                tmp, tn = tmps[half]
                tt("dve", tmp[:], ps[bi][:], gbuf[:, half * 512:(half + 1) * 512], ALU.mult, [PSB[bi], B("gbuf")], [B(tn)])
                xs = xres[:, tb, half * 512:(half + 1) * 512]
                fw.op("dve", lambda e: e.scalar_tensor_tensor(out=xs, in0=tmp[:], scalar=rstd[:, c1], in1=xs, op0=ALU.mult, op1=ALU.add),
                      reads=[B(tn), B("rstdP%d" % tb), B(XN[tb])], writes=[B(XN[tb])])

        def boundary_bufs(scope):
            xn = sb(scope, "xn", [128, 4, D], BF16)
            junkN = None
            junkP = None
            tmps = [(sb(scope, "pn_a", [128, 512], F32), "pn_a"), (sb(scope, "pn_b", [128, 512], F32), "pn_b")]
            return xn, junkN, junkP, tmps

        def out_proj_fused(actT, actbuf, gain_d, gi_next, scope):
            xn, junkN, junkP, tmps = boundary_bufs(scope)
            fw.dma("sp", lambda e: e.dma_start(out=gbuf[:], in_=gain_d[0:1, :].to_broadcast([128, D])), writes=[B("gbuf")], dsem=ds_g)
            g0 = next_group()
            g1 = next_group()
            for s in range(6):
                if s < 4:
                    tb = s
                    for half, (si, views) in enumerate((g0, g1)):
                        wv = views[0]
                        for kc in range(8):
                            fw.op("pe", lambda e: e.matmul(ps[2 * tb + half][:], lhsT=actT[:, kc, tb * 128:(tb + 1) * 128], rhs=wv[:, kc, :], start=(kc == 0), stop=(kc == 7)),
                                  reads=[B(actbuf), RB[si]], writes=[PSB[2 * tb + half]], signal=(kc == 7))
                    post_tb(tb, tmps, junkP)
                if 1 <= s <= 4:
                    normA_tb(s - 1, xn)
                if 2 <= s <= 5:
                    normB_tb(s - 2, gi_next, xn)
            group_done()

        def out_proj_tokmajor(actT, actbuf, nkc_list):
            first = True
            for gi_, (k0, k1) in enumerate(nkc_list):
                for half in range(2):
                    si, views = next_group()
                    wv = views[0]
                    for tb in range(4):
                        for kc in range(k0, k1):
                            st_ = (gi_ == 0 and kc == k0)
                            sp_ = (gi_ == len(nkc_list) - 1 and kc == k1 - 1)
                            fw.op("pe", lambda e, tb=tb, kc=kc, half=half, wv=wv, k0=k0, st_=st_, sp_=sp_: e.matmul(
                                ps[2 * tb + half][:], lhsT=actT[:, kc, tb * 128:(tb + 1) * 128], rhs=wv[:, kc - k0, :], start=st_, stop=sp_),
                                reads=[B(actbuf), RB[si]], writes=[PSB[2 * tb + half]], signal=(kc == k1 - 1))
                    group_done()

        def mem_kv_phase():
            fw.barrier(with_sp=True)
            with arena.scope() as ph:
                memx = sb(ph, "memx", [128, 2, D], F32)
                memhT = sb(ph, "memhT", [128, 8, NMEM], BF16)
                fw.dma("sp", lambda e: e.dma_start(out=memx[:], in_=mem_d.rearrange("(t p) d -> p t d", p=128)), writes=[B("memx")], dsem=fw.new_dsem())
                with arena.scope() as ph2:
                    rms_to_hT(ph2, memx, "memx", 2, 3, memhT, "memhT")
                    fw.barrier()
                for half in range(2):
                    si, views = next_group()
                    wv = views[0]
                    for c in range(4):
                        bi = nb()
                        for kc in range(8):
                            fw.op("pe", lambda e, c=c, kc=kc, wv=wv, bi=bi: e.matmul(ps[bi][:, 0:NMEM], lhsT=wv[:, kc, c * 128:(c + 1) * 128], rhs=memhT[:, kc, :],
                                                                                    start=(kc == 0), stop=(kc == 7)),
                                  reads=[RB[si], B("memhT")], writes=[PSB[bi]], signal=(kc == 7))
                        evac_copy(memKT[:, half * 4 + c, :], ps[bi][:, 0:NMEM], [PSB[bi]], [B("memKT")])
                    group_done()
                for half in range(2):
                    si, views = next_group()
                    wv = views[0]
                    for mb in range(2):
                        bi = nb()
                        for kc in range(8):
                            fw.op("pe", lambda e, mb=mb, kc=kc, wv=wv, bi=bi: e.matmul(ps[bi][:], lhsT=memhT[:, kc, mb * 128:(mb + 1) * 128], rhs=wv[:, kc, :],
                                                                                     start=(kc == 0), stop=(kc == 7)),
                                  reads=[RB[si], B("memhT")], writes=[PSB[bi]], signal=(kc == 7))
                        evac_copy(memV[:, mb, half * 512:(half + 1) * 512], ps[bi][:], [PSB[bi]], [B("memV")])
                    group_done()
                fw.barrier(with_sp=True)

        for ti in range(ntiles):
            t0 = ti * TT
            if ti == 0:
                for tb in range(4):
                    fw.dma("sp", lambda e, tb=tb: e.dma_start(out=xres[:, tb, :], in_=x_d[128 * tb:128 * tb + 128, :]),
                           writes=[B(XN[tb])], dsem=ds_xt[tb])
            with arena.scope() as ph1:
                qT = sb(ph1, "qT", [128, 4, TT], BF16)
                U = sb(ph1, "U", [128, 16, 128], BF16)
                oaT = sb(ph1, "oaT", [128, 4, TT], BF16)
                osT = sb(ph1, "osT", [128, 4, TT], BF16)
                if ti == 0:
                    with arena.scope() as ph:
                        xn0, junkN0, _, _ = boundary_bufs(ph)
                        norm_pipelined(0, xn0)
                        fw.barrier()
                    dump("hT", hT[:], "hT", [128, 8, TT], BF16)
                for blk in range(2):
                    si, views = next_group()
                    wv = views[0]
                    for c in range(4):
                        bi = nb()
                        for kc in range(8):
                            fw.op("pe", lambda e, c=c, kc=kc, wv=wv, bi=bi: e.matmul(ps[bi][:], lhsT=wv[:, kc, c * 128:(c + 1) * 128], rhs=hT[:, kc, :],
                                                                                    start=(kc == 0), stop=(kc == 7)),
                                  reads=[RB[si], B("hT")], writes=[PSB[bi]], signal=(kc == 7))
                        if blk == 0:
                            evac_copy(qT[:, c, :], ps[bi][:], [PSB[bi]], [B("qT")], scale=0.125)
                        else:
                            evac_copy(kT[:, c, t0:t0 + TT], ps[bi][:], [PSB[bi]], [B("kT")])
                    group_done()
                si, views = next_group()
                wv = views[0]
                for tb in range(4):
                    bi = nb()
                    for kc in range(8):
                        fw.op("pe", lambda e, tb=tb, kc=kc, wv=wv, bi=bi: e.matmul(ps[bi][:], lhsT=hT[:, kc, tb * 128:(tb + 1) * 128], rhs=wv[:, kc, :],
                                                                                 start=(kc == 0), stop=(kc == 7)),
                              reads=[RB[si], B("hT")], writes=[PSB[bi]], signal=(kc == 7))
                    evac_copy(vtok[:, 4 * ti + tb, :], ps[bi][:], [PSB[bi]], [B("vtok")])
                group_done()
                si, views = next_group()
                wv = views[0]
                hT4 = hT[:].rearrange("p c (k j) -> p c j k", j=4)
                for qb in range(4):
                    bi = nb()
                    for qq in range(4):
                        q = 4 * qb + qq
                        for kc in range(8):
                            for j in range(4):
                                fw.op("pe", lambda e, q=q, qq=qq, j=j, kc=kc, wv=wv, bi=bi: e.matmul(
                                    ps[bi][32 * j:32 * j + 32, qq * 128:(qq + 1) * 128], lhsT=wv[:, kc, 32 * q:32 * q + 32], rhs=hT4[:, kc, j, :],
                                    start=(kc == 0), stop=(kc == 7), tile_position=(0, 32 * j)),
                                    reads=[RB[si], B("hT")], writes=[PSB[bi]], signal=(kc == 7 and j == 3 and qq == 3))
                    evac_copy(U[:, 4 * qb:4 * qb + 4, :], ps[bi][:].rearrange("p (a k) -> p a k", a=4), [PSB[bi]], [B("U")])
                group_done()
                fw.barrier()
                if ti == 0:
                    dump("qT", qT[:], "qT", [128, 4, TT], BF16)
                    dump("U", U[:], "U", [128, 16, 128], BF16)
                    dump("vtok0", vtok[:, 0:4, :], "vtok", [128, 4, 512], BF16)

                with arena.scope() as ph:
                    e_sb = sb(ph, "e_sb", [128, 2, TT], BF16)
                    sp_sb = [sb(ph, "sp_sb%d" % i, [128, 2, TT], BF16) for i in range(2)]
                    w_sb = [sb(ph, "w_sb%d" % i, [128, 2, TT], BF16) for i in range(2)]
                    Rb = [sb(ph, "R%d" % i, [128, 2, TT], BF16) for i in range(2)]
                    PZ3 = pp[0][:].rearrange("p (h c) -> p h c", h=2)
                    PW3 = pp[1][:].rearrange("p (h c) -> p h c", h=2)
                    Zb = (0, 1); Wb = (2, 3); AVB = (4, 5)
                    A1 = sb(ph, "A1", [128, 256], F32); A2 = sb(ph, "A2", [128, 256], F32)
                    VtR = sb(ph, "VtR", [128, 2, 128], F32); VtI = sb(ph, "VtI", [128, 2, 128], F32)
                    XtR = sb(ph, "XtR", [128, 2, 128], F32); XtI = sb(ph, "XtI", [128, 2, 128], F32)
                    XsR = sb(ph, "XsR", [128, 2, 132], BF16); XsI = sb(ph, "XsI", [128, 2, 132], BF16)
                    yg = sb(ph, "yg", [128, 2, 128], BF16)
                    ygT = sb(ph, "ygT", [128, 4, TT], BF16)
                    f2 = lambda t: t[:].rearrange("p a k -> p (a k)")
                    si_glu, views_glu = next_group()
                    ssm_copy_eng = "act" if ti <= 1 else "dve"
                    thunks = []
                    BV, BY = 6, 7

                    def mk_batch(b_):
                        q0 = 2 * b_; cc = b_ // 2; hb = b_ % 2
                        Cq = Ctab[:, q0:q0 + 2, :].rearrange("p a k -> p (a k)")
                        Sq = Stab[:, q0:q0 + 2, :].rearrange("p a k -> p (a k)")
                        vre = ps[BV][:, 0:256]; vim = ps[BV][:, 256:512]

                        def t1():
                            for qq in range(2):
                                fw.op("pe", lambda e: e.matmul(ps[BV][:, qq * 128:(qq + 1) * 128], lhsT=BcRe[:, q0 + qq, :], rhs=U[:, q0 + qq, :], start=True, stop=True),
                                      reads=[B("BcRe"), B("U")], writes=[PSB[BV]], signal=False)
                            for qq in range(2):
                                fw.op("pe", lambda e: e.matmul(ps[BV][:, 256 + qq * 128:256 + (qq + 1) * 128], lhsT=BcIm[:, q0 + qq, :], rhs=U[:, q0 + qq, :], start=True, stop=True),
                                      reads=[B("BcIm"), B("U")], writes=[PSB[BV]], signal=(qq == 1))

                        def t2():
                            tt("dve", A1[:], vre, Cq, ALU.mult, [PSB[BV], B("Ctab")], [B("A1")])
                            tt("dve", A2[:], vim, Sq, ALU.mult, [PSB[BV], B("Stab")], [B("A2")])
                            tt("dve", f2(VtR), A1[:], A2[:], ALU.add, [B("A1"), B("A2")], [B("VtR")])
                            tt("dve", A1[:], vim, Cq, ALU.mult, [PSB[BV], B("Ctab")], [B("A1")])
                            tt("dve", A2[:], vre, Sq, ALU.mult, [PSB[BV], B("Stab")], [B("A2")])
                            tt("dve", f2(VtI), A1[:], A2[:], ALU.subtract, [B("A1"), B("A2")], [B("VtI")])

                        def t3():
                            fw.op("dve", lambda e: e.tensor_copy(out=XsR[:, :, 0:1], in_=XcRe[:, q0:q0 + 2].unsqueeze(2)), reads=[B("XcRe")], writes=[B("XsR")])
                            fw.op("dve", lambda e: e.tensor_copy(out=XsI[:, :, 0:1], in_=XcIm[:, q0:q0 + 2].unsqueeze(2)), reads=[B("XcIm")], writes=[B("XsI")])
                            for qq in range(2):
                                q = q0 + qq
                                fw.op("dve", lambda e: e.tensor_tensor_scan(out=XtR[:, qq, :], data0=rho4[:, q:q + 1].to_broadcast([128, 128]), data1=VtR[:, qq, :],
                                                                            initial=XcRe[:, q:q + 1], op0=ALU.mult, op1=ALU.add),
                                      reads=[B("rho4"), B("VtR"), B("XcRe")], writes=[B("XtR")])
                                fw.op("dve", lambda e: e.tensor_tensor_scan(out=XtI[:, qq, :], data0=rho4[:, q:q + 1].to_broadcast([128, 128]), data1=VtI[:, qq, :],
                                                                            initial=XcIm[:, q:q + 1], op0=ALU.mult, op1=ALU.add),
                                      reads=[B("rho4"), B("VtI"), B("XcIm")], writes=[B("XtI")])

                        def t4():
                            tt("dve", A1[:], f2(XtR), Cq, ALU.mult, [B("XtR"), B("Ctab")], [B("A1")])
                            tt("dve", A2[:], f2(XtI), Sq, ALU.mult, [B("XtI"), B("Stab")], [B("A2")])
                            tt("dve", f2(VtR), A1[:], A2[:], ALU.subtract, [B("A1"), B("A2")], [B("VtR")])
                            tt("dve", A1[:], f2(XtI), Cq, ALU.mult, [B("XtI"), B("Ctab")], [B("A1")])
                            tt("dve", A2[:], f2(XtR), Sq, ALU.mult, [B("XtR"), B("Stab")], [B("A2")])
                            tt("dve", f2(VtI), A1[:], A2[:], ALU.add, [B("A1"), B("A2")], [B("VtI")])
                            evac_copy(XsR[:, :, 1:129], VtR[:], [B("VtR")], [B("XsR")], eng=ssm_copy_eng)
                            evac_copy(XsI[:, :, 1:129], VtI[:], [B("VtI")], [B("XsI")], eng=ssm_copy_eng)
                            fw.op("dve", lambda e: e.tensor_copy(out=XcRe[:, q0:q0 + 2].unsqueeze(2), in_=VtR[:, :, 127:128]), reads=[B("VtR")], writes=[B("XcRe")])
                            fw.op("dve", lambda e: e.tensor_copy(out=XcIm[:, q0:q0 + 2].unsqueeze(2), in_=VtI[:, :, 127:128]), reads=[B("VtI")], writes=[B("XcIm")])

                        def t5():
                            for qq in range(2):
                                q = q0 + qq
                                o = ps[BY][:, qq * 128:(qq + 1) * 128]
                                fw.op("pe", lambda e: e.matmul(o, lhsT=Gm[:, q, :], rhs=U[:, q, :], start=True, stop=False),
                                      reads=[B("Gm"), B("U")], writes=[PSB[BY]], signal=False)
                                fw.op("pe", lambda e: e.matmul(o, lhsT=CmRe[:, q, :], rhs=XsR[:, qq, 0:128], start=False, stop=False),
                                      reads=[B("CmRe"), B("XsR")], writes=[PSB[BY]], signal=False)
                                fw.op("pe", lambda e: e.matmul(o, lhsT=CmIm[:, q, :], rhs=XsI[:, qq, 0:128], start=False, stop=True),
                                      reads=[B("CmIm"), B("XsI")], writes=[PSB[BY]], signal=(qq == 1))
                            for qq in range(2):
                                q = q0 + qq
                                fw.op("dve", lambda e: e.scalar_tensor_tensor(out=A1[:, qq * 128:(qq + 1) * 128], in0=U[:, q, :], scalar=dvec[:, q:q + 1],
                                                                              in1=ps[BY][:, qq * 128:(qq + 1) * 128], op0=ALU.mult, op1=ALU.add),
                                      reads=[B("U"), B("dvec"), PSB[BY]], writes=[B("A1")])

                        def t6():
                            tt("dve", A2[:], A1[:], A1[:], ALU.mult, [B("A1")], [B("A2")])
                            fw.op("dve", lambda e: e.tensor_scalar(out=A2[:], in0=A2[:], scalar1=0.044715, scalar2=1.0, op0=ALU.mult, op1=ALU.add), reads=[B("A2")], writes=[B("A2")])
                            tt("dve", A2[:], A2[:], A1[:], ALU.mult, [B("A2"), B("A1")], [B("A2")])
                            fw.op("dve", lambda e: e.tensor_scalar(out=A2[:], in0=A2[:], scalar1=-40.0, scalar2=None, op0=ALU.max), reads=[B("A2")], writes=[B("A2")])
                            fw.op("act", lambda e: e.activation(out=A2[:], in_=A2[:], func=AF.Exp, scale=-1.5957691216), reads=[B("A2")], writes=[B("A2")])
                            fw.op("dve", lambda e: e.tensor_scalar(out=A2[:], in0=A2[:], scalar1=1.0, scalar2=None, op0=ALU.add), reads=[B("A2")], writes=[B("A2")])
                            fw.op("dve", lambda e: e.reciprocal(out=A2[:], in_=A2[:]), reads=[B("A2")], writes=[B("A2")])
                            tt("dve", f2(yg), A1[:], A2[:], ALU.mult, [B("A1"), B("A2")], [B("yg")])

                        def t7():
                            for i in range(4):
                                for qq in range(2):
                                    pb = 32 * (2 * hb + qq)
                                    fw.op("pe", lambda e: e.matmul(ps[BY][pb:pb + 32, i * 128:(i + 1) * 128], lhsT=ident_bf[:, 32 * i:32 * i + 32], rhs=yg[:, qq, :],
                                                                   start=True, stop=True, tile_position=(0, pb)),
                                          reads=[B("ident_bf"), B("yg")], writes=[PSB[BY]], signal=(qq == 1 and i == 3))
                            evac_copy(ygT[64 * hb:64 * hb + 64, cc, :].rearrange("p (k i) -> p i k", i=4),
                                      ps[BY][64 * hb:64 * hb + 64, :].rearrange("p (i k) -> p i k", i=4), [PSB[BY]], [B("ygT")], eng=ssm_copy_eng)
                        return [t1, t2, t3, t4, t5, t6, t7]

                    for b_ in range(8):
                        thunks.extend(mk_batch(b_))

                    def mk_glu(co, hf):
                        wg = views_glu[0]
                        bk = BV if (2 * co + hf) % 2 == 0 else BY
                        cs = slice(256 * hf, 256 * hf + 256)

                        def tg():
                            for ci in range(4):
                                fw.op("pe", lambda e: e.matmul(ps[bk][:, 0:256], lhsT=wg[:, ci, co * 128:(co + 1) * 128], rhs=ygT[:, ci, cs], start=(ci == 0), stop=(ci == 3)),
                                      reads=[RB[si_glu], B("ygT")], writes=[PSB[bk]], signal=(ci == 3))
                            fw.op("act", lambda e: e.activation(out=A2[:], in_=ps[bk][:, 0:256], func=AF.Exp, scale=-1.0, bias=nbglu[:, co:co + 1]),
                                  reads=[PSB[bk], B("nbglu")], writes=[B("A2")])
                            fw.op("dve", lambda e: e.tensor_scalar(out=A2[:], in0=A2[:], scalar1=1.0, scalar2=None, op0=ALU.add), reads=[B("A2")], writes=[B("A2")])
                            fw.op("dve", lambda e: e.reciprocal(out=A2[:], in_=A2[:]), reads=[B("A2")], writes=[B("A2")])
                            tt("dve", osT[:, co, cs], ygT[:, co, cs], A2[:], ALU.mult, [B("ygT"), B("A2")], [B("osT")])
                        return tg

                    for co in range(4):
                        for hf in range(2):
                            thunks.append(mk_glu(co, hf))
                    thunks.append(group_done)

                    nkb = 4 * ti + 4
                    steps = []
                    for p in range(4):
                        for kbi, kb in enumerate(range(nkb - 1, -1, -1)):
                            steps.append((p, kb, kbi))
                    N = len(steps)

                    def c0_of(kb):
                        j = kb - 4 * ti
                        return 128 * j if j > 0 else 0

                    def qk_mm(bank, p, kb, hh, more):
                        r0 = 64 * hh
                        diag = kb >= 4 * ti
                        c0 = c0_of(kb)
                        fw.op("pe", lambda e: e.matmul(ps[bank][:, c0:TT], lhsT=kT[r0:r0 + 64, p, kb * 128:(kb + 1) * 128], rhs=qT[r0:r0 + 64, p, c0:TT],
                                                       start=True, stop=(not diag and not more)),
                              reads=[B("kT"), B("qT")], writes=[PSB[bank]], signal=False)
                        if diag:
                            fw.op("pe", lambda e: e.matmul(ps[bank][:, c0:TT], lhsT=ident_bf[:], rhs=maskb[:, 0:TT - c0], start=False, stop=(not more)),
                                  reads=[B("ident_bf"), B("maskb")], writes=[PSB[bank]], signal=False)

                    def qk_pair(banks, p, kb, more):
                        diag = kb >= 4 * ti
                        c0 = c0_of(kb)
                        for hh in range(2):
                            r0 = 64 * hh
                            fw.op("pe", lambda e: e.matmul(ps[banks[hh]][:, c0:TT], lhsT=kT[r0:r0 + 64, p, kb * 128:(kb + 1) * 128], rhs=qT[r0:r0 + 64, p, c0:TT],
                                                           start=True, stop=(not diag and not more)),
                                  reads=[B("kT"), B("qT")], writes=[PSB[banks[hh]]], signal=False)
                        if diag:
                            for hh in range(2):
                                fw.op("pe", lambda e: e.matmul(ps[banks[hh]][:, c0:TT], lhsT=ident_bf[:], rhs=maskb[:, 0:TT - c0], start=False, stop=(not more)),
                                      reads=[B("ident_bf"), B("maskb")], writes=[PSB[banks[hh]]], signal=False)

                    def S0(n):
                        p, kb, kbi = steps[n]
                        qk_pair(Zb, p, kb, False)
                        sig("pe")

                    def S1(n):
                        p, kb, kbi = steps[n]
                        c0 = c0_of(kb)
                        i2 = n % 2
                        fw.op("act", lambda e: e.activation(out=e_sb[:, :, c0:TT], in_=PZ3[:, :, c0:TT], func=AF.Exp), reads=[PSB[Zb[0]], PSB[Zb[1]]], writes=[B("e_sb")])
                        fw.op("act", lambda e: e.activation(out=sp_sb[i2][:, :, c0:TT], in_=e_sb[:, :, c0:TT], func=AF.Ln, bias=1.0), reads=[B("e_sb")], writes=[B("sp_sb%d" % i2)])

                    def S2(n):
                        p, kb, kbi = steps[n]
                        c0 = c0_of(kb)
                        i2 = n % 2
                        first = (kbi == 0)
                        last = (kb == 0)
                        rc, rn = Rb[kbi % 2], Rb[(kbi + 1) % 2]
                        rcn, rnn = "R%d" % (kbi % 2), "R%d" % ((kbi + 1) % 2)
                        if first:
                            fw.op("pool", lambda e: e.memset(Rb[0][:], 0.0), writes=[B("R0")])
                            fw.op("pool", lambda e: e.memset(Rb[1][:], 0.0), writes=[B("R1")])
                        qk_pair(Wb, p, kb, True)
                        for hh in range(2):
                            fw.op("pe", lambda e: e.matmul(ps[Wb[hh]][:, c0:TT], lhsT=ntri[:], rhs=sp_sb[i2][:, hh, c0:TT], start=False, stop=first),
                                  reads=[B("ntri"), B("sp_sb%d" % i2)], writes=[PSB[Wb[hh]]], signal=False)
                            if not first:
                                fw.op("pe", lambda e: e.matmul(ps[Wb[hh]][:, c0:TT], lhsT=nones[:], rhs=rc[:, hh, c0:TT], start=False, stop=True),
                                      reads=[B("nones"), B(rcn)], writes=[PSB[Wb[hh]]], signal=False)
                        sig("pe")
                        if not last:
                            if first:
                                fw.op("dve", lambda e: e.tensor_copy(out=rn[:, :, c0:TT], in_=sp_sb[i2][:, :, c0:TT]), reads=[B("sp_sb%d" % i2)], writes=[B(rnn)])
                            else:
                                tt("dve", rn[:, :, c0:TT], rc[:, :, c0:TT], sp_sb[i2][:, :, c0:TT], ALU.add, [B(rcn), B("sp_sb%d" % i2)], [B(rnn)])

                    def S3(n):
                        p, kb, kbi = steps[n]
                        c0 = c0_of(kb)
                        i2 = n % 2
                        fw.op("act", lambda e: e.activation(out=w_sb[i2][:, :, c0:TT], in_=PW3[:, :, c0:TT], func=AF.Exp), reads=[PSB[Wb[0]], PSB[Wb[1]]], writes=[B("w_sb%d" % i2)])

                    def S4(n):
                        p, kb, kbi = steps[n]
                        c0 = c0_of(kb)
                        i2 = n % 2
                        ab = AVB[p % 2]
                        for hh in range(2):
                            h = 2 * p + hh
                            fw.op("pe", lambda e: e.matmul(ps[ab][64 * hh:64 * hh + 64, c0:TT], lhsT=vtok[:, kb, h * 64:(h + 1) * 64], rhs=w_sb[i2][:, hh, c0:TT],
                                                           start=(kbi == 0), stop=(kb == 0), skip_group_check=True),
                                  reads=[B("vtok"), B("w_sb%d" % i2)], writes=[PSB[ab]], signal=False)
                        sig("pe")
                        if kb == 0:
                            evac_copy(oaT[:, p, :], ps[ab][:], [PSB[ab]], [B("oaT")], eng="dve")

                    rate = len(thunks) / max(1.0, 1.0 * N)
                    acc = 0.0
                    tq = list(thunks)
                    for n in range(N + 2):
                        if n < N:
                            S0(n)
                            S1(n)
                        if 1 <= n <= N:
                            S2(n - 1)
                            S3(n - 1)
                        if n >= 2:
                            S4(n - 2)
                        acc += rate
                        while acc >= 1.0 and tq:
                            tq.pop(0)()
                            acc -= 1.0
                    while tq:
                        tq.pop(0)()
                    fw.barrier()
                if ti == 0:
                    dump("oaT", oaT[:], "oaT", [128, 4, TT], BF16)
                    dump("osT", osT[:], "osT", [128, 4, TT], BF16)

                with arena.scope() as ph:
                    mergedT = sb(ph, "mergedT", [128, 8, TT], BF16)
                    phm = arena.scope().__enter__()
                    sga = sb(phm, "sga", [128, TT], F32); sgs = sb(phm, "sgs", [128, TT], F32)
                    m1 = sb(phm, "m1", [128, TT], F32); m2 = sb(phm, "m2", [128, TT], F32)
                    for pr in range(4):
                        si_a, vg = next_group()
                        si_b, vb = next_group()
                        si_s = si_a
                        va = [vg[0]]; vs = [vg[1]]
                        for mm_ in range(2):
                            m = 2 * pr + mm_
                            b_ga, b_gs, b_ba, b_bs = nb(), nb(), nb(), nb()
                            for kc in range(8):
                                fw.op("pe", lambda e, kc=kc, mm_=mm_, b=b_ga, w=va[0]: e.matmul(ps[b][:], lhsT=w[:, kc, mm_ * 128:(mm_ + 1) * 128], rhs=hT[:, kc, :], start=(kc == 0), stop=(kc == 7)),
                                      reads=[RB[si_a], B("hT")], writes=[PSB[b_ga]], signal=(kc == 7))
                            for kc in range(8):
                                fw.op("pe", lambda e, kc=kc, mm_=mm_, b=b_gs, w=vs[0]: e.matmul(ps[b][:], lhsT=w[:, kc, mm_ * 128:(mm_ + 1) * 128], rhs=hT[:, kc, :], start=(kc == 0), stop=(kc == 7)),
                                      reads=[RB[si_s], B("hT")], writes=[PSB[b_gs]], signal=(kc == 7))
                            for ci in range(4):
                                fw.op("pe", lambda e, ci=ci, mm_=mm_, b=b_ba, w=vb[0]: e.matmul(ps[b][:], lhsT=w[:, ci, mm_ * 128:(mm_ + 1) * 128], rhs=oaT[:, ci, :], start=(ci == 0), stop=(ci == 3)),
                                      reads=[RB[si_b], B("oaT")], writes=[PSB[b_ba]], signal=(ci == 3))
                            for ci in range(4):
                                fw.op("pe", lambda e, ci=ci, mm_=mm_, b=b_bs, w=vb[1]: e.matmul(ps[b][:], lhsT=w[:, ci, mm_ * 128:(mm_ + 1) * 128], rhs=osT[:, ci, :], start=(ci == 0), stop=(ci == 3)),
                                      reads=[RB[si_b], B("osT")], writes=[PSB[b_bs]], signal=(ci == 3))
                            fw.op("act", lambda e, m=m, b=b_ga: e.activation(out=sga[:], in_=ps[b][:], func=AF.Sigmoid, bias=bgate[:, m:m + 1]),
                                  reads=[PSB[b_ga], B("bgate")], writes=[B("sga")])
                            fw.op("act", lambda e, m=m, b=b_gs: e.activation(out=sgs[:], in_=ps[b][:], func=AF.Sigmoid, bias=bgate[:, 8 + m:9 + m]),
                                  reads=[PSB[b_gs], B("bgate")], writes=[B("sgs")])
                            tt("dve", m1[:], sga[:], ps[b_ba][:], ALU.mult, [B("sga"), PSB[b_ba]], [B("m1")])
                            tt("dve", m2[:], sgs[:], ps[b_bs][:], ALU.mult, [B("sgs"), PSB[b_bs]], [B("m2")])
                            tt("dve", mergedT[:, m, :], m1[:], m2[:], ALU.add, [B("m1"), B("m2")], [B("mergedT")])
                        group_done()
                    if ti == 0:
                        dump("mergedT", mergedT[:], "mergedT", [128, 8, TT], BF16)
                    fw.barrier()
                    phm.__exit__(None, None, None)
                    out_proj_fused(mergedT, "mergedT", g_mix_post, 1, ph)
                    fw.barrier()
            if ti == 0:
                dump("x1", xres[:], XN, [128, 4, D], F32)

            if ti == 0:
                mem_kv_phase()
            with arena.scope() as ph2:
                qxT = sb(ph2, "qxT", [128, 8, TT], BF16)
                oxT = sb(ph2, "oxT", [128, 8, TT], BF16)
                Pm = [sb(ph2, "Pm%d" % i, [128, 4, NMEM], BF16) for i in range(2)]
                PT = [sb(ph2, "PT%d" % i, [128, 2, TT], BF16) for i in range(4)]
                mx = sb(ph2, "mx", [128, 16], F32); nmx = sb(ph2, "nmx", [128, 16], F32)
                sm = sb(ph2, "sm", [128, 16], F32); rs = sb(ph2, "rs", [128, 16], F32)
                SC = 1.0 / 16.0
                for half in range(2):
                    si, views = next_group()
                    wv = views[0]
                    for c in range(4):
                        bi = nb()
                        for kc in range(8):
                            fw.op("pe", lambda e, c=c, kc=kc, wv=wv, bi=bi: e.matmul(ps[bi][:], lhsT=wv[:, kc, c * 128:(c + 1) * 128], rhs=hT[:, kc, :], start=(kc == 0), stop=(kc == 7)),
                                  reads=[RB[si], B("hT")], writes=[PSB[bi]], signal=(kc == 7))
                        evac_copy(qxT[:, half * 4 + c, :], ps[bi][:], [PSB[bi]], [B("qxT")])
                    group_done()
                def xa_scores(tb):
                    b0, b1 = nb(), nb()
                    for h in range(4):
                        bi = b0 if h < 2 else b1
                        o = ps[bi][:, (h % 2) * NMEM:(h % 2 + 1) * NMEM]
                        for c2 in range(2):
                            ch = 2 * h + c2
                            fw.op("pe", lambda e: e.matmul(o, lhsT=qxT[:, ch, tb * 128:(tb + 1) * 128], rhs=memKT[:, ch, :], start=(c2 == 0), stop=(c2 == 1)),
                                  reads=[B("qxT"), B("memKT")], writes=[PSB[bi]], signal=(c2 == 1 and h % 2 == 1))
                    return b0, b1

                def xa_softmax(tb, b0, b1):
                    pm = Pm[tb % 2]; pmn = "Pm%d" % (tb % 2)
                    s4 = slice(4 * tb, 4 * tb + 4)
                    stn = "xst%d" % tb
                    for hp, bi in enumerate((b0, b1)):
                        fw.op("dve", lambda e: e.tensor_reduce(out=mx[:, 4 * tb + 2 * hp:4 * tb + 2 * hp + 2], in_=ps[bi][:].rearrange("p (h m) -> p h m", h=2), axis=mybir.AxisListType.X, op=ALU.max),
                              reads=[PSB[bi]], writes=[B(stn + "mx")])
                    fw.op("dve", lambda e: e.tensor_scalar(out=nmx[:, s4], in0=mx[:, s4], scalar1=-SC, scalar2=None, op0=ALU.mult), reads=[B(stn + "mx")], writes=[B(stn + "nmx")])
                    for h in range(4):
                        bi = b0 if h < 2 else b1
                        o = ps[bi][:, (h % 2) * NMEM:(h % 2 + 1) * NMEM]
                        fw.op("act", lambda e: e.activation(out=pm[:, h, :], in_=o, func=AF.Exp, scale=SC, bias=nmx[:, 4 * tb + h:4 * tb + h + 1], accum_out=sm[:, 4 * tb + h:4 * tb + h + 1]),
                              reads=[PSB[bi], B(stn + "nmx")], writes=[B(pmn), B(stn + "sm")])
                    fw.op("dve", lambda e: e.reciprocal(out=rs[:, s4], in_=sm[:, s4]), reads=[B(stn + "sm")], writes=[B(stn + "rs")])
                    fw.op("dve", lambda e: e.tensor_tensor(out=pm[:], in0=pm[:], in1=rs[:, s4].unsqueeze(2).to_broadcast([128, 4, NMEM]), op=ALU.mult),
                          reads=[B(pmn), B(stn + "rs")], writes=[B(pmn)])

                def xa_transpose(tb):
                    pm = Pm[tb % 2]; pmn = "Pm%d" % (tb % 2)
                    for hp in range(2):
                        bi = nb()
                        pv = ps[bi][:].bitcast(BF16)
                        for hh in range(2):
                            h = 2 * hp + hh
                            for mb in range(2):
                                fw.op("pe", lambda e: e.transpose(out=pv[:, hh * 256 + mb * 128:hh * 256 + (mb + 1) * 128], in_=pm[:, h, mb * 128:(mb + 1) * 128], identity=ident_bf[:]),
                                      reads=[B(pmn), B("ident_bf")], writes=[PSB[bi]], signal=(mb == 1 and hh == 1))
                        for hh in range(2):
                            h = 2 * hp + hh
                            evac_copy(PT[h][:, :, tb * 128:(tb + 1) * 128], pv[:, hh * 256:(hh + 1) * 256].rearrange("p (m t) -> p m t", m=2), [PSB[bi]], [B("PT%d" % h)],
                                      eng=("act" if hh == 0 else "dve"))

                sc = {}
                sc[0] = xa_scores(0)
                sc[1] = xa_scores(1)
                xa_softmax(0, *sc[0])
                sc[2] = xa_scores(2)
                xa_softmax(1, *sc[1])
                xa_transpose(0)
                sc[3] = xa_scores(3)
                xa_softmax(2, *sc[2])
                xa_transpose(1)
                xa_softmax(3, *sc[3])
                xa_transpose(2)
                xa_transpose(3)
                for ch in range(8):
                    h = ch // 2
                    bi = nb()
                    for mb in range(2):
                        fw.op("pe", lambda e, ch=ch, mb=mb, h=h, bi=bi: e.matmul(ps[bi][:], lhsT=memV[:, mb, ch * 128:(ch + 1) * 128], rhs=PT[h][:, mb, :], start=(mb == 0), stop=(mb == 1)),
                              reads=[B("memV"), B("PT%d" % h)], writes=[PSB[bi]], signal=(mb == 1))
                    evac_copy(oxT[:, ch, :], ps[bi][:], [PSB[bi]], [B("oxT")])
                fw.barrier()
                out_proj_fused(oxT, "oxT", g_xa_post, 2, ph2)
                fw.barrier()
            if ti == 0:
                dump("x2", xres[:], XN, [128, 4, D], F32)

            with arena.scope() as ph3:
                actT = sb(ph3, "actT", [128, 22, TT], BF16)
                with arena.scope() as phu:
                    upx = [sb(phu, "upx%d" % i, [128, TT + 2], F32) for i in range(2)]
                    cgs = [sb(phu, "cg%d" % i, [128, TT], F32) for i in range(2)]
                    cvs = [sb(phu, "cv%d" % i, [128, TT], F32) for i in range(2)]
                    gls = [sb(phu, "gl%d" % i, [128, TT], F32) for i in range(2)]
                    for j in range(11):
                        si, views = next_group()
                        for f2 in range(2):
                            f = 2 * j + f2
                            par = f % 2
                            for isval in range(2):
                                wv = views[isval]
                                chunk = f + 22 * isval
                                bi = nb()
                                for kc in range(8):
                                    fw.op("pe", lambda e, kc=kc, f2=f2, wv=wv, bi=bi: e.matmul(ps[bi][:], lhsT=wv[:, kc, f2 * 128:(f2 + 1) * 128], rhs=hT[:, kc, :], start=(kc == 0), stop=(kc == 7)),
                                          reads=[RB[si], B("hT")], writes=[PSB[bi]], signal=(kc == 7))
                                ux = upx[isval]; uxn = "upx%d" % isval
                                co, con = (cgs[par], "cg%d" % par) if isval == 0 else (cvs[par], "cv%d" % par)
                                fw.op("pool", lambda e, ux=ux, chunk=chunk: e.tensor_copy(out=ux[:, 0:2], in_=halo[:, chunk, :]), reads=[B("halo")], writes=[B(uxn)])
                                fw.op("act", lambda e, ux=ux, bi=bi: e.activation(out=ux[:, 2:TT + 2], in_=ps[bi][:], func=AF.Copy), reads=[PSB[bi]], writes=[B(uxn)])
                                fw.op("act", lambda e, co=co, bi=bi, chunk=chunk: e.activation(out=co[:], in_=ps[bi][:], func=AF.Identity, scale=convw[:, 2, chunk:chunk + 1], bias=convb[:, chunk:chunk + 1]),
                                      reads=[PSB[bi], B("convw"), B("convb")], writes=[B(con)])
                                fw.op("pool", lambda e, ux=ux, chunk=chunk: e.tensor_copy(out=halo[:, chunk, :], in_=ux[:, TT:TT + 2]), reads=[B(uxn)], writes=[B("halo")])
                                fw.op("dve", lambda e, ux=ux, co=co, chunk=chunk: e.scalar_tensor_tensor(out=co[:], in0=ux[:, 1:TT + 1], scalar=convw[:, 1, chunk:chunk + 1], in1=co[:], op0=ALU.mult, op1=ALU.add),
                                      reads=[B(uxn), B("convw"), B(con)], writes=[B(con)])
                                fw.op("dve", lambda e, ux=ux, co=co, chunk=chunk: e.scalar_tensor_tensor(out=co[:], in0=ux[:, 0:TT], scalar=convw[:, 0, chunk:chunk + 1], in1=co[:], op0=ALU.mult, op1=ALU.add),
                                      reads=[B(uxn), B("convw"), B(con)], writes=[B(con)])
                            fw.op("act", lambda e, par=par: e.activation(out=gls[par][:], in_=cgs[par][:], func=AF.Gelu_apprx_tanh), reads=[B("cg%d" % par)], writes=[B("gl%d" % par)])
                            tt("dve", actT[:, f, :], gls[par][:], cvs[par][:], ALU.mult, [B("gl%d" % par), B("cv%d" % par)], [B("actT")])
                        group_done()
                fw.barrier()
                xn3, junkN3, junkP3, tmps3 = boundary_bufs(ph3)
                fw.dma("sp", lambda e: e.dma_start(out=gbuf[:], in_=g_ffn_post[0:1, :].to_broadcast([128, D])), writes=[B("gbuf")], dsem=ds_g)
                out_proj_tokmajor(actT, "actT", [(0, 8), (8, 16), (16, 22)])
                nxt = ti + 1 < ntiles
                for s in range(6):
                    if s < 4:
                        tb = s
                        post_tb(tb, tmps3, junkP3)
                        ev = fw.dma("sp", lambda e, t0=t0, tb=tb: e.dma_start(out=out_d[t0 + 128 * tb:t0 + 128 * tb + 128, :], in_=xres[:, tb, :]),
                                    reads=[B(XN[tb])], writes=[B("out%d" % tb)], dsem=ds_ot[tb])
                        out_events.append(ev)
                        if nxt:
                            t1 = t0 + TT
                            fw.dma("sp", lambda e, t1=t1, tb=tb: e.dma_start(out=xres[:, tb, :], in_=x_d[t1 + 128 * tb:t1 + 128 * tb + 128, :]),
                                   writes=[B(XN[tb])], dsem=ds_xt[tb])
                    if nxt and 1 <= s <= 4:
                        normA_tb(s - 1, xn3)
                    if nxt and 2 <= s <= 5:
                        normB_tb(s - 2, 0, xn3)
                fw.barrier()

        for ev in out_events:
            fw._wait("sp", ev)

        with nc.allow_non_contiguous_dma(reason="small param layouts"):
            with nc.Block() as block:
                @block.tensor
                def _(eng):
                    fw.replay("pe", eng)

                @block.scalar
                def _(eng):
                    fw.replay("act", eng)

                @block.vector
                def _(eng):
                    fw.replay("dve", eng)

                @block.gpsimd
                def _(eng):
                    fw.replay("pool", eng)

                @block.sync
                def _(eng):
                    fw.replay("sp", eng)
        build.arena_peak = arena.peak
    build.dbg_specs = dbg_specs
    build.nops = {e: len(v) for e, v in fw.ops.items()}
    return nc


def ssm_setup(nc, fw, B, sb, ps, PSB, arena, L):
    a_re_d, a_im_d, ldt_d = L["a_re_d"], L["a_im_d"], L["ldt_d"]
    b_re_d, b_im_d, c_re_d, c_im_d = L["b_re_d"], L["b_im_d"], L["c_re_d"], L["c_im_d"]
    Gm, BcRe, BcIm, CmRe, CmIm, Ctab, Stab, rho4 = L["Gm"], L["BcRe"], L["BcIm"], L["CmRe"], L["CmIm"], L["Ctab"], L["Stab"], L["rho4"]
    ident_f, maskG, kk = L["ident_f"], L["maskG"], L["kk"]
    arK = Arena(L["kT"][:].rearrange("p a b -> p (a b)"), 32768)
    arV = Arena(L["vtok"][:].rearrange("p a b -> p (a b)"), 32768)
    ds = fw.new_dsem()
    cnt = {"i": 0}
    CE = ("pe", "act", "dve")
    NP = 16

    def nbk():
        cnt["i"] = (cnt["i"] + 1) % 8
        return cnt["i"]

    def tt(out_ap, a, b, op, reads, writes, eng="dve"):
        fw.op(eng, lambda e: e.tensor_tensor(out=out_ap, in0=a, in1=b, op=op), reads=reads, writes=writes)

    def mset(ap, val, bname):
        fw.op("dve", lambda e: e.memset(ap, val), writes=[B(bname)])

    with arena.scope() as ph:
        are = sb(ph, "s_are", [128, 16], F32); aim = sb(ph, "s_aim", [128, 16], F32); dt = sb(ph, "s_dt", [128, 16], F32)
        th = sb(ph, "s_th", [128, 16], F32); mag = sb(ph, "s_mag", [128, 16], F32)
        t16 = [sb(ph, "s_t%d" % i, [128, 16], F32) for i in range(4)]
        ti16 = sb(ph, "s_ti", [128, 16], I32)
        lamR = sb(ph, "s_lamR", [128, 5, 16], F32); lamI = sb(ph, "s_lamI", [128, 5, 16], F32)
        muR = sb(ph, "s_muR", [128, 4, 16], F32); muI = sb(ph, "s_muI", [128, 4, 16], F32)
        fR = sb(ph, "s_fR", [128, 16], F32); fI = sb(ph, "s_fI", [128, 16], F32)
        Sa = sb(ph, "s_Sa", [16, 256], F32)
        dsa = fw.new_dsem()
        fw.dma("sp", lambda e: e.dma_start(out=Sa[:, 0:128], in_=a_re_d.rearrange("(q g) p -> q (g p)", g=2)), writes=[B("s_Sa0")], dsem=dsa)
        fw.dma("sp", lambda e: e.dma_start(out=Sa[:, 128:256], in_=a_im_d.rearrange("(q g) p -> q (g p)", g=2)), writes=[B("s_Sa1")], dsem=dsa)
        fw.op("pe", lambda e: e.transpose(out=ps[6][:, 0:16], in_=Sa[0:16, 0:128], identity=ident_f[0:16, 0:16]), reads=[B("s_Sa0"), B("ident_f")], writes=[PSB[6]], signal=False)
        fw.op("pe", lambda e: e.transpose(out=ps[6][:, 16:32], in_=Sa[0:16, 128:256], identity=ident_f[0:16, 0:16]), reads=[B("s_Sa1"), B("ident_f")], writes=[PSB[6]], signal=True)
        fw.op("act", lambda e: e.activation(out=are[:], in_=ps[6][:, 0:16], func=AF.Copy), reads=[PSB[6]], writes=[B("s_are")])
        fw.op("act", lambda e: e.activation(out=aim[:], in_=ps[6][:, 16:32], func=AF.Copy), reads=[PSB[6]], writes=[B("s_aim")])
        ldv = ldt_d.rearrange("o (q g) -> o g q", g=2)
        for g in range(2):
            fw.dma("sp", lambda e, g=g: e.dma_start(out=dt[64 * g:64 * g + 64, :], in_=ldv[0:1, g, :].to_broadcast([64, 16])), writes=[B("s_dt%d" % g)], dsem=ds)
        phvP = arV.scope().__enter__()
        FFR = sb(phvP, "s_FFR", [128, NP, 160], F32); FFI = sb(phvP, "s_FFI", [128, NP, 160], F32)
        CnR = sb(phvP, "s_CnR", [32, NP, 128], F32)
        BrR = sb(phvP, "s_BrR", [128, NP, 32], F32); BrI = sb(phvP, "s_BrI", [128, NP, 32], F32)
        CnI = sb(ph, "s_CnI", [32, NP, 128], F32)
        CbR = sb(ph, "s_CbR", [128, NP, 32], F32); CbI = sb(ph, "s_CbI", [128, NP, 32], F32)
        BbR = sb(ph, "s_BbR", [128, NP, 32], F32); BbI = sb(ph, "s_BbI", [128, NP, 32], F32)
        p1 = sb(ph, "s_p1", [128, NP, 32], F32); p2 = sb(ph, "s_p2", [128, NP, 32], F32)
        for t_, tn in ((BrR, "s_BrR"), (BrI, "s_BrI"), (CnR, "s_CnR"), (CnI, "s_CnI")):
            mset(t_[:], 0.0, tn)
        bvR = b_re_d.rearrange("(q g) p c -> g p q c", g=2)
        bvI = b_im_d.rearrange("(q g) p c -> g p q c", g=2)
        cvR = c_re_d.rearrange("(q g) c p -> g c q p", g=2)
        cvI = c_im_d.rearrange("(q g) c p -> g c q p", g=2)
        ds2 = fw.new_dsem()
        for g in range(2):
            fw.dma("sp", lambda e, g=g: e.dma_start(out=BrR[64 * g:64 * g + 64, :, 16 * g:16 * g + 16], in_=bvR[g]), reads=[B("s_BrR")], writes=[B("s_BrR_%d" % g)], dsem=ds2)
            fw.dma("sp", lambda e, g=g: e.dma_start(out=BrI[64 * g:64 * g + 64, :, 16 * g:16 * g + 16], in_=bvI[g]), reads=[B("s_BrI")], writes=[B("s_BrI_%d" % g)], dsem=ds2)
            fw.dma("sp", lambda e, g=g: e.dma_start(out=CnR[16 * g:16 * g + 16, :, 64 * g:64 * g + 64], in_=cvR[g]), reads=[B("s_CnR")], writes=[B("s_CnR_%d" % g)], dsem=ds2)
            fw.dma("sp", lambda e, g=g: e.dma_start(out=CnI[16 * g:16 * g + 16, :, 64 * g:64 * g + 64], in_=cvI[g]), reads=[B("s_CnI")], writes=[B("s_CnI_%d" % g)], dsem=ds2)
        BRR = [B("s_BrR_0"), B("s_BrR_1")]; BRI = [B("s_BrI_0"), B("s_BrI_1")]
        CNR = [B("s_CnR_0"), B("s_CnR_1")]; CNI = [B("s_CnI_0"), B("s_CnI_1")]
        fw.op("act", lambda e: e.activation(out=dt[:], in_=dt[:], func=AF.Exp), reads=[B("s_dt0"), B("s_dt1")], writes=[B("s_dt")])
        tt(t16[0][:], are[:], dt[:], ALU.mult, [B("s_are"), B("s_dt")], [B("s_t0")])
        fw.op("act", lambda e: e.activation(out=mag[:], in_=t16[0][:], func=AF.Exp), reads=[B("s_t0")], writes=[B("s_mag")])
        tt(th[:], aim[:], dt[:], ALU.mult, [B("s_aim"), B("s_dt")], [B("s_th")])

        def sincos(ang, angn, outc, outcn, outs, outsn, tu, tun, tnf, tnfn, tint, tintn, tu2, tu2n):
            for (shift, out, outn, u_, un_) in ((0.25, outc, outcn, tu, tun), (0.0, outs, outsn, tu2, tu2n)):
                fw.op("dve", lambda e, shift=shift, u_=u_: e.tensor_scalar(out=u_, in0=ang, scalar1=1.0 / TWO_PI, scalar2=shift, op0=ALU.mult, op1=ALU.add),
                      reads=[B(angn)], writes=[B(un_)])
                fw.op("dve", lambda e, u_=u_: e.tensor_copy(out=tint, in_=u_), reads=[B(un_)], writes=[B(tintn)])
                fw.op("dve", lambda e: e.tensor_copy(out=tnf, in_=tint), reads=[B(tintn)], writes=[B(tnfn)])
                tt(u_, u_, tnf, ALU.subtract, [B(un_), B(tnfn)], [B(un_)])
                fw.op("act", lambda e, out=out, u_=u_: e.activation(out=out, in_=u_, func=AF.Sin, scale=TWO_PI), reads=[B(un_)], writes=[B(outn)])

        tcos = sb(ph, "s_tcos", [128, 16], F32); tsin = sb(ph, "s_tsin", [128, 16], F32)
        sincos(th[:], "s_th", tcos[:], "s_tcos", tsin[:], "s_tsin", t16[0][:], "s_t0", t16[3][:], "s_t3", ti16[:], "s_ti", t16[1][:], "s_t1")
        mset(lamR[:, 0, :], 1.0, "s_lamR0"); mset(lamI[:, 0, :], 0.0, "s_lamI0")
        mset(muR[:, 0, :], 1.0, "s_muR0"); mset(muI[:, 0, :], 0.0, "s_muI0")
        tt(lamR[:, 1, :], mag[:], tcos[:], ALU.mult, [B("s_mag"), B("s_tcos")], [B("s_lamR")])
        tt(lamI[:, 1, :], mag[:], tsin[:], ALU.mult, [B("s_mag"), B("s_tsin")], [B("s_lamI")])
        tt(t16[0][:], mag[:], mag[:], ALU.mult, [B("s_mag")], [B("s_t0")])
        fw.op("dve", lambda e: e.reciprocal(out=t16[3][:], in_=t16[0][:]), reads=[B("s_t0")], writes=[B("s_t3")])
        tt(muR[:, 1, :], lamR[:, 1, :], t16[3][:], ALU.mult, [B("s_lamR"), B("s_t3")], [B("s_muR")])
        fw.op("dve", lambda e: e.scalar_tensor_tensor(out=muI[:, 1, :], in0=lamI[:, 1, :], scalar=-1.0, in1=t16[3][:], op0=ALU.mult, op1=ALU.mult),
              reads=[B("s_lamI"), B("s_t3")], writes=[B("s_muI")])

        def cmul(oR, oI, oRn, oIn, aR, aI, aRn, aIn, bR, bI, bRn, bIn):
            tt(t16[0][:], aR, bR, ALU.mult, [B(aRn), B(bRn)], [B("s_t0")])
            tt(t16[1][:], aI, bI, ALU.mult, [B(aIn), B(bIn)], [B("s_t1")])
            tt(t16[2][:], aR, bI, ALU.mult, [B(aRn), B(bIn)], [B("s_t2")])
            tt(t16[3][:], aI, bR, ALU.mult, [B(aIn), B(bRn)], [B("s_t3")])
            tt(oR, t16[0][:], t16[1][:], ALU.subtract, [B("s_t0"), B("s_t1")], [B(oRn)])
            tt(oI, t16[2][:], t16[3][:], ALU.add, [B("s_t2"), B("s_t3")], [B(oIn)])

        for n in range(2, 5):
            cmul(lamR[:, n, :], lamI[:, n, :], "s_lamR", "s_lamI", lamR[:, n - 1, :], lamI[:, n - 1, :], "s_lamR", "s_lamI", lamR[:, 1, :], lamI[:, 1, :], "s_lamR", "s_lamI")
        for n in range(2, 4):
            cmul(muR[:, n, :], muI[:, n, :], "s_muR", "s_muI", muR[:, n - 1, :], muI[:, n - 1, :], "s_muR", "s_muI", muR[:, 1, :], muI[:, 1, :], "s_muR", "s_muI")
        tt(t16[0][:], mag[:], mag[:], ALU.mult, [B("s_mag")], [B("s_t0")])
        tt(rho4[:], t16[0][:], t16[0][:], ALU.mult, [B("s_t0")], [B("rho4")])
        fw.op("dve", lambda e: e.tensor_scalar(out=t16[0][:], in0=lamR[:, 1, :], scalar1=-1.0, scalar2=None, op0=ALU.add), reads=[B("s_lamR")], writes=[B("s_t0")])
        tt(t16[1][:], are[:], are[:], ALU.mult, [B("s_are")], [B("s_t1")])
        tt(t16[2][:], aim[:], aim[:], ALU.mult, [B("s_aim")], [B("s_t2")])
        tt(t16[1][:], t16[1][:], t16[2][:], ALU.add, [B("s_t1"), B("s_t2")], [B("s_t1")])
        fw.op("dve", lambda e: e.reciprocal(out=t16[3][:], in_=t16[1][:]), reads=[B("s_t1")], writes=[B("s_t3")])
        tt(t16[1][:], t16[0][:], are[:], ALU.mult, [B("s_t0"), B("s_are")], [B("s_t1")])
        tt(t16[2][:], lamI[:, 1, :], aim[:], ALU.mult, [B("s_lamI"), B("s_aim")], [B("s_t2")])
        tt(t16[1][:], t16[1][:], t16[2][:], ALU.add, [B("s_t1"), B("s_t2")], [B("s_t1")])
        tt(fR[:], t16[1][:], t16[3][:], ALU.mult, [B("s_t1"), B("s_t3")], [B("s_fR")])
        tt(t16[1][:], lamI[:, 1, :], are[:], ALU.mult, [B("s_lamI"), B("s_are")], [B("s_t1")])
        tt(t16[2][:], t16[0][:], aim[:], ALU.mult, [B("s_t0"), B("s_aim")], [B("s_t2")])
        tt(t16[1][:], t16[1][:], t16[2][:], ALU.subtract, [B("s_t1"), B("s_t2")], [B("s_t1")])
        tt(fI[:], t16[1][:], t16[3][:], ALU.mult, [B("s_t1"), B("s_t3")], [B("s_fI")])
        LAMR = [B("s_lamR"), B("s_lamR0")]; LAMI = [B("s_lamI"), B("s_lamI0")]; MUR = [B("s_muR"), B("s_muR0")]; MUI = [B("s_muI"), B("s_muI0")]

        with arK.scope() as phk:
            ang = sb(phk, "s_ang", [128, NP, 128], F32); tu = sb(phk, "s_tu", [128, NP, 128], F32)
            tnf = sb(phk, "s_tnf", [128, NP, 128], F32); tint = sb(phk, "s_tint", [128, NP, 128], I32)
            with arena.scope() as phv:
                tu2 = sb(phv, "s_tu2", [128, NP, 128], F32)
                tt(ang[:], th[:].unsqueeze(2).to_broadcast([128, NP, 128]), kk[:].unsqueeze(1).to_broadcast([128, NP, 128]), ALU.mult,
                   [B("s_th"), B("kk")], [B("s_ang")])
                sincos(ang[:], "s_ang", Ctab[:], "Ctab", Stab[:], "Stab", tu[:], "s_tu", tnf[:], "s_tnf", tint[:], "s_tint", tu2[:], "s_tu2")
                fw.barrier(engs=CE, with_sp=True)

        with arK.scope() as phk:
            EER = sb(phk, "s_EER", [128, NP, 128], F32); EEIn = sb(phk, "s_EEIn", [128, NP, 128], F32)
            E3R = sb(phk, "s_E3R", [128, NP, 128], F32); E3I = sb(phk, "s_E3I", [128, NP, 128], F32)
            bc = lambda t, n=None: (t if n is None else t[:, n, :]).unsqueeze(2).to_broadcast([128, NP, 32])

            def cmul_b(oR, oI, oRb, oIb, sR, sI, sRb, sIb, xR, xI, xRb, xIb, neg_im=False):
                tt(p1[:], xR, sR, ALU.mult, xRb + sRb, [B("s_p1")])
                tt(p2[:], xI, sI, ALU.mult, xIb + sIb, [B("s_p2")])
                tt(oR, p1[:], p2[:], ALU.subtract, [B("s_p1"), B("s_p2")], oRb)
                tt(p1[:], xI, sR, ALU.mult, xIb + sRb, [B("s_p1")])
                tt(p2[:], xR, sI, ALU.mult, xRb + sIb, [B("s_p2")])
                if neg_im:
                    fw.op("dve", lambda e: e.scalar_tensor_tensor(out=oI, in0=p1[:], scalar=-1.0, in1=p2[:], op0=ALU.mult, op1=ALU.subtract),
                          reads=[B("s_p1"), B("s_p2")], writes=oIb)
                else:
                    tt(oI, p1[:], p2[:], ALU.add, [B("s_p1"), B("s_p2")], oIb)

            cmul_b(BbR[:], BbI[:], [B("s_BbR")], [B("s_BbI")], bc(fR[:]), bc(fI[:]), [B("s_fR")], [B("s_fI")], BrR[:], BrI[:], BRR, BRI)
            for (Cn, Cnb, Cb, Cbn) in ((CnR, CNR, CbR, "s_CbR"), (CnI, CNI, CbI, "s_CbI")):
                bi = nbk()
                for qq in range(NP):
                    fw.op("pe", lambda e, Cn=Cn, qq=qq, bi=bi: e.transpose(out=ps[bi][:, qq * 32:(qq + 1) * 32], in_=Cn[0:32, qq, :], identity=ident_f[0:32, 0:32]),
                          reads=Cnb + [B("ident_f")], writes=[PSB[bi]], signal=(qq == NP - 1))
                fw.op("act", lambda e, Cb=Cb, bi=bi: e.activation(out=Cb[:].rearrange("p a c -> p (a c)"), in_=ps[bi][:, 0:NP * 32], func=AF.Copy), reads=[PSB[bi]], writes=[B(Cbn)])
            for n in range(5):
                cmul_b(FFR[:, :, 32 * n:32 * n + 32], FFI[:, :, 32 * n:32 * n + 32], [B("s_FFR")], [B("s_FFI")], bc(lamR, n), bc(lamI, n), LAMR, LAMI,
                       CbR[:], CbI[:], [B("s_CbR")], [B("s_CbI")])
            fw.op("act", lambda e: e.activation(out=CmRe[:], in_=FFR[:, :, 32:160], func=AF.Copy), reads=[B("s_FFR")], writes=[B("CmRe")])
            fw.op("act", lambda e: e.activation(out=CmIm[:], in_=FFI[:, :, 32:160], func=AF.Copy, scale=-1.0), reads=[B("s_FFI")], writes=[B("CmIm")])
            for j in range(4):
                cmul_b(EER[:, :, 32 * j:32 * j + 32], EEIn[:, :, 32 * j:32 * j + 32], [B("s_EER")], [B("s_EEIn")], bc(muR, j), bc(muI, j), MUR, MUI,
                       BbR[:], BbI[:], [B("s_BbR")], [B("s_BbI")], neg_im=True)
                cmul_b(E3R[:, :, 32 * j:32 * j + 32], E3I[:, :, 32 * j:32 * j + 32], [B("s_E3R")], [B("s_E3I")], bc(lamR, 3 - j), bc(lamI, 3 - j), LAMR, LAMI,
                       BbR[:], BbI[:], [B("s_BbR")], [B("s_BbI")])
            for q in range(NP):
                bi = nbk()
                fw.op("pe", lambda e, q=q, bi=bi: e.matmul(ps[bi][:, 0:128], lhsT=EER[:, q, :], rhs=FFR[:, q, 0:128], start=True, stop=False),
                      reads=[B("s_EER"), B("s_FFR")], writes=[PSB[bi]], signal=False)
                fw.op("pe", lambda e, q=q, bi=bi: e.matmul(ps[bi][:, 0:128], lhsT=EEIn[:, q, :], rhs=FFI[:, q, 0:128], start=False, stop=True),
                      reads=[B("s_EEIn"), B("s_FFI")], writes=[PSB[bi]], signal=False)
                fw.op("pe", lambda e, q=q, bi=bi: e.transpose(out=ps[bi][:, 128:256], in_=E3R[:, q, :], identity=ident_f[:]),
                      reads=[B("s_E3R"), B("ident_f")], writes=[PSB[bi]], signal=False)
                fw.op("pe", lambda e, q=q, bi=bi: e.transpose(out=ps[bi][:, 256:384], in_=E3I[:, q, :], identity=ident_f[:]),
                      reads=[B("s_E3I"), B("ident_f")], writes=[PSB[bi]], signal=True)
                tt(Gm[:, q, :], ps[bi][:, 0:128], maskG[:], ALU.mult, [PSB[bi], B("maskG")], [B("Gm")])
                fw.op("act", lambda e, q=q, bi=bi: e.activation(out=BcRe[:, q, :], in_=ps[bi][:, 128:256], func=AF.Copy), reads=[PSB[bi]], writes=[B("BcRe")])
                fw.op("act", lambda e, q=q, bi=bi: e.activation(out=BcIm[:, q, :], in_=ps[bi][:, 256:384], func=AF.Copy), reads=[PSB[bi]], writes=[B("BcIm")])
            fw.barrier(engs=CE, with_sp=True)
    fw.barrier(engs=CE, with_sp=True)


_CACHE = {}

_PARAM_SHAPES = {
    "norm_mix_pre": (1, D), "norm_mix_post": (1, D), "w_in": (D, 4096), "b_gate": (1, 2048),
    "ssm_a_re": (32, 64), "ssm_a_im": (32, 64), "ssm_log_dt": (1, 32), "ssm_b_re": (32, 64, 16), "ssm_b_im": (32, 64, 16),
    "ssm_c_re": (32, 16, 64), "ssm_c_im": (32, 16, 64), "ssm_d": (1, 512), "ssm_w_glu": (512, 512), "ssm_b_glu": (1, 512),
    "w_branch_attn": (512, D), "w_branch_ssm": (512, D), "w_out": (D, D), "norm_xa_pre": (1, D), "norm_xa_post": (1, D),
    "norm_mem": (1, D), "xa_wq": (D, D), "xa_wk": (D, D), "xa_wv": (D, D), "xa_wo": (D, D), "norm_ffn_pre": (1, D),
    "norm_ffn_post": (1, D), "ffn_w_up": (D, 2 * DFF), "ffn_conv_w": (3, 2 * DFF), "ffn_conv_b": (1, 2 * DFF), "ffn_w_down": (DFF, D),
}


def make_in_maps(inputs, ncores=8):
    params = {k: np.ascontiguousarray(np.asarray(inputs[k], dtype=np.float32).reshape(shp)) for k, shp in _PARAM_SHAPES.items()}
    x = np.asarray(inputs["x"], dtype=np.float32)
    mem = np.asarray(inputs["mem"], dtype=np.float32)
    maps = []
    for b in range(ncores):
        m = dict(params)
        m["x"] = np.ascontiguousarray(x[b])
        m["mem"] = np.ascontiguousarray(mem[b])
        maps.append(m)
    return maps


def kernel(**inputs):
    if "nc" not in _CACHE:
        _CACHE["nc"] = build()
    nc = _CACHE["nc"]
    in_maps = make_in_maps(inputs, 8)
    res = run_bass_kernel_spmd(nc, in_maps, core_ids=list(range(8)))
    out = np.stack([np.asarray(res.results[b]["out"], dtype=np.float32) for b in range(8)], axis=0)
    return out
```

```python
import contextlib
import numpy as np
import concourse.bass as bass
import concourse.mybir as mybir
from concourse.bass_utils import run_bass_kernel_spmd

F32 = mybir.dt.float32
BF16 = mybir.dt.bfloat16
I32 = mybir.dt.int32
AF = mybir.ActivationFunctionType
ALU = mybir.AluOpType

S = 4096
D = 1024
TT = 512
NTILES = S // TT
DFF = 2816
NMEM = 256
EPS = 1e-6
TWO_PI = 6.283185307179586
SEM_LIMIT = 30000
ARENA_BYTES = 43520


class Buf:
    __slots__ = ("name", "w", "r", "psum")

    def __init__(self, name):
        self.name = name
        self.w = None
        self.r = []
        self.psum = name.startswith("ps")


class _Rec:
    def __init__(self):
        self.call = None

    def __getattr__(self, name):
        def f(*a, **k):
            self.call = (name, a, k)
            return self
        return f


def _record(fn):
    r = _Rec()
    fn(r)
    assert r.call is not None
    return r.call


class ArenaScope:
    def __init__(self, arena):
        self.arena = arena

    def __enter__(self):
        self.mark = self.arena.top
        return self

    def __exit__(self, *a):
        self.arena.top = self.mark
        return False


class Arena:
    def __init__(self, tensor, nbytes):
        self.t = tensor
        self.nbytes = nbytes
        self.top = 0
        self.peak = 0

    def scope(self):
        return ArenaScope(self)

    def alloc(self, name, shape, dt):
        esz = 4 if dt in (F32, I32) else 2
        n = 1
        for s_ in shape[1:]:
            n *= s_
        nb_ = (n * esz + 31) // 32 * 32
        off = self.top
        self.top += nb_
        self.peak = max(self.peak, self.top)
        assert self.top <= self.nbytes, "arena overflow %s: %d > %d" % (name, self.top, self.nbytes)
        ap = self.t[0:shape[0], off // 2:(off + n * esz) // 2]
        if esz == 4:
            ap = ap.bitcast(dt)
        if len(shape) == 3:
            ap = ap.rearrange("p (a b) -> p a b", a=shape[1])
        elif len(shape) == 4:
            ap = ap.rearrange("p (a b c) -> p a b c", a=shape[1], b=shape[2])
        return ap


class FW:
    ENG = ["pe", "act", "dve", "pool", "sp"]

    def __init__(self, nc, stack):
        self.nc = nc
        self.stack = stack
        self.ops = {e: [] for e in self.ENG}
        self.sem = {e: stack.enter_context(nc.semaphore("c_" + e + "0")) for e in self.ENG}
        self.semn = {e: 0 for e in self.ENG}
        self.cnt = {e: 0 for e in self.ENG}
        self.last = {e: None for e in self.ENG}
        self.waited = {e: {} for e in self.ENG}
        self.pend = {e: ([], []) for e in self.ENG}
        self.dsem = {}
        self.nd = 0

    def new_dsem(self):
        self.nd += 1
        return self.stack.enter_context(self.nc.semaphore("d%d" % self.nd))

    def _wait(self, e, ev):
        if ev is None or ev == "PEND":
            return
        sem, val = ev
        key = id(sem)
        if key in self.dsem:
            val = max(val, self.dsem[key])
        if self.waited[e].get(key, 0) >= val:
            return
        self.waited[e][key] = val
        self.ops[e].append(("wait", sem, val))

    def deps(self, e, reads, writes):
        for b in reads:
            if b.w == "PEND":
                assert b in self.pend[e][1], "read of pending write: " + b.name
            else:
                self._wait(e, b.w)
            if b.psum:
                for ev in b.r:
                    self._wait(e, ev)
        for b in writes:
            if b.w == "PEND":
                assert b in self.pend[e][1], "write of pending write: " + b.name
            else:
                self._wait(e, b.w)
            for ev in b.r:
                self._wait(e, ev)
            for e2 in self.ENG:
                if e2 != e:
                    assert b not in self.pend[e2][0], "WAR on pending read: " + b.name

    def op(self, e, fn, reads=(), writes=(), signal=True):
        call = _record(fn)
        self.deps(e, reads, writes)
        pr, pw = self.pend[e]
        pr.extend(reads)
        pw.extend(writes)
        for b in writes:
            b.w = "PEND"
            b.r = []
        if signal:
            if self.cnt[e] >= SEM_LIMIT:
                self.semn[e] += 1
                self.sem[e] = self.stack.enter_context(self.nc.semaphore("c_%s%d" % (e, self.semn[e])))
                self.cnt[e] = 0
            self.cnt[e] += 1
            sem, val = self.sem[e], self.cnt[e]
            self.ops[e].append(("op", call, sem, 1))
            ev = (sem, val)
            self.last[e] = ev
            for b in pw:
                b.w = ev
                b.r = []
            for b in pr:
                if b.w != ev:
                    b.r.append(ev)
            self.pend[e] = ([], [])
        else:
            self.ops[e].append(("op", call, None, 0))

    def flush(self, e):
        pr, pw = self.pend[e]
        if not pr and not pw:
            return
        item = self.ops[e][-1]
        assert item[0] == "op" and item[2] is None
        if self.cnt[e] >= SEM_LIMIT:
            self.semn[e] += 1
            self.sem[e] = self.stack.enter_context(self.nc.semaphore("c_%s%d" % (e, self.semn[e])))
            self.cnt[e] = 0
        self.cnt[e] += 1
        sem, val = self.sem[e], self.cnt[e]
        self.ops[e][-1] = ("op", item[1], sem, 1)
        ev = (sem, val)
        self.last[e] = ev
        for b in pw:
            b.w = ev
            b.r = []
        for b in pr:
            if b.w != ev:
                b.r.append(ev)
        self.pend[e] = ([], [])

    def dma(self, e, fn, reads=(), writes=(), dsem=None):
        call = _record(fn)
        self.deps(e, reads, writes)
        c = self.dsem.get(id(dsem), 0) + 16
        self.dsem[id(dsem)] = c
        self.ops[e].append(("op", call, dsem, 16))
        ev = (dsem, c)
        for b in writes:
            b.w = ev
            b.r = []
        for b in reads:
            b.r.append(ev)
        return ev

    def barrier(self, engs=("pe", "act", "dve", "pool"), with_sp=False):
        for e in engs:
            assert not self.pend[e][0] and not self.pend[e][1], "pending at barrier " + e
        for e in engs:
            if e == "pe":
                continue
            for e2 in engs:
                if e2 != e:
                    self._wait(e, self.last[e2])
        if with_sp:
            for e2 in engs:
                self._wait("sp", self.last[e2])

    def replay(self, e, eng):
        for item in self.ops[e]:
            if item[0] == "wait":
                eng.wait_ge(item[1], item[2])
            else:
                name, a, k = item[1]
                inst = getattr(eng, name)(*a, **k)
                if item[2] is not None:
                    inst.then_inc(item[2], item[3])


def build(ntiles=NTILES, dbg=None):
    nc = bass.Bass("TRN2", target_bir_lowering=False)
    dbg_specs = {}

    def din(name, shape):
        return nc.dram_tensor(name, list(shape), F32, kind="ExternalInput").ap()

    x_d = din("x", [S, D])
    mem_d = din("mem", [NMEM, D])
    g_mix_pre = din("norm_mix_pre", [1, D]); g_mix_post = din("norm_mix_post", [1, D])
    w_in_d = din("w_in", [D, 4096]); b_gate_d = din("b_gate", [1, 2048])
    a_re_d = din("ssm_a_re", [32, 64]); a_im_d = din("ssm_a_im", [32, 64]); ldt_d = din("ssm_log_dt", [1, 32])
    b_re_d = din("ssm_b_re", [32, 64, 16]); b_im_d = din("ssm_b_im", [32, 64, 16])
    c_re_d = din("ssm_c_re", [32, 16, 64]); c_im_d = din("ssm_c_im", [32, 16, 64])
    d_d = din("ssm_d", [1, 512]); w_glu_d = din("ssm_w_glu", [512, 512]); b_glu_d = din("ssm_b_glu", [1, 512])
    w_ba_d = din("w_branch_attn", [512, D]); w_bs_d = din("w_branch_ssm", [512, D]); w_out_d = din("w_out", [D, D])
    g_xa_pre = din("norm_xa_pre", [1, D]); g_xa_post = din("norm_xa_post", [1, D]); g_mem = din("norm_mem", [1, D])
    wq_d = din("xa_wq", [D, D]); wk_d = din("xa_wk", [D, D]); wv_d = din("xa_wv", [D, D]); wo_d = din("xa_wo", [D, D])
    g_ffn_pre = din("norm_ffn_pre", [1, D]); g_ffn_post = din("norm_ffn_post", [1, D])
    up_d = din("ffn_w_up", [D, 2 * DFF]); cw_d = din("ffn_conv_w", [3, 2 * DFF]); cb_d = din("ffn_conv_b", [1, 2 * DFF])
    down_d = din("ffn_w_down", [DFF, D])
    out_d = nc.dram_tensor("out", [S, D], F32, kind="ExternalOutput").ap()

    def scratch(name, shape):
        return nc.dram_tensor(name, list(shape), BF16, kind="Internal").ap()

    wsrc = {"wk": wk_d, "wv": wv_d, "w_in": w_in_d, "w_glu": w_glu_d, "w_ba": w_ba_d, "w_bs": w_bs_d,
            "w_out": w_out_d, "wq": wq_d, "wo": wo_d, "up": up_d, "down": down_d}
    wbf = {k: scratch(k + "_bf", v.shape) for k, v in wsrc.items()}

    with contextlib.ExitStack() as st:
        E = st.enter_context
        fw = FW(nc, st)
        bufs = {}

        def B(name):
            if name not in bufs:
                bufs[name] = Buf(name)
            return bufs[name]

        def sb(stack, name, shape, dt):
            if isinstance(stack, ArenaScope):
                return stack.arena.alloc(name, list(shape), dt)
            return stack.enter_context(nc.sbuf_tensor(name, list(shape), dt))

        kT = sb(st, "kT", [128, 4, S], BF16)
        vtok = sb(st, "vtok", [128, 32, 512], BF16)
        memKT = sb(st, "memKT", [128, 8, NMEM], BF16)
        memV = sb(st, "memV", [128, 2, D], BF16)
        ident_bf = sb(st, "ident_bf", [128, 128], BF16)
        ident_f = sb(st, "ident_f", [128, 128], F32)
        ntri = sb(st, "ntri", [128, 128], BF16)
        nones = sb(st, "nones", [128, 128], BF16)
        maskb = sb(st, "maskb", [128, 512], BF16)
        nbglu = sb(st, "nbglu", [128, 4], F32)
        pcols = sb(st, "pcols", [128, 96], F32)
        gcols = pcols[:, 0:32].rearrange("p (a b) -> p a b", a=4)
        bgate = pcols[:, 32:48]
        bglu = pcols[:, 48:52]
        convb = pcols[:, 52:96]
        convw = sb(st, "convw", [128, 3, 44], F32)
        halo = sb(st, "halo", [128, 44, 2], F32)
        Gm = sb(st, "Gm", [128, 16, 128], BF16)
        BcRe = sb(st, "BcRe", [128, 16, 128], BF16)
        BcIm = sb(st, "BcIm", [128, 16, 128], BF16)
        CmRe = sb(st, "CmRe", [128, 16, 128], BF16)
        CmIm = sb(st, "CmIm", [128, 16, 128], BF16)
        Ctab = sb(st, "Ctab", [128, 16, 128], F32)
        Stab = sb(st, "Stab", [128, 16, 128], F32)
        rho4 = sb(st, "rho4", [128, 16], F32)
        dvec = sb(st, "dvec", [128, 16], F32)
        XcRe = sb(st, "XcRe", [128, 16], F32)
        XcIm = sb(st, "XcIm", [128, 16], F32)
        maskG = sb(st, "maskG", [128, 128], F32)
        kk = sb(st, "kk", [128, 128], F32)
        gbuf = sb(st, "gbuf", [128, D], F32)
        xres = sb(st, "xres", [128, 4, D], F32)
        hT = sb(st, "hT", [128, 8, TT], BF16)
        ring = [sb(st, "ring%d" % i, [128, 4096], BF16) for i in range(3)]
        ss = sb(st, "ss", [128, 8], F32)
        ssum = sb(st, "ssum", [128, 4], F32)
        ssN = sb(st, "ssN", [128, 4], F32)
        lnN = sb(st, "lnN", [128, 4], F32)
        rstdN = sb(st, "rstdN", [128, 4], F32)
        lnv = sb(st, "lnv", [128, 4], F32)
        rstd = sb(st, "rstd", [128, 4], F32)
        arena_t = sb(st, "arena", [128, ARENA_BYTES // 2], BF16)
        arena = Arena(arena_t, ARENA_BYTES)
        pp = [E(nc.psum_tensor("pp%d" % i, [128, 1024], F32)) for i in range(4)]
        ps = [pp[i // 2][:, 512 * (i % 2):512 * (i % 2) + 512] for i in range(8)]
        PSB = [B("ps%d" % i) for i in range(8)]

        XN = ["xres%d" % tb for tb in range(4)]
        ds_xt = [fw.new_dsem() for _ in range(4)]
        ds_ot = [fw.new_dsem() for _ in range(4)]
        ds_x = fw.new_dsem(); ds_out = fw.new_dsem(); ds_g = fw.new_dsem(); ds_misc = fw.new_dsem()
        ds_ring = [fw.new_dsem() for _ in range(3)]
        ds_dbg = fw.new_dsem()

        out_events = []

        def dump(name, ap, bname, shape, dt=F32):
            if dbg is None or name not in dbg:
                return
            o = nc.dram_tensor("dbg_" + name, list(shape), dt, kind="ExternalOutput").ap()
            dbg_specs[name] = (list(shape), dt)
            bl = bname if isinstance(bname, list) else [bname]
            ev = fw.dma("sp", lambda e: e.dma_start(out=o, in_=ap), reads=[B(b_) for b_ in bl], writes=[B("dbg_" + name)], dsem=fw.new_dsem())
            out_events.append(ev)
            for e_ in ("pe", "act", "dve", "pool"):
                fw._wait(e_, ev)

        def rows512(ap2d):
            return ap2d.rearrange("r (a c) -> (r a) c", c=512)

        pieces = []
        for c in range(4):
            pieces.append(("w_in_%d" % c, [(w_in_d[:, 512 * c:512 * c + 512], wbf["w_in"][:, 512 * c:512 * c + 512])]))
        for k in ("w_glu", "w_ba", "w_bs"):
            pieces.append((k, [(rows512(wsrc[k]), rows512(wbf[k]))]))
        for c in (4, 6, 5, 7):
            pieces.append(("w_in_%d" % c, [(w_in_d[:, 512 * c:512 * c + 512], wbf["w_in"][:, 512 * c:512 * c + 512])]))
        for k in ("w_out", "wk", "wv", "wq", "wo"):
            pieces.append((k, [(rows512(wsrc[k]), rows512(wbf[k]))]))
        for p_ in range(4):
            nj = min(3, 11 - 3 * p_)
            pieces.append(("up_g%d" % p_, [(up_d[:, 768 * p_:768 * p_ + 256 * nj], wbf["up"][:, 768 * p_:768 * p_ + 256 * nj])]))
            pieces.append(("up_v%d" % p_, [(up_d[:, DFF + 768 * p_:DFF + 768 * p_ + 256 * nj], wbf["up"][:, DFF + 768 * p_:DFF + 768 * p_ + 256 * nj])]))
        for g_ in range(3):
            r0, r1 = 1024 * g_, min(1024 * (g_ + 1), DFF)
            pieces.append(("down_%d" % g_, [(rows512(down_d[r0:r1, :]), rows512(wbf["down"][r0:r1, :]))]))
        def piece_name(key, k0, c0):
            if key == "w_in":
                return "w_in_%d" % (c0 // 512)
            if key == "up":
                return ("up_g%d" if c0 < DFF else "up_v%d") % (((c0 % DFF) // 256) // 3)
            if key == "down":
                return "down_%d" % (k0 // 8)
            return key

        ring_state = {"n": 0}
        RB = [B("ring%d" % i) for i in range(3)]

        def wview(key, k0, k1, c0, c1):
            return wbf[key].rearrange("(kc p) n -> p kc n", p=128)[:, k0:k1, c0:c1]

        def ring_load(parts):
            n = ring_state["n"]
            ring_state["n"] += 1
            si = n % 3
            views = []
            off = 0
            for (key, k0, k1, c0, c1) in parts:
                nk, ncol = k1 - k0, c1 - c0
                dstv = ring[si][:, off:off + nk * ncol].rearrange("p (k n) -> p k n", k=nk)
                srcv = wview(key, k0, k1, c0, c1)
                fw.dma("sp", lambda e, dstv=dstv, srcv=srcv: e.dma_start(out=dstv, in_=srcv),
                       reads=[B("wbf_" + piece_name(key, k0, c0))], writes=[RB[si]], dsem=ds_ring[si])
                views.append(dstv)
                off += nk * ncol
            assert off <= 4096
            return si, views

        def tile_groups(ti_):
            g = []
            for c in range(4):
                g.append([("w_in", 0, 8, 512 * c, 512 * c + 512)])
            g.append([("w_glu", 0, 4, 0, 512)])
            for pr in range(4):
                g.append([("w_in", 0, 8, 2048 + 256 * pr, 2048 + 256 * pr + 256), ("w_in", 0, 8, 3072 + 256 * pr, 3072 + 256 * pr + 256)])
                g.append([("w_ba", 0, 4, 256 * pr, 256 * pr + 256), ("w_bs", 0, 4, 256 * pr, 256 * pr + 256)])
            for half in range(2):
                g.append([("w_out", 0, 8, 512 * half, 512 * half + 512)])
            if ti_ == 0:
                for half in range(2):
                    g.append([("wk", 0, 8, 512 * half, 512 * half + 512)])
                for half in range(2):
                    g.append([("wv", 0, 8, 512 * half, 512 * half + 512)])
            for half in range(2):
                g.append([("wq", 0, 8, 512 * half, 512 * half + 512)])
            for half in range(2):
                g.append([("wo", 0, 8, 512 * half, 512 * half + 512)])
            for j in range(11):
                g.append([("up", 0, 8, 256 * j, 256 * j + 256), ("up", 0, 8, DFF + 256 * j, DFF + 256 * j + 256)])
            for (k0, k1) in ((0, 8), (8, 16), (16, 22)):
                for half in range(2):
                    g.append([("down", k0, k1, 512 * half, 512 * half + 512)])
            return g

        groups = []
        for ti_ in range(ntiles):
            groups.extend(tile_groups(ti_))
        gstate = {"issued": 0, "used": 0, "loaded": {}}

        def issue_loads(upto):
            while gstate["issued"] < min(upto, len(groups)):
                gi = gstate["issued"]
                gstate["loaded"][gi] = ring_load(groups[gi])
                gstate["issued"] += 1

        def next_group():
            gi = gstate["used"]
            issue_loads(gi + 1)
            si, views = gstate["loaded"].pop(gi)
            gstate["used"] += 1
            return si, views

        def group_done():
            issue_loads(gstate["used"] + 2)

        bank_rr = {"i": 0}

        def nb():
            i = bank_rr["i"]
            bank_rr["i"] = (i + 1) % 8
            return i

        evac_rr = {"i": 0}

        def evac_copy(out_ap, in_ap, reads, writes, scale=None, eng=None):
            if eng is None:
                eng = "act" if evac_rr["i"] % 2 == 0 else "dve"
                evac_rr["i"] += 1
            if eng == "act":
                if scale is None:
                    fw.op("act", lambda e: e.activation(out=out_ap, in_=in_ap, func=AF.Copy), reads=reads, writes=writes)
                else:
                    fw.op("act", lambda e: e.activation(out=out_ap, in_=in_ap, func=AF.Copy, scale=scale), reads=reads, writes=writes)
            else:
                if scale is None:
                    fw.op("dve", lambda e: e.tensor_copy(out=out_ap, in_=in_ap), reads=reads, writes=writes)
                else:
                    fw.op("dve", lambda e: e.tensor_scalar(out=out_ap, in0=in_ap, scalar1=float(scale), scalar2=None, op0=ALU.mult),
                          reads=reads, writes=writes)

        sig = fw.flush

        def tt(eng, out_ap, a, b, op, reads, writes):
            fw.op(eng, lambda e: e.tensor_tensor(out=out_ap, in0=a, in1=b, op=op), reads=reads, writes=writes)

        with nc.allow_non_contiguous_dma(reason="small param layouts"):
            fw.op("pool", lambda e: e.memset(ident_f[:], 1.0), writes=[B("ident_f")])
            fw.op("pool", lambda e: e.affine_select(out=ident_f[:], in_=ident_f[:], pattern=[[-1, 128]], compare_op=ALU.is_equal,
                                                    fill=0.0, base=0, channel_multiplier=1), reads=[B("ident_f")], writes=[B("ident_f")])
            fw.op("dve", lambda e: e.tensor_copy(out=ident_bf[:], in_=ident_f[:]), reads=[B("ident_f")], writes=[B("ident_bf")])
            fw.op("pool", lambda e: e.memset(nones[:], -1.0), writes=[B("nones")])
            with arena.scope() as ph:
                tmpf = sb(ph, "c_tmpf", [128, 4, 512], F32)
                fw.op("pool", lambda e: e.memset(tmpf[:, 0, 0:128], -1.0), writes=[B("c_tmpf")])
                fw.op("pool", lambda e: e.affine_select(out=tmpf[:, 0, 0:128], in_=tmpf[:, 0, 0:128], pattern=[[-1, 128]], compare_op=ALU.is_ge,
                                                        fill=0.0, base=0, channel_multiplier=1), reads=[B("c_tmpf")], writes=[B("c_tmpf")])
                fw.op("dve", lambda e: e.tensor_copy(out=ntri[:], in_=tmpf[:, 0, 0:128]), reads=[B("c_tmpf")], writes=[B("ntri")])
                fw.op("pool", lambda e: e.memset(tmpf[:, 1, :], 0.0), writes=[B("c_tmpf1")])
                fw.op("pool", lambda e: e.affine_select(out=tmpf[:, 1, :], in_=tmpf[:, 1, :], pattern=[[1, 512]], compare_op=ALU.is_gt,
                                                        fill=-30000.0, base=0, channel_multiplier=-1), reads=[B("c_tmpf1")], writes=[B("c_tmpf1")])
                fw.op("dve", lambda e: e.tensor_copy(out=maskb[:], in_=tmpf[:, 1, :]), reads=[B("c_tmpf1")], writes=[B("maskb")])
                fw.barrier(with_sp=True)
            fw.op("pool", lambda e: e.memset(maskG[:], 1.0), writes=[B("maskG")])
            fw.op("pool", lambda e: e.affine_select(out=maskG[:].rearrange("p (i c) -> p i c", i=4), in_=maskG[:].rearrange("p (i c) -> p i c", i=4), pattern=[[32, 4], [0, 32]],
                                                    compare_op=ALU.is_ge, fill=0.0, base=31, channel_multiplier=-1), reads=[B("maskG")], writes=[B("maskG")])
            with arena.scope() as phk_:
                kki = sb(phk_, "c_kki", [128, 128], I32)
                fw.op("pool", lambda e: e.iota(kki[:], pattern=[[4, 128]], base=4, channel_multiplier=0), writes=[B("c_kki")])
                fw.op("dve", lambda e: e.tensor_copy(out=kk[:], in_=kki[:]), reads=[B("c_kki")], writes=[B("kk")])
                fw.barrier(with_sp=True)
            fw.op("pool", lambda e: e.memset(halo[:], 0.0), writes=[B("halo")])
            fw.op("pool", lambda e: e.memset(XcRe[:], 0.0), writes=[B("XcRe")])
            fw.op("pool", lambda e: e.memset(XcIm[:], 0.0), writes=[B("XcIm")])
            cast_evs = []
            for (pname, parts) in pieces:
                assert len(parts) == 1
                sv, dv = parts[0]
                assert sv.shape[0] <= 2048
                if len(cast_evs) >= 2:
                    fw._wait("pool", cast_evs[-2])
                ev_ = fw.dma("pool", lambda e, sv=sv, dv=dv: e.dma_start(out=dv, in_=sv), writes=[B("wbf_" + pname)], dsem=fw.new_dsem())
                cast_evs.append(ev_)

            if True:
                r2f = ring[2][:].bitcast(F32)
                S1 = r2f[0:96, 0:128]; S2 = r2f[0:88, 128:256]; S3 = r2f[0:44, 256:384]; S4 = r2f[0:16, 384:512]
                dsp = fw.new_dsem()
                for i, g in enumerate([g_mix_pre, g_xa_pre, g_ffn_pre, g_mem]):
                    fw.dma("sp", lambda e, i=i, g=g: e.dma_start(out=S1[8 * i:8 * i + 8, :], in_=g.rearrange("o (k p) -> (o k) p", p=128)), writes=[B("pS1_%d" % i)], dsem=dsp)
                fw.dma("sp", lambda e: e.dma_start(out=S1[32:48, :], in_=b_gate_d.rearrange("o (k p) -> (o k) p", p=128)), writes=[B("pS1_4")], dsem=dsp)
                fw.dma("sp", lambda e: e.dma_start(out=S1[48:52, :], in_=b_glu_d.rearrange("o (k p) -> (o k) p", p=128)), writes=[B("pS1_5")], dsem=dsp)
                fw.dma("sp", lambda e: e.dma_start(out=S1[52:96, :], in_=cb_d.rearrange("o (c p) -> (o c) p", p=128)), writes=[B("pS1_6")], dsem=dsp)
                cwv = cw_d.rearrange("t (c p) -> t c p", p=128)
                for t in range(2):
                    fw.dma("sp", lambda e, t=t: e.dma_start(out=S2[44 * t:44 * t + 44, :], in_=cwv[t]), writes=[B("pS2_%d" % t)], dsem=dsp)
                fw.dma("sp", lambda e: e.dma_start(out=S3[:, :], in_=cwv[2]), writes=[B("pS3")], dsem=dsp)
                for i in range(4):
                    fw.dma("sp", lambda e, i=i: e.dma_start(out=S4[:, 32 * i:32 * i + 32], in_=d_d.rearrange("o (q c) -> (o q) c", c=32)), writes=[B("pS4_%d" % i)], dsem=dsp)
                bp = 7
                fw.op("pe", lambda e: e.transpose(out=ps[bp][:, 0:96], in_=S1[0:96, :], identity=ident_f[0:96, 0:96]),
                      reads=[B("pS1_%d" % i) for i in range(7)] + [B("ident_f")], writes=[PSB[bp]], signal=False)
                fw.op("pe", lambda e: e.transpose(out=ps[bp][:, 96:184], in_=S2[0:88, :], identity=ident_f[0:88, 0:88]),
                      reads=[B("pS2_0"), B("pS2_1"), B("ident_f")], writes=[PSB[bp]], signal=False)
                fw.op("pe", lambda e: e.transpose(out=ps[bp][:, 184:228], in_=S3[0:44, :], identity=ident_f[0:44, 0:44]),
                      reads=[B("pS3"), B("ident_f")], writes=[PSB[bp]], signal=False)
                fw.op("pe", lambda e: e.transpose(out=ps[bp][:, 228:244], in_=S4[0:16, :], identity=ident_f[0:16, 0:16]),
                      reads=[B("pS4_%d" % i) for i in range(4)] + [B("ident_f"), RB[2]], writes=[PSB[bp]], signal=True)
                fw.op("act", lambda e: e.activation(out=pcols[:], in_=ps[bp][:, 0:96], func=AF.Copy), reads=[PSB[bp]],
                      writes=[B("gcols"), B("bgate"), B("bglu"), B("convb")])
                fw.op("act", lambda e: e.activation(out=convw[:].rearrange("p t c -> p (t c)"), in_=ps[bp][:, 96:228], func=AF.Copy), reads=[PSB[bp]], writes=[B("convw")])
                fw.op("act", lambda e: e.activation(out=dvec[:], in_=ps[bp][:, 228:244], func=AF.Copy), reads=[PSB[bp]], writes=[B("dvec")])
                fw.op("dve", lambda e: e.tensor_scalar(out=nbglu[:], in0=bglu[:], scalar1=-1.0, scalar2=None, op0=ALU.mult), reads=[B("bglu")], writes=[B("nbglu")])
            ssm_setup(nc, fw, B, sb, ps, PSB, arena, locals())

        def rms_to_hT(ph, src, srcbuf, ntb, gi, dst, dstbuf):
            xn = sb(ph, "xn", [128, ntb, D], BF16)
            junk = sb(ph, "junk", [128, D], BF16)
            sbn = srcbuf if isinstance(srcbuf, list) else [srcbuf] * ntb
            for tb in range(ntb):
                fw.op("act", lambda e, tb=tb: e.activation(out=junk[:], in_=src[:, tb, :], func=AF.Square, accum_out=ss[:, tb:tb + 1]),
                      reads=[B(sbn[tb])], writes=[B("junk"), B("ss")])
            fw.op("act", lambda e: e.activation(out=lnv[:, 0:ntb], in_=ss[:, 0:ntb], func=AF.Ln, scale=1.0 / D, bias=EPS),
                  reads=[B("ss")], writes=[B("lnv")])
            fw.op("act", lambda e: e.activation(out=rstd[:, 0:ntb], in_=lnv[:, 0:ntb], func=AF.Exp, scale=-0.5),
                  reads=[B("lnv")], writes=[B("rstd")])
            for tb in range(ntb):
                fw.op("dve", lambda e, tb=tb: e.tensor_scalar(out=xn[:, tb, :], in0=src[:, tb, :], scalar1=rstd[:, tb:tb + 1], scalar2=None, op0=ALU.mult),
                      reads=[B(sbn[tb]), B("rstd")], writes=[B("xn")])
            for kc in range(8):
                bi = nb()
                pv = ps[bi][:].bitcast(BF16)
                for tb in range(ntb):
                    fw.op("pe", lambda e, tb=tb, kc=kc, pv=pv: e.transpose(out=pv[:, tb * 128:(tb + 1) * 128], in_=xn[:, tb, kc * 128:(kc + 1) * 128], identity=ident_bf[:]),
                          reads=[B("xn"), B("ident_bf")], writes=[PSB[bi]], signal=(tb == ntb - 1))
                if kc % 2 == 0:
                    fw.op("act", lambda e, kc=kc, pv=pv: e.activation(out=dst[:, kc, 0:ntb * 128], in_=pv[:, 0:ntb * 128], func=AF.Copy, scale=gcols[:, gi, kc:kc + 1]),
                          reads=[PSB[bi], B("gcols")], writes=[B(dstbuf)])
                else:
                    fw.op("dve", lambda e, kc=kc, pv=pv: e.tensor_scalar(out=dst[:, kc, 0:ntb * 128], in0=pv[:, 0:ntb * 128], scalar1=gcols[:, gi, kc:kc + 1], scalar2=None, op0=ALU.mult),
                          reads=[PSB[bi], B("gcols")], writes=[B(dstbuf)])

        def post_norm_residual(ph, gain_d):
            tmpa = sb(ph, "pn_a", [128, 512], F32)
            tmpb = sb(ph, "pn_b", [128, 512], F32)
            junk = sb(ph, "pn_junk", [128, 512], BF16)
            fw.dma("sp", lambda e: e.dma_start(out=gbuf[:], in_=gain_d[0:1, :].to_broadcast([128, D])), writes=[B("gbuf")], dsem=ds_g)
            for tb in range(4):
                for half in range(2):
                    bi = 2 * tb + half
                    fw.op("act", lambda e, bi=bi: e.activation(out=junk[:], in_=ps[bi][:], func=AF.Square, accum_out=ss[:, bi:bi + 1]),
                          reads=[PSB[bi]], writes=[B("pn_junk"), B("ss")])
            ssv = ss[:].rearrange("p (t h) -> p t h", h=2)
            tt("dve", ssum[:], ssv[:, :, 0], ssv[:, :, 1], ALU.add, [B("ss")], [B("ssum")])
            fw.op("act", lambda e: e.activation(out=lnv[:], in_=ssum[:], func=AF.Ln, scale=1.0 / D, bias=EPS), reads=[B("ssum")], writes=[B("lnv")])
            fw.op("act", lambda e: e.activation(out=rstd[:], in_=lnv[:], func=AF.Exp, scale=-0.5), reads=[B("lnv")], writes=[B("rstd")])
            k = 0
            for tb in range(4):
                for half in range(2):
                    bi = 2 * tb + half
                    tmp, tn = (tmpa, "pn_a") if k % 2 == 0 else (tmpb, "pn_b")
                    k += 1
                    tt("dve", tmp[:], ps[bi][:], gbuf[:, half * 512:(half + 1) * 512], ALU.mult, [PSB[bi], B("gbuf")], [B(tn)])
                    xs = xres[:, tb, half * 512:(half + 1) * 512]
                    fw.op("dve", lambda e, tmp=tmp, xs=xs, tb=tb: e.scalar_tensor_tensor(out=xs, in0=tmp[:], scalar=rstd[:, tb:tb + 1], in1=xs, op0=ALU.mult, op1=ALU.add),
                          reads=[B(tn), B("rstd"), B(XN[tb])], writes=[B(XN[tb])])

        def normA_tb(tb, xn):
            c1 = slice(tb, tb + 1)
            fw.op("act", lambda e: e.activation(out=xn[:, tb, :], in_=xres[:, tb, :], func=AF.Square, accum_out=ssN[:, c1]),
                  reads=[B(XN[tb])], writes=[B("xn%d" % tb), B("ssN%d" % tb)])
            fw.op("act", lambda e: e.activation(out=lnN[:, c1], in_=ssN[:, c1], func=AF.Ln, scale=1.0 / D, bias=EPS), reads=[B("ssN%d" % tb)], writes=[B("lnN%d" % tb)])
            fw.op("act", lambda e: e.activation(out=rstdN[:, c1], in_=lnN[:, c1], func=AF.Exp, scale=-0.5), reads=[B("lnN%d" % tb)], writes=[B("rstdN%d" % tb)])
            fw.op("act", lambda e: e.activation(out=xn[:, tb, :], in_=xres[:, tb, :], func=AF.Copy, scale=rstdN[:, c1]),
                  reads=[B(XN[tb]), B("rstdN%d" % tb)], writes=[B("xn%d" % tb)])

        def normB_tb(tb, gi, xn):
            bi = 2 * tb
            pv = ps[bi][:].bitcast(BF16)
            for kc in range(8):
                fw.op("pe", lambda e: e.transpose(out=pv[:, kc * 128:(kc + 1) * 128], in_=xn[:, tb, kc * 128:(kc + 1) * 128], identity=ident_bf[:]),
                      reads=[B("xn%d" % tb), B("ident_bf")], writes=[PSB[bi]], signal=(kc == 7))
            fw.op("dve", lambda e: e.tensor_tensor(out=hT[:, :, tb * 128:(tb + 1) * 128], in0=pv[:, 0:1024].rearrange("p (k t) -> p k t", k=8),
                                                   in1=gcols[:, gi, :].unsqueeze(2).to_broadcast([128, 8, 128]), op=ALU.mult),
                  reads=[PSB[bi], B("gcols")], writes=[B("hT")])

        def norm_tb(tb, gi, xn, junkN):
            normA_tb(tb, xn)
            normB_tb(tb, gi, xn)

        def norm_pipelined(gi, xn):
            for s in range(5):
                if s < 4:
                    normA_tb(s, xn)
                if s >= 1:
                    normB_tb(s - 1, gi, xn)

        def post_tb(tb, tmps, junkP):
            c1 = slice(tb, tb + 1)
            for half in range(2):
                bi = 2 * tb + half
                jt, jn = tmps[half]
                fw.op("act", lambda e: e.activation(out=jt[:].bitcast(BF16)[:, 0:512], in_=ps[bi][:], func=AF.Square, accum_out=ss[:, bi:bi + 1]),
                      reads=[PSB[bi]], writes=[B(jn), B("ssP%d" % tb)])
            tt("dve", ssum[:, c1], ss[:, 2 * tb:2 * tb + 1], ss[:, 2 * tb + 1:2 * tb + 2], ALU.add, [B("ssP%d" % tb)], [B("ssumP%d" % tb)])
            fw.op("act", lambda e: e.activation(out=lnv[:, c1], in_=ssum[:, c1], func=AF.Ln, scale=1.0 / D, bias=EPS), reads=[B("ssumP%d" % tb)], writes=[B("lnP%d" % tb)])
            fw.op("act", lambda e: e.activation(out=rstd[:, c1], in_=lnv[:, c1], func=AF.Exp, scale=-0.5), reads=[B("lnP%d" % tb)], writes=[B("rstdP%d" % tb)])
            for half in range(2):
                bi = 2 * tb + half
                tmp, tn = tmps[half]
                tt("dve", tmp[:], ps[bi][:], gbuf[:, half * 512:(half + 1) * 512], ALU.mult, [PSB[bi], B("gbuf")], [B(tn)])
                xs = xres[:, tb, half * 512:(half + 1) * 512]
                fw.op("dve", lambda e: e.scalar_tensor_tensor(out=xs, in0=tmp[:], scalar=rstd[:, c1], in1=xs, op0=ALU.mult, op1=ALU.add),
                      reads=[B(tn), B("rstdP%d" % tb), B(XN[tb])], writes=[B(XN[tb])])

        def boundary_bufs(scope):
            xn = sb(scope, "xn", [128, 4, D], BF16)
            junkN = None
            junkP = None
            tmps = [(sb(scope, "pn_a", [128, 512], F32), "pn_a"), (sb(scope, "pn_b", [128, 512], F32), "pn_b")]
            return xn, junkN, junkP, tmps

        def out_proj_fused(actT, actbuf, gain_d, gi_next, scope):
            xn, junkN, junkP, tmps = boundary_bufs(scope)
            fw.dma("sp", lambda e: e.dma_start(out=gbuf[:], in_=gain_d[0:1, :].to_broadcast([128, D])), writes=[B("gbuf")], dsem=ds_g)
            g0 = next_group()
            g1 = next_group()
            for s in range(6):
                if s < 4:
                    tb = s
                    for half, (si, views) in enumerate((g0, g1)):
                        wv = views[0]
                        for kc in range(8):
                            fw.op("pe", lambda e: e.matmul(ps[2 * tb + half][:], lhsT=actT[:, kc, tb * 128:(tb + 1) * 128], rhs=wv[:, kc, :], start=(kc == 0), stop=(kc == 7)),
                                  reads=[B(actbuf), RB[si]], writes=[PSB[2 * tb + half]], signal=(kc == 7))
                    post_tb(tb, tmps, junkP)
                if 1 <= s <= 4:
                    normA_tb(s - 1, xn)
                if 2 <= s <= 5:
                    normB_tb(s - 2, gi_next, xn)
            group_done()

        def out_proj_tokmajor(actT, actbuf, nkc_list):
            first = True
            for gi_, (k0, k1) in enumerate(nkc_list):
                for half in range(2):
                    si, views = next_group()
                    wv = views[0]
                    for tb in range(4):
                        for kc in range(k0, k1):
                            st_ = (gi_ == 0 and kc == k0)
                            sp_ = (gi_ == len(nkc_list) - 1 and kc == k1 - 1)
                            fw.op("pe", lambda e, tb=tb, kc=kc, half=half, wv=wv, k0=k0, st_=st_, sp_=sp_: e.matmul(
                                ps[2 * tb + half][:], lhsT=actT[:, kc, tb * 128:(tb + 1) * 128], rhs=wv[:, kc - k0, :], start=st_, stop=sp_),
                                reads=[B(actbuf), RB[si]], writes=[PSB[2 * tb + half]], signal=(kc == k1 - 1))
                    group_done()

        def mem_kv_phase():
            fw.barrier(with_sp=True)
            with arena.scope() as ph:
                memx = sb(ph, "memx", [128, 2, D], F32)
                memhT = sb(ph, "memhT", [128, 8, NMEM], BF16)
                fw.dma("sp", lambda e: e.dma_start(out=memx[:], in_=mem_d.rearrange("(t p) d -> p t d", p=128)), writes=[B("memx")], dsem=fw.new_dsem())
                with arena.scope() as ph2:
                    rms_to_hT(ph2, memx, "memx", 2, 3, memhT, "memhT")
                    fw.barrier()
                for half in range(2):
                    si, views = next_group()
                    wv = views[0]
                    for c in range(4):
                        bi = nb()
                        for kc in range(8):
                            fw.op("pe", lambda e, c=c, kc=kc, wv=wv, bi=bi: e.matmul(ps[bi][:, 0:NMEM], lhsT=wv[:, kc, c * 128:(c + 1) * 128], rhs=memhT[:, kc, :],
                                                                                    start=(kc == 0), stop=(kc == 7)),
                                  reads=[RB[si], B("memhT")], writes=[PSB[bi]], signal=(kc == 7))
                        evac_copy(memKT[:, half * 4 + c, :], ps[bi][:, 0:NMEM], [PSB[bi]], [B("memKT")])
                    group_done()
                for half in range(2):
                    si, views = next_group()
                    wv = views[0]
                    for mb in range(2):
                        bi = nb()
                        for kc in range(8):
                            fw.op("pe", lambda e, mb=mb, kc=kc, wv=wv, bi=bi: e.matmul(ps[bi][:], lhsT=memhT[:, kc, mb * 128:(mb + 1) * 128], rhs=wv[:, kc, :],
                                                                                     start=(kc == 0), stop=(kc == 7)),
                                  reads=[RB[si], B("memhT")], writes=[PSB[bi]], signal=(kc == 7))
                        evac_copy(memV[:, mb, half * 512:(half + 1) * 512], ps[bi][:], [PSB[bi]], [B("memV")])
                    group_done()
                fw.barrier(with_sp=True)

        for ti in range(ntiles):
            t0 = ti * TT
            if ti == 0:
                for tb in range(4):
                    fw.dma("sp", lambda e, tb=tb: e.dma_start(out=xres[:, tb, :], in_=x_d[128 * tb:128 * tb + 128, :]),
                           writes=[B(XN[tb])], dsem=ds_xt[tb])
            with arena.scope() as ph1:
                qT = sb(ph1, "qT", [128, 4, TT], BF16)
                U = sb(ph1, "U", [128, 16, 128], BF16)
                oaT = sb(ph1, "oaT", [128, 4, TT], BF16)
                osT = sb(ph1, "osT", [128, 4, TT], BF16)
                if ti == 0:
                    with arena.scope() as ph:
                        xn0, junkN0, _, _ = boundary_bufs(ph)
                        norm_pipelined(0, xn0)
                        fw.barrier()
                    dump("hT", hT[:], "hT", [128, 8, TT], BF16)
                for blk in range(2):
                    si, views = next_group()
                    wv = views[0]
                    for c in range(4):
                        bi = nb()
                        for kc in range(8):
                            fw.op("pe", lambda e, c=c, kc=kc, wv=wv, bi=bi: e.matmul(ps[bi][:], lhsT=wv[:, kc, c * 128:(c + 1) * 128], rhs=hT[:, kc, :],
                                                                                    start=(kc == 0), stop=(kc == 7)),
                                  reads=[RB[si], B("hT")], writes=[PSB[bi]], signal=(kc == 7))
                        if blk == 0:
                            evac_copy(qT[:, c, :], ps[bi][:], [PSB[bi]], [B("qT")], scale=0.125)
                        else:
                            evac_copy(kT[:, c, t0:t0 + TT], ps[bi][:], [PSB[bi]], [B("kT")])
                    group_done()
                si, views = next_group()
                wv = views[0]
                for tb in range(4):
                    bi = nb()
                    for kc in range(8):
                        fw.op("pe", lambda e, tb=tb, kc=kc, wv=wv, bi=bi: e.matmul(ps[bi][:], lhsT=hT[:, kc, tb * 128:(tb + 1) * 128], rhs=wv[:, kc, :],
                                                                                 start=(kc == 0), stop=(kc == 7)),
                              reads=[RB[si], B("hT")], writes=[PSB[bi]], signal=(kc == 7))
                    evac_copy(vtok[:, 4 * ti + tb, :], ps[bi][:], [PSB[bi]], [B("vtok")])
                group_done()
                si, views = next_group()
                wv = views[0]
                hT4 = hT[:].rearrange("p c (k j) -> p c j k", j=4)
                for qb in range(4):
                    bi = nb()
                    for qq in range(4):
                        q = 4 * qb + qq
                        for kc in range(8):
                            for j in range(4):
                                fw.op("pe", lambda e, q=q, qq=qq, j=j, kc=kc, wv=wv, bi=bi: e.matmul(
                                    ps[bi][32 * j:32 * j + 32, qq * 128:(qq + 1) * 128], lhsT=wv[:, kc, 32 * q:32 * q + 32], rhs=hT4[:, kc, j, :],
                                    start=(kc == 0), stop=(kc == 7), tile_position=(0, 32 * j)),
                                    reads=[RB[si], B("hT")], writes=[PSB[bi]], signal=(kc == 7 and j == 3 and qq == 3))
                    evac_copy(U[:, 4 * qb:4 * qb + 4, :], ps[bi][:].rearrange("p (a k) -> p a k", a=4), [PSB[bi]], [B("U")])
                group_done()
                fw.barrier()
                if ti == 0:
                    dump("qT", qT[:], "qT", [128, 4, TT], BF16)
                    dump("U", U[:], "U", [128, 16, 128], BF16)
                    dump("vtok0", vtok[:, 0:4, :], "vtok", [128, 4, 512], BF16)

                with arena.scope() as ph:
                    e_sb = sb(ph, "e_sb", [128, 2, TT], BF16)
                    sp_sb = [sb(ph, "sp_sb%d" % i, [128, 2, TT], BF16) for i in range(2)]
                    w_sb = [sb(ph, "w_sb%d" % i, [128, 2, TT], BF16) for i in range(2)]
                    Rb = [sb(ph, "R%d" % i, [128, 2, TT], BF16) for i in range(2)]
                    PZ3 = pp[0][:].rearrange("p (h c) -> p h c", h=2)
                    PW3 = pp[1][:].rearrange("p (h c) -> p h c", h=2)
                    Zb = (0, 1); Wb = (2, 3); AVB = (4, 5)
                    A1 = sb(ph, "A1", [128, 256], F32); A2 = sb(ph, "A2", [128, 256], F32)
                    VtR = sb(ph, "VtR", [128, 2, 128], F32); VtI = sb(ph, "VtI", [128, 2, 128], F32)
                    XtR = sb(ph, "XtR", [128, 2, 128], F32); XtI = sb(ph, "XtI", [128, 2, 128], F32)
                    XsR = sb(ph, "XsR", [128, 2, 132], BF16); XsI = sb(ph, "XsI", [128, 2, 132], BF16)
                    yg = sb(ph, "yg", [128, 2, 128], BF16)
                    ygT = sb(ph, "ygT", [128, 4, TT], BF16)
                    f2 = lambda t: t[:].rearrange("p a k -> p (a k)")
                    si_glu, views_glu = next_group()
                    ssm_copy_eng = "act" if ti <= 1 else "dve"
                    thunks = []
                    BV, BY = 6, 7

                    def mk_batch(b_):
                        q0 = 2 * b_; cc = b_ // 2; hb = b_ % 2
                        Cq = Ctab[:, q0:q0 + 2, :].rearrange("p a k -> p (a k)")
                        Sq = Stab[:, q0:q0 + 2, :].rearrange("p a k -> p (a k)")
                        vre = ps[BV][:, 0:256]; vim = ps[BV][:, 256:512]

                        def t1():
                            for qq in range(2):
                                fw.op("pe", lambda e: e.matmul(ps[BV][:, qq * 128:(qq + 1) * 128], lhsT=BcRe[:, q0 + qq, :], rhs=U[:, q0 + qq, :], start=True, stop=True),
                                      reads=[B("BcRe"), B("U")], writes=[PSB[BV]], signal=False)
                            for qq in range(2):
                                fw.op("pe", lambda e: e.matmul(ps[BV][:, 256 + qq * 128:256 + (qq + 1) * 128], lhsT=BcIm[:, q0 + qq, :], rhs=U[:, q0 + qq, :], start=True, stop=True),
                                      reads=[B("BcIm"), B("U")], writes=[PSB[BV]], signal=(qq == 1))

                        def t2():
                            tt("dve", A1[:], vre, Cq, ALU.mult, [PSB[BV], B("Ctab")], [B("A1")])
                            tt("dve", A2[:], vim, Sq, ALU.mult, [PSB[BV], B("Stab")], [B("A2")])
                            tt("dve", f2(VtR), A1[:], A2[:], ALU.add, [B("A1"), B("A2")], [B("VtR")])
                            tt("dve", A1[:], vim, Cq, ALU.mult, [PSB[BV], B("Ctab")], [B("A1")])
                            tt("dve", A2[:], vre, Sq, ALU.mult, [PSB[BV], B("Stab")], [B("A2")])
                            tt("dve", f2(VtI), A1[:], A2[:], ALU.subtract, [B("A1"), B("A2")], [B("VtI")])

                        def t3():
                            fw.op("dve", lambda e: e.tensor_copy(out=XsR[:, :, 0:1], in_=XcRe[:, q0:q0 + 2].unsqueeze(2)), reads=[B("XcRe")], writes=[B("XsR")])
                            fw.op("dve", lambda e: e.tensor_copy(out=XsI[:, :, 0:1], in_=XcIm[:, q0:q0 + 2].unsqueeze(2)), reads=[B("XcIm")], writes=[B("XsI")])
                            for qq in range(2):
                                q = q0 + qq
                                fw.op("dve", lambda e: e.tensor_tensor_scan(out=XtR[:, qq, :], data0=rho4[:, q:q + 1].to_broadcast([128, 128]), data1=VtR[:, qq, :],
                                                                            initial=XcRe[:, q:q + 1], op0=ALU.mult, op1=ALU.add),
                                      reads=[B("rho4"), B("VtR"), B("XcRe")], writes=[B("XtR")])
                                fw.op("dve", lambda e: e.tensor_tensor_scan(out=XtI[:, qq, :], data0=rho4[:, q:q + 1].to_broadcast([128, 128]), data1=VtI[:, qq, :],
                                                                            initial=XcIm[:, q:q + 1], op0=ALU.mult, op1=ALU.add),
                                      reads=[B("rho4"), B("VtI"), B("XcIm")], writes=[B("XtI")])

                        def t4():
                            tt("dve", A1[:], f2(XtR), Cq, ALU.mult, [B("XtR"), B("Ctab")], [B("A1")])
                            tt("dve", A2[:], f2(XtI), Sq, ALU.mult, [B("XtI"), B("Stab")], [B("A2")])
                            tt("dve", f2(VtR), A1[:], A2[:], ALU.subtract, [B("A1"), B("A2")], [B("VtR")])
                            tt("dve", A1[:], f2(XtI), Cq, ALU.mult, [B("XtI"), B("Ctab")], [B("A1")])
                            tt("dve", A2[:], f2(XtR), Sq, ALU.mult, [B("XtR"), B("Stab")], [B("A2")])
                            tt("dve", f2(VtI), A1[:], A2[:], ALU.add, [B("A1"), B("A2")], [B("VtI")])
                            evac_copy(XsR[:, :, 1:129], VtR[:], [B("VtR")], [B("XsR")], eng=ssm_copy_eng)
                            evac_copy(XsI[:, :, 1:129], VtI[:], [B("VtI")], [B("XsI")], eng=ssm_copy_eng)
                            fw.op("dve", lambda e: e.tensor_copy(out=XcRe[:, q0:q0 + 2].unsqueeze(2), in_=VtR[:, :, 127:128]), reads=[B("VtR")], writes=[B("XcRe")])
                            fw.op("dve", lambda e: e.tensor_copy(out=XcIm[:, q0:q0 + 2].unsqueeze(2), in_=VtI[:, :, 127:128]), reads=[B("VtI")], writes=[B("XcIm")])

                        def t5():
                            for qq in range(2):
                                q = q0 + qq
                                o = ps[BY][:, qq * 128:(qq + 1) * 128]
                                fw.op("pe", lambda e: e.matmul(o, lhsT=Gm[:, q, :], rhs=U[:, q, :], start=True, stop=False),
                                      reads=[B("Gm"), B("U")], writes=[PSB[BY]], signal=False)
                                fw.op("pe", lambda e: e.matmul(o, lhsT=CmRe[:, q, :], rhs=XsR[:, qq, 0:128], start=False, stop=False),
                                      reads=[B("CmRe"), B("XsR")], writes=[PSB[BY]], signal=False)
                                fw.op("pe", lambda e: e.matmul(o, lhsT=CmIm[:, q, :], rhs=XsI[:, qq, 0:128], start=False, stop=True),
                                      reads=[B("CmIm"), B("XsI")], writes=[PSB[BY]], signal=(qq == 1))
                            for qq in range(2):
                                q = q0 + qq
                                fw.op("dve", lambda e: e.scalar_tensor_tensor(out=A1[:, qq * 128:(qq + 1) * 128], in0=U[:, q, :], scalar=dvec[:, q:q + 1],
                                                                              in1=ps[BY][:, qq * 128:(qq + 1) * 128], op0=ALU.mult, op1=ALU.add),
                                      reads=[B("U"), B("dvec"), PSB[BY]], writes=[B("A1")])

                        def t6_native():
                            fw.op("act", lambda e: e.activation(out=f2(yg), in_=A1[:], func=AF.Gelu_apprx_tanh), reads=[B("A1")], writes=[B("yg")])

                        def t6():
                            tt("dve", A2[:], A1[:], A1[:], ALU.mult, [B("A1")], [B("A2")])
                            fw.op("dve", lambda e: e.tensor_scalar(out=A2[:], in0=A2[:], scalar1=0.044715, scalar2=1.0, op0=ALU.mult, op1=ALU.add), reads=[B("A2")], writes=[B("A2")])
                            tt("dve", A2[:], A2[:], A1[:], ALU.mult, [B("A2"), B("A1")], [B("A2")])
                            fw.op("dve", lambda e: e.tensor_scalar(out=A2[:], in0=A2[:], scalar1=-40.0, scalar2=None, op0=ALU.max), reads=[B("A2")], writes=[B("A2")])
                            fw.op("act", lambda e: e.activation(out=A2[:], in_=A2[:], func=AF.Exp, scale=-1.5957691216), reads=[B("A2")], writes=[B("A2")])
                            fw.op("dve", lambda e: e.tensor_scalar(out=A2[:], in0=A2[:], scalar1=1.0, scalar2=None, op0=ALU.add), reads=[B("A2")], writes=[B("A2")])
                            fw.op("dve", lambda e: e.reciprocal(out=A2[:], in_=A2[:]), reads=[B("A2")], writes=[B("A2")])
                            tt("dve", f2(yg), A1[:], A2[:], ALU.mult, [B("A1"), B("A2")], [B("yg")])

                        def t7():
                            for i in range(4):
                                for qq in range(2):
                                    pb = 32 * (2 * hb + qq)
                                    fw.op("pe", lambda e: e.matmul(ps[BY][pb:pb + 32, i * 128:(i + 1) * 128], lhsT=ident_bf[:, 32 * i:32 * i + 32], rhs=yg[:, qq, :],
                                                                   start=True, stop=True, tile_position=(0, pb)),
                                          reads=[B("ident_bf"), B("yg")], writes=[PSB[BY]], signal=(qq == 1 and i == 3))
                            evac_copy(ygT[64 * hb:64 * hb + 64, cc, :].rearrange("p (k i) -> p i k", i=4),
                                      ps[BY][64 * hb:64 * hb + 64, :].rearrange("p (i k) -> p i k", i=4), [PSB[BY]], [B("ygT")], eng=ssm_copy_eng)
                        if ti <= 1:
                            return [t1, t2, t3, t4, t5, t6_native, t7]
                        return [t1, t2, t3, t4, t5, t6, t7]

                    for b_ in range(8):
                        thunks.extend(mk_batch(b_))

                    def mk_glu(co, hf):
                        wg = views_glu[0]
                        bk = BV if (2 * co + hf) % 2 == 0 else BY
                        cs = slice(256 * hf, 256 * hf + 256)

                        def tg():
                            for ci in range(4):
                                fw.op("pe", lambda e: e.matmul(ps[bk][:, 0:256], lhsT=wg[:, ci, co * 128:(co + 1) * 128], rhs=ygT[:, ci, cs], start=(ci == 0), stop=(ci == 3)),
                                      reads=[RB[si_glu], B("ygT")], writes=[PSB[bk]], signal=(ci == 3))
                            if ti <= 1:
                                fw.op("act", lambda e: e.activation(out=A2[:], in_=ps[bk][:, 0:256], func=AF.Sigmoid, bias=bglu[:, co:co + 1]),
                                      reads=[PSB[bk], B("bglu")], writes=[B("A2")])
                            else:
                                fw.op("act", lambda e: e.activation(out=A2[:], in_=ps[bk][:, 0:256], func=AF.Exp, scale=-1.0, bias=nbglu[:, co:co + 1]),
                                      reads=[PSB[bk], B("nbglu")], writes=[B("A2")])
                                fw.op("dve", lambda e: e.tensor_scalar(out=A2[:], in0=A2[:], scalar1=1.0, scalar2=None, op0=ALU.add), reads=[B("A2")], writes=[B("A2")])
                                fw.op("dve", lambda e: e.reciprocal(out=A2[:], in_=A2[:]), reads=[B("A2")], writes=[B("A2")])
                            tt("dve", osT[:, co, cs], ygT[:, co, cs], A2[:], ALU.mult, [B("ygT"), B("A2")], [B("osT")])
                        return tg

                    for co in range(4):
                        for hf in range(2):
                            thunks.append(mk_glu(co, hf))
                    thunks.append(group_done)

                    nkb = 4 * ti + 4
                    steps = []
                    for p in range(4):
                        for kbi, kb in enumerate(range(nkb - 1, -1, -1)):
                            steps.append((p, kb, kbi))
                    N = len(steps)

                    def c0_of(kb):
                        j = kb - 4 * ti
                        return 128 * j if j > 0 else 0

                    def qk_mm(bank, p, kb, hh, more):
                        r0 = 64 * hh
                        diag = kb >= 4 * ti
                        c0 = c0_of(kb)
                        fw.op("pe", lambda e: e.matmul(ps[bank][:, c0:TT], lhsT=kT[r0:r0 + 64, p, kb * 128:(kb + 1) * 128], rhs=qT[r0:r0 + 64, p, c0:TT],
                                                       start=True, stop=(not diag and not more)),
                              reads=[B("kT"), B("qT")], writes=[PSB[bank]], signal=False)
                        if diag:
                            fw.op("pe", lambda e: e.matmul(ps[bank][:, c0:TT], lhsT=ident_bf[:], rhs=maskb[:, 0:TT - c0], start=False, stop=(not more)),
                                  reads=[B("ident_bf"), B("maskb")], writes=[PSB[bank]], signal=False)

                    def qk_pair(banks, p, kb, more):
                        diag = kb >= 4 * ti
                        c0 = c0_of(kb)
                        for hh in range(2):
                            r0 = 64 * hh
                            fw.op("pe", lambda e: e.matmul(ps[banks[hh]][:, c0:TT], lhsT=kT[r0:r0 + 64, p, kb * 128:(kb + 1) * 128], rhs=qT[r0:r0 + 64, p, c0:TT],
                                                           start=True, stop=(not diag and not more)),
                                  reads=[B("kT"), B("qT")], writes=[PSB[banks[hh]]], signal=False)
                        if diag:
                            for hh in range(2):
                                fw.op("pe", lambda e: e.matmul(ps[banks[hh]][:, c0:TT], lhsT=ident_bf[:], rhs=maskb[:, 0:TT - c0], start=False, stop=(not more)),
                                      reads=[B("ident_bf"), B("maskb")], writes=[PSB[banks[hh]]], signal=False)

                    def S0(n):
                        p, kb, kbi = steps[n]
                        qk_pair(Zb, p, kb, False)
                        sig("pe")

                    def S1(n):
                        p, kb, kbi = steps[n]
                        c0 = c0_of(kb)
                        i2 = n % 2
                        fw.op("act", lambda e: e.activation(out=e_sb[:, :, c0:TT], in_=PZ3[:, :, c0:TT], func=AF.Exp), reads=[PSB[Zb[0]], PSB[Zb[1]]], writes=[B("e_sb")])
                        fw.op("act", lambda e: e.activation(out=sp_sb[i2][:, :, c0:TT], in_=e_sb[:, :, c0:TT], func=AF.Ln, bias=1.0), reads=[B("e_sb")], writes=[B("sp_sb%d" % i2)])

                    def S2(n):
                        p, kb, kbi = steps[n]
                        c0 = c0_of(kb)
                        i2 = n % 2
                        first = (kbi == 0)
                        last = (kb == 0)
                        rc, rn = Rb[kbi % 2], Rb[(kbi + 1) % 2]
                        rcn, rnn = "R%d" % (kbi % 2), "R%d" % ((kbi + 1) % 2)
                        if first:
                            fw.op("pool", lambda e: e.memset(Rb[0][:], 0.0), writes=[B("R0")])
                            fw.op("pool", lambda e: e.memset(Rb[1][:], 0.0), writes=[B("R1")])
                        qk_pair(Wb, p, kb, True)
                        for hh in range(2):
                            fw.op("pe", lambda e: e.matmul(ps[Wb[hh]][:, c0:TT], lhsT=ntri[:], rhs=sp_sb[i2][:, hh, c0:TT], start=False, stop=first),
                                  reads=[B("ntri"), B("sp_sb%d" % i2)], writes=[PSB[Wb[hh]]], signal=False)
                            if not first:
                                fw.op("pe", lambda e: e.matmul(ps[Wb[hh]][:, c0:TT], lhsT=nones[:], rhs=rc[:, hh, c0:TT], start=False, stop=True),
                                      reads=[B("nones"), B(rcn)], writes=[PSB[Wb[hh]]], signal=False)
                        sig("pe")
                        if not last:
                            if first:
                                fw.op("dve", lambda e: e.tensor_copy(out=rn[:, :, c0:TT], in_=sp_sb[i2][:, :, c0:TT]), reads=[B("sp_sb%d" % i2)], writes=[B(rnn)])
                            else:
                                tt("dve", rn[:, :, c0:TT], rc[:, :, c0:TT], sp_sb[i2][:, :, c0:TT], ALU.add, [B(rcn), B("sp_sb%d" % i2)], [B(rnn)])

                    def S3(n):
                        p, kb, kbi = steps[n]
                        c0 = c0_of(kb)
                        i2 = n % 2
                        fw.op("act", lambda e: e.activation(out=w_sb[i2][:, :, c0:TT], in_=PW3[:, :, c0:TT], func=AF.Exp), reads=[PSB[Wb[0]], PSB[Wb[1]]], writes=[B("w_sb%d" % i2)])

                    def S4(n):
                        p, kb, kbi = steps[n]
                        c0 = c0_of(kb)
                        i2 = n % 2
                        ab = AVB[p % 2]
                        for hh in range(2):
                            h = 2 * p + hh
                            fw.op("pe", lambda e: e.matmul(ps[ab][64 * hh:64 * hh + 64, c0:TT], lhsT=vtok[:, kb, h * 64:(h + 1) * 64], rhs=w_sb[i2][:, hh, c0:TT],
                                                           start=(kbi == 0), stop=(kb == 0), skip_group_check=True),
                                  reads=[B("vtok"), B("w_sb%d" % i2)], writes=[PSB[ab]], signal=False)
                        sig("pe")
                        if kb == 0:
                            evac_copy(oaT[:, p, :], ps[ab][:], [PSB[ab]], [B("oaT")], eng="dve")

                    rate = len(thunks) / max(1.0, 1.0 * N)
                    acc = 0.0
                    tq = list(thunks)
                    for n in range(N + 2):
                        if n < N:
                            S0(n)
                            S1(n)
                        if 1 <= n <= N:
                            S2(n - 1)
                            S3(n - 1)
                        if n >= 2:
                            S4(n - 2)
                        acc += rate
                        while acc >= 1.0 and tq:
                            tq.pop(0)()
                            acc -= 1.0
                    while tq:
                        tq.pop(0)()
                    fw.barrier()
                if ti == 0:
                    dump("oaT", oaT[:], "oaT", [128, 4, TT], BF16)
                    dump("osT", osT[:], "osT", [128, 4, TT], BF16)

                with arena.scope() as ph:
                    mergedT = sb(ph, "mergedT", [128, 8, TT], BF16)
                    phm = arena.scope().__enter__()
                    sga = sb(phm, "sga", [128, TT], F32); sgs = sb(phm, "sgs", [128, TT], F32)
                    m1 = sb(phm, "m1", [128, TT], F32); m2 = sb(phm, "m2", [128, TT], F32)
                    for pr in range(4):
                        si_a, vg = next_group()
                        si_b, vb = next_group()
                        si_s = si_a
                        va = [vg[0]]; vs = [vg[1]]
                        for mm_ in range(2):
                            m = 2 * pr + mm_
                            b_ga, b_gs, b_ba, b_bs = nb(), nb(), nb(), nb()
                            for kc in range(8):
                                fw.op("pe", lambda e, kc=kc, mm_=mm_, b=b_ga, w=va[0]: e.matmul(ps[b][:], lhsT=w[:, kc, mm_ * 128:(mm_ + 1) * 128], rhs=hT[:, kc, :], start=(kc == 0), stop=(kc == 7)),
                                      reads=[RB[si_a], B("hT")], writes=[PSB[b_ga]], signal=(kc == 7))
                            for kc in range(8):
                                fw.op("pe", lambda e, kc=kc, mm_=mm_, b=b_gs, w=vs[0]: e.matmul(ps[b][:], lhsT=w[:, kc, mm_ * 128:(mm_ + 1) * 128], rhs=hT[:, kc, :], start=(kc == 0), stop=(kc == 7)),
                                      reads=[RB[si_s], B("hT")], writes=[PSB[b_gs]], signal=(kc == 7))
                            for ci in range(4):
                                fw.op("pe", lambda e, ci=ci, mm_=mm_, b=b_ba, w=vb[0]: e.matmul(ps[b][:], lhsT=w[:, ci, mm_ * 128:(mm_ + 1) * 128], rhs=oaT[:, ci, :], start=(ci == 0), stop=(ci == 3)),
                                      reads=[RB[si_b], B("oaT")], writes=[PSB[b_ba]], signal=(ci == 3))
                            for ci in range(4):
                                fw.op("pe", lambda e, ci=ci, mm_=mm_, b=b_bs, w=vb[1]: e.matmul(ps[b][:], lhsT=w[:, ci, mm_ * 128:(mm_ + 1) * 128], rhs=osT[:, ci, :], start=(ci == 0), stop=(ci == 3)),
                                      reads=[RB[si_b], B("osT")], writes=[PSB[b_bs]], signal=(ci == 3))
                            fw.op("act", lambda e, m=m, b=b_ga: e.activation(out=sga[:], in_=ps[b][:], func=AF.Sigmoid, bias=bgate[:, m:m + 1]),
                                  reads=[PSB[b_ga], B("bgate")], writes=[B("sga")])
                            fw.op("act", lambda e, m=m, b=b_gs: e.activation(out=sgs[:], in_=ps[b][:], func=AF.Sigmoid, bias=bgate[:, 8 + m:9 + m]),
                                  reads=[PSB[b_gs], B("bgate")], writes=[B("sgs")])
                            tt("dve", m1[:], sga[:], ps[b_ba][:], ALU.mult, [B("sga"), PSB[b_ba]], [B("m1")])
                            tt("dve", m2[:], sgs[:], ps[b_bs][:], ALU.mult, [B("sgs"), PSB[b_bs]], [B("m2")])
                            tt("dve", mergedT[:, m, :], m1[:], m2[:], ALU.add, [B("m1"), B("m2")], [B("mergedT")])
                        group_done()
                    if ti == 0:
                        dump("mergedT", mergedT[:], "mergedT", [128, 8, TT], BF16)
                    fw.barrier()
                    phm.__exit__(None, None, None)
                    out_proj_fused(mergedT, "mergedT", g_mix_post, 1, ph)
                    fw.barrier()
            if ti == 0:
                dump("x1", xres[:], XN, [128, 4, D], F32)

            if ti == 0:
                mem_kv_phase()
            with arena.scope() as ph2:
                qxT = sb(ph2, "qxT", [128, 8, TT], BF16)
                oxT = sb(ph2, "oxT", [128, 8, TT], BF16)
                Pm = [sb(ph2, "Pm%d" % i, [128, 4, NMEM], BF16) for i in range(2)]
                PT = [sb(ph2, "PT%d" % i, [128, 2, TT], BF16) for i in range(4)]
                mx = sb(ph2, "mx", [128, 16], F32); nmx = sb(ph2, "nmx", [128, 16], F32)
                sm = sb(ph2, "sm", [128, 16], F32); rs = sb(ph2, "rs", [128, 16], F32)
                SC = 1.0 / 16.0
                for half in range(2):
                    si, views = next_group()
                    wv = views[0]
                    for c in range(4):
                        bi = nb()
                        for kc in range(8):
                            fw.op("pe", lambda e, c=c, kc=kc, wv=wv, bi=bi: e.matmul(ps[bi][:], lhsT=wv[:, kc, c * 128:(c + 1) * 128], rhs=hT[:, kc, :], start=(kc == 0), stop=(kc == 7)),
                                  reads=[RB[si], B("hT")], writes=[PSB[bi]], signal=(kc == 7))
                        evac_copy(qxT[:, half * 4 + c, :], ps[bi][:], [PSB[bi]], [B("qxT")])
                    group_done()
                def xa_scores(tb):
                    b0, b1 = nb(), nb()
                    for h in range(4):
                        bi = b0 if h < 2 else b1
                        o = ps[bi][:, (h % 2) * NMEM:(h % 2 + 1) * NMEM]
                        for c2 in range(2):
                            ch = 2 * h + c2
                            fw.op("pe", lambda e: e.matmul(o, lhsT=qxT[:, ch, tb * 128:(tb + 1) * 128], rhs=memKT[:, ch, :], start=(c2 == 0), stop=(c2 == 1)),
                                  reads=[B("qxT"), B("memKT")], writes=[PSB[bi]], signal=(c2 == 1 and h % 2 == 1))
                    return b0, b1

                def xa_softmax(tb, b0, b1):
                    pm = Pm[tb % 2]; pmn = "Pm%d" % (tb % 2)
                    s4 = slice(4 * tb, 4 * tb + 4)
                    stn = "xst%d" % tb
                    for hp, bi in enumerate((b0, b1)):
                        fw.op("dve", lambda e: e.tensor_reduce(out=mx[:, 4 * tb + 2 * hp:4 * tb + 2 * hp + 2], in_=ps[bi][:].rearrange("p (h m) -> p h m", h=2), axis=mybir.AxisListType.X, op=ALU.max),
                              reads=[PSB[bi]], writes=[B(stn + "mx")])
                    fw.op("dve", lambda e: e.tensor_scalar(out=nmx[:, s4], in0=mx[:, s4], scalar1=-SC, scalar2=None, op0=ALU.mult), reads=[B(stn + "mx")], writes=[B(stn + "nmx")])
                    for h in range(4):
                        bi = b0 if h < 2 else b1
                        o = ps[bi][:, (h % 2) * NMEM:(h % 2 + 1) * NMEM]
                        fw.op("act", lambda e: e.activation(out=pm[:, h, :], in_=o, func=AF.Exp, scale=SC, bias=nmx[:, 4 * tb + h:4 * tb + h + 1], accum_out=sm[:, 4 * tb + h:4 * tb + h + 1]),
                              reads=[PSB[bi], B(stn + "nmx")], writes=[B(pmn), B(stn + "sm")])
                    fw.op("dve", lambda e: e.reciprocal(out=rs[:, s4], in_=sm[:, s4]), reads=[B(stn + "sm")], writes=[B(stn + "rs")])
                    fw.op("dve", lambda e: e.tensor_tensor(out=pm[:], in0=pm[:], in1=rs[:, s4].unsqueeze(2).to_broadcast([128, 4, NMEM]), op=ALU.mult),
                          reads=[B(pmn), B(stn + "rs")], writes=[B(pmn)])

                def xa_transpose(tb):
                    pm = Pm[tb % 2]; pmn = "Pm%d" % (tb % 2)
                    for hp in range(2):
                        bi = nb()
                        pv = ps[bi][:].bitcast(BF16)
                        for hh in range(2):
                            h = 2 * hp + hh
                            for mb in range(2):
                                fw.op("pe", lambda e: e.transpose(out=pv[:, hh * 256 + mb * 128:hh * 256 + (mb + 1) * 128], in_=pm[:, h, mb * 128:(mb + 1) * 128], identity=ident_bf[:]),
                                      reads=[B(pmn), B("ident_bf")], writes=[PSB[bi]], signal=(mb == 1 and hh == 1))
                        for hh in range(2):
                            h = 2 * hp + hh
                            evac_copy(PT[h][:, :, tb * 128:(tb + 1) * 128], pv[:, hh * 256:(hh + 1) * 256].rearrange("p (m t) -> p m t", m=2), [PSB[bi]], [B("PT%d" % h)],
                                      eng=("act" if hh == 0 else "dve"))

                sc = {}
                sc[0] = xa_scores(0)
                sc[1] = xa_scores(1)
                xa_softmax(0, *sc[0])
                sc[2] = xa_scores(2)
                xa_softmax(1, *sc[1])
                xa_transpose(0)
                sc[3] = xa_scores(3)
                xa_softmax(2, *sc[2])
                xa_transpose(1)
                xa_softmax(3, *sc[3])
                xa_transpose(2)
                xa_transpose(3)
                for ch in range(8):
                    h = ch // 2
                    bi = nb()
                    for mb in range(2):
                        fw.op("pe", lambda e, ch=ch, mb=mb, h=h, bi=bi: e.matmul(ps[bi][:], lhsT=memV[:, mb, ch * 128:(ch + 1) * 128], rhs=PT[h][:, mb, :], start=(mb == 0), stop=(mb == 1)),
                              reads=[B("memV"), B("PT%d" % h)], writes=[PSB[bi]], signal=(mb == 1))
                    evac_copy(oxT[:, ch, :], ps[bi][:], [PSB[bi]], [B("oxT")])
                fw.barrier()
                out_proj_fused(oxT, "oxT", g_xa_post, 2, ph2)
                fw.barrier()
            if ti == 0:
                dump("x2", xres[:], XN, [128, 4, D], F32)

            with arena.scope() as ph3:
                actT = sb(ph3, "actT", [128, 22, TT], BF16)
                with arena.scope() as phu:
                    upx = [sb(phu, "upx%d" % i, [128, TT + 2], F32) for i in range(2)]
                    cgs = [sb(phu, "cg%d" % i, [128, TT], F32) for i in range(2)]
                    cvs = [sb(phu, "cv%d" % i, [128, TT], F32) for i in range(2)]
                    gls = [sb(phu, "gl%d" % i, [128, TT], F32) for i in range(2)]
                    for j in range(11):
                        si, views = next_group()
                        for f2 in range(2):
                            f = 2 * j + f2
                            par = f % 2
                            for isval in range(2):
                                wv = views[isval]
                                chunk = f + 22 * isval
                                bi = nb()
                                for kc in range(8):
                                    fw.op("pe", lambda e, kc=kc, f2=f2, wv=wv, bi=bi: e.matmul(ps[bi][:], lhsT=wv[:, kc, f2 * 128:(f2 + 1) * 128], rhs=hT[:, kc, :], start=(kc == 0), stop=(kc == 7)),
                                          reads=[RB[si], B("hT")], writes=[PSB[bi]], signal=(kc == 7))
                                ux = upx[isval]; uxn = "upx%d" % isval
                                co, con = (cgs[par], "cg%d" % par) if isval == 0 else (cvs[par], "cv%d" % par)
                                fw.op("pool", lambda e, ux=ux, chunk=chunk: e.tensor_copy(out=ux[:, 0:2], in_=halo[:, chunk, :]), reads=[B("halo")], writes=[B(uxn)])
                                fw.op("act", lambda e, ux=ux, bi=bi: e.activation(out=ux[:, 2:TT + 2], in_=ps[bi][:], func=AF.Copy), reads=[PSB[bi]], writes=[B(uxn)])
                                fw.op("act", lambda e, co=co, bi=bi, chunk=chunk: e.activation(out=co[:], in_=ps[bi][:], func=AF.Identity, scale=convw[:, 2, chunk:chunk + 1], bias=convb[:, chunk:chunk + 1]),
                                      reads=[PSB[bi], B("convw"), B("convb")], writes=[B(con)])
                                fw.op("pool", lambda e, ux=ux, chunk=chunk: e.tensor_copy(out=halo[:, chunk, :], in_=ux[:, TT:TT + 2]), reads=[B(uxn)], writes=[B("halo")])
                                fw.op("dve", lambda e, ux=ux, co=co, chunk=chunk: e.scalar_tensor_tensor(out=co[:], in0=ux[:, 1:TT + 1], scalar=convw[:, 1, chunk:chunk + 1], in1=co[:], op0=ALU.mult, op1=ALU.add),
                                      reads=[B(uxn), B("convw"), B(con)], writes=[B(con)])
                                fw.op("dve", lambda e, ux=ux, co=co, chunk=chunk: e.scalar_tensor_tensor(out=co[:], in0=ux[:, 0:TT], scalar=convw[:, 0, chunk:chunk + 1], in1=co[:], op0=ALU.mult, op1=ALU.add),
                                      reads=[B(uxn), B("convw"), B(con)], writes=[B(con)])
                            fw.op("act", lambda e, par=par: e.activation(out=gls[par][:], in_=cgs[par][:], func=AF.Gelu_apprx_tanh), reads=[B("cg%d" % par)], writes=[B("gl%d" % par)])
                            tt("dve", actT[:, f, :], gls[par][:], cvs[par][:], ALU.mult, [B("gl%d" % par), B("cv%d" % par)], [B("actT")])
                        group_done()
                fw.barrier()
                xn3, junkN3, junkP3, tmps3 = boundary_bufs(ph3)
                fw.dma("sp", lambda e: e.dma_start(out=gbuf[:], in_=g_ffn_post[0:1, :].to_broadcast([128, D])), writes=[B("gbuf")], dsem=ds_g)
                out_proj_tokmajor(actT, "actT", [(0, 8), (8, 16), (16, 22)])
                nxt = ti + 1 < ntiles
                for s in range(6):
                    if s < 4:
                        tb = s
                        post_tb(tb, tmps3, junkP3)
                        ev = fw.dma("sp", lambda e, t0=t0, tb=tb: e.dma_start(out=out_d[t0 + 128 * tb:t0 + 128 * tb + 128, :], in_=xres[:, tb, :]),
                                    reads=[B(XN[tb])], writes=[B("out%d" % tb)], dsem=ds_ot[tb])
                        out_events.append(ev)
                        if nxt:
                            t1 = t0 + TT
                            fw.dma("sp", lambda e, t1=t1, tb=tb: e.dma_start(out=xres[:, tb, :], in_=x_d[t1 + 128 * tb:t1 + 128 * tb + 128, :]),
                                   writes=[B(XN[tb])], dsem=ds_xt[tb])
                    if nxt and 1 <= s <= 4:
                        normA_tb(s - 1, xn3)
                    if nxt and 2 <= s <= 5:
                        normB_tb(s - 2, 0, xn3)
                fw.barrier()

        for ev in out_events:
            fw._wait("sp", ev)

        with nc.allow_non_contiguous_dma(reason="small param layouts"):
            with nc.Block() as block:
                @block.tensor
                def _(eng):
                    fw.replay("pe", eng)

                @block.scalar
                def _(eng):
                    fw.replay("act", eng)

                @block.vector
                def _(eng):
                    fw.replay("dve", eng)

                @block.gpsimd
                def _(eng):
                    fw.replay("pool", eng)

                @block.sync
                def _(eng):
                    fw.replay("sp", eng)
        build.arena_peak = arena.peak
    build.dbg_specs = dbg_specs
    build.nops = {e: len(v) for e, v in fw.ops.items()}
    return nc


def ssm_setup(nc, fw, B, sb, ps, PSB, arena, L):
    a_re_d, a_im_d, ldt_d = L["a_re_d"], L["a_im_d"], L["ldt_d"]
    b_re_d, b_im_d, c_re_d, c_im_d = L["b_re_d"], L["b_im_d"], L["c_re_d"], L["c_im_d"]
    Gm, BcRe, BcIm, CmRe, CmIm, Ctab, Stab, rho4 = L["Gm"], L["BcRe"], L["BcIm"], L["CmRe"], L["CmIm"], L["Ctab"], L["Stab"], L["rho4"]
    ident_f, maskG, kk = L["ident_f"], L["maskG"], L["kk"]
    arK = Arena(L["kT"][:].rearrange("p a b -> p (a b)"), 32768)
    arV = Arena(L["vtok"][:].rearrange("p a b -> p (a b)"), 32768)
    ds = fw.new_dsem()
    cnt = {"i": 0}
    CE = ("pe", "act", "dve")
    NP = 16

    def nbk():
        cnt["i"] = (cnt["i"] + 1) % 8
        return cnt["i"]

    def tt(out_ap, a, b, op, reads, writes, eng="dve"):
        fw.op(eng, lambda e: e.tensor_tensor(out=out_ap, in0=a, in1=b, op=op), reads=reads, writes=writes)

    def mset(ap, val, bname):
        fw.op("dve", lambda e: e.memset(ap, val), writes=[B(bname)])

    with arena.scope() as ph:
        are = sb(ph, "s_are", [128, 16], F32); aim = sb(ph, "s_aim", [128, 16], F32); dt = sb(ph, "s_dt", [128, 16], F32)
        th = sb(ph, "s_th", [128, 16], F32); mag = sb(ph, "s_mag", [128, 16], F32)
        t16 = [sb(ph, "s_t%d" % i, [128, 16], F32) for i in range(4)]
        ti16 = sb(ph, "s_ti", [128, 16], I32)
        lamR = sb(ph, "s_lamR", [128, 5, 16], F32); lamI = sb(ph, "s_lamI", [128, 5, 16], F32)
        muR = sb(ph, "s_muR", [128, 4, 16], F32); muI = sb(ph, "s_muI", [128, 4, 16], F32)
        fR = sb(ph, "s_fR", [128, 16], F32); fI = sb(ph, "s_fI", [128, 16], F32)
        Sa = sb(ph, "s_Sa", [16, 256], F32)
        dsa = fw.new_dsem()
        fw.dma("sp", lambda e: e.dma_start(out=Sa[:, 0:128], in_=a_re_d.rearrange("(q g) p -> q (g p)", g=2)), writes=[B("s_Sa0")], dsem=dsa)
        fw.dma("sp", lambda e: e.dma_start(out=Sa[:, 128:256], in_=a_im_d.rearrange("(q g) p -> q (g p)", g=2)), writes=[B("s_Sa1")], dsem=dsa)
        fw.op("pe", lambda e: e.transpose(out=ps[6][:, 0:16], in_=Sa[0:16, 0:128], identity=ident_f[0:16, 0:16]), reads=[B("s_Sa0"), B("ident_f")], writes=[PSB[6]], signal=False)
        fw.op("pe", lambda e: e.transpose(out=ps[6][:, 16:32], in_=Sa[0:16, 128:256], identity=ident_f[0:16, 0:16]), reads=[B("s_Sa1"), B("ident_f")], writes=[PSB[6]], signal=True)
        fw.op("act", lambda e: e.activation(out=are[:], in_=ps[6][:, 0:16], func=AF.Copy), reads=[PSB[6]], writes=[B("s_are")])
        fw.op("act", lambda e: e.activation(out=aim[:], in_=ps[6][:, 16:32], func=AF.Copy), reads=[PSB[6]], writes=[B("s_aim")])
        ldv = ldt_d.rearrange("o (q g) -> o g q", g=2)
        for g in range(2):
            fw.dma("sp", lambda e, g=g: e.dma_start(out=dt[64 * g:64 * g + 64, :], in_=ldv[0:1, g, :].to_broadcast([64, 16])), writes=[B("s_dt%d" % g)], dsem=ds)
        phvP = arV.scope().__enter__()
        FFR = sb(phvP, "s_FFR", [128, NP, 160], F32); FFI = sb(phvP, "s_FFI", [128, NP, 160], F32)
        CnR = sb(phvP, "s_CnR", [32, NP, 128], F32)
        BrR = sb(phvP, "s_BrR", [128, NP, 32], F32); BrI = sb(phvP, "s_BrI", [128, NP, 32], F32)
        CnI = sb(ph, "s_CnI", [32, NP, 128], F32)
        CbR = sb(ph, "s_CbR", [128, NP, 32], F32); CbI = sb(ph, "s_CbI", [128, NP, 32], F32)
        BbR = sb(ph, "s_BbR", [128, NP, 32], F32); BbI = sb(ph, "s_BbI", [128, NP, 32], F32)
        p1 = sb(ph, "s_p1", [128, NP, 32], F32); p2 = sb(ph, "s_p2", [128, NP, 32], F32)
        for t_, tn in ((BrR, "s_BrR"), (BrI, "s_BrI"), (CnR, "s_CnR"), (CnI, "s_CnI")):
            mset(t_[:], 0.0, tn)
        bvR = b_re_d.rearrange("(q g) p c -> g p q c", g=2)
        bvI = b_im_d.rearrange("(q g) p c -> g p q c", g=2)
        cvR = c_re_d.rearrange("(q g) c p -> g c q p", g=2)
        cvI = c_im_d.rearrange("(q g) c p -> g c q p", g=2)
        ds2 = fw.new_dsem()
        for g in range(2):
            fw.dma("sp", lambda e, g=g: e.dma_start(out=BrR[64 * g:64 * g + 64, :, 16 * g:16 * g + 16], in_=bvR[g]), reads=[B("s_BrR")], writes=[B("s_BrR_%d" % g)], dsem=ds2)
            fw.dma("sp", lambda e, g=g: e.dma_start(out=BrI[64 * g:64 * g + 64, :, 16 * g:16 * g + 16], in_=bvI[g]), reads=[B("s_BrI")], writes=[B("s_BrI_%d" % g)], dsem=ds2)
            fw.dma("sp", lambda e, g=g: e.dma_start(out=CnR[16 * g:16 * g + 16, :, 64 * g:64 * g + 64], in_=cvR[g]), reads=[B("s_CnR")], writes=[B("s_CnR_%d" % g)], dsem=ds2)
            fw.dma("sp", lambda e, g=g: e.dma_start(out=CnI[16 * g:16 * g + 16, :, 64 * g:64 * g + 64], in_=cvI[g]), reads=[B("s_CnI")], writes=[B("s_CnI_%d" % g)], dsem=ds2)
        BRR = [B("s_BrR_0"), B("s_BrR_1")]; BRI = [B("s_BrI_0"), B("s_BrI_1")]
        CNR = [B("s_CnR_0"), B("s_CnR_1")]; CNI = [B("s_CnI_0"), B("s_CnI_1")]
        fw.op("act", lambda e: e.activation(out=dt[:], in_=dt[:], func=AF.Exp), reads=[B("s_dt0"), B("s_dt1")], writes=[B("s_dt")])
        tt(t16[0][:], are[:], dt[:], ALU.mult, [B("s_are"), B("s_dt")], [B("s_t0")])
        fw.op("act", lambda e: e.activation(out=mag[:], in_=t16[0][:], func=AF.Exp), reads=[B("s_t0")], writes=[B("s_mag")])
        tt(th[:], aim[:], dt[:], ALU.mult, [B("s_aim"), B("s_dt")], [B("s_th")])

        def sincos(ang, angn, outc, outcn, outs, outsn, tu, tun, tnf, tnfn, tint, tintn, tu2, tu2n):
            for (shift, out, outn, u_, un_) in ((0.25, outc, outcn, tu, tun), (0.0, outs, outsn, tu2, tu2n)):
                fw.op("dve", lambda e, shift=shift, u_=u_: e.tensor_scalar(out=u_, in0=ang, scalar1=1.0 / TWO_PI, scalar2=shift, op0=ALU.mult, op1=ALU.add),
                      reads=[B(angn)], writes=[B(un_)])
                fw.op("dve", lambda e, u_=u_: e.tensor_copy(out=tint, in_=u_), reads=[B(un_)], writes=[B(tintn)])
                fw.op("dve", lambda e: e.tensor_copy(out=tnf, in_=tint), reads=[B(tintn)], writes=[B(tnfn)])
                tt(u_, u_, tnf, ALU.subtract, [B(un_), B(tnfn)], [B(un_)])
                fw.op("act", lambda e, out=out, u_=u_: e.activation(out=out, in_=u_, func=AF.Sin, scale=TWO_PI), reads=[B(un_)], writes=[B(outn)])

        tcos = sb(ph, "s_tcos", [128, 16], F32); tsin = sb(ph, "s_tsin", [128, 16], F32)
        sincos(th[:], "s_th", tcos[:], "s_tcos", tsin[:], "s_tsin", t16[0][:], "s_t0", t16[3][:], "s_t3", ti16[:], "s_ti", t16[1][:], "s_t1")
        mset(lamR[:, 0, :], 1.0, "s_lamR0"); mset(lamI[:, 0, :], 0.0, "s_lamI0")
        mset(muR[:, 0, :], 1.0, "s_muR0"); mset(muI[:, 0, :], 0.0, "s_muI0")
        tt(lamR[:, 1, :], mag[:], tcos[:], ALU.mult, [B("s_mag"), B("s_tcos")], [B("s_lamR")])
        tt(lamI[:, 1, :], mag[:], tsin[:], ALU.mult, [B("s_mag"), B("s_tsin")], [B("s_lamI")])
        tt(t16[0][:], mag[:], mag[:], ALU.mult, [B("s_mag")], [B("s_t0")])
        fw.op("dve", lambda e: e.reciprocal(out=t16[3][:], in_=t16[0][:]), reads=[B("s_t0")], writes=[B("s_t3")])
        tt(muR[:, 1, :], lamR[:, 1, :], t16[3][:], ALU.mult, [B("s_lamR"), B("s_t3")], [B("s_muR")])
        fw.op("dve", lambda e: e.scalar_tensor_tensor(out=muI[:, 1, :], in0=lamI[:, 1, :], scalar=-1.0, in1=t16[3][:], op0=ALU.mult, op1=ALU.mult),
              reads=[B("s_lamI"), B("s_t3")], writes=[B("s_muI")])

        def cmul(oR, oI, oRn, oIn, aR, aI, aRn, aIn, bR, bI, bRn, bIn):
            tt(t16[0][:], aR, bR, ALU.mult, [B(aRn), B(bRn)], [B("s_t0")])
            tt(t16[1][:], aI, bI, ALU.mult, [B(aIn), B(bIn)], [B("s_t1")])
            tt(t16[2][:], aR, bI, ALU.mult, [B(aRn), B(bIn)], [B("s_t2")])
            tt(t16[3][:], aI, bR, ALU.mult, [B(aIn), B(bRn)], [B("s_t3")])
            tt(oR, t16[0][:], t16[1][:], ALU.subtract, [B("s_t0"), B("s_t1")], [B(oRn)])
            tt(oI, t16[2][:], t16[3][:], ALU.add, [B("s_t2"), B("s_t3")], [B(oIn)])

        for n in range(2, 5):
            cmul(lamR[:, n, :], lamI[:, n, :], "s_lamR", "s_lamI", lamR[:, n - 1, :], lamI[:, n - 1, :], "s_lamR", "s_lamI", lamR[:, 1, :], lamI[:, 1, :], "s_lamR", "s_lamI")
        for n in range(2, 4):
            cmul(muR[:, n, :], muI[:, n, :], "s_muR", "s_muI", muR[:, n - 1, :], muI[:, n - 1, :], "s_muR", "s_muI", muR[:, 1, :], muI[:, 1, :], "s_muR", "s_muI")
        tt(t16[0][:], mag[:], mag[:], ALU.mult, [B("s_mag")], [B("s_t0")])
        tt(rho4[:], t16[0][:], t16[0][:], ALU.mult, [B("s_t0")], [B("rho4")])
        fw.op("dve", lambda e: e.tensor_scalar(out=t16[0][:], in0=lamR[:, 1, :], scalar1=-1.0, scalar2=None, op0=ALU.add), reads=[B("s_lamR")], writes=[B("s_t0")])
        tt(t16[1][:], are[:], are[:], ALU.mult, [B("s_are")], [B("s_t1")])
        tt(t16[2][:], aim[:], aim[:], ALU.mult, [B("s_aim")], [B("s_t2")])
        tt(t16[1][:], t16[1][:], t16[2][:], ALU.add, [B("s_t1"), B("s_t2")], [B("s_t1")])
        fw.op("dve", lambda e: e.reciprocal(out=t16[3][:], in_=t16[1][:]), reads=[B("s_t1")], writes=[B("s_t3")])
        tt(t16[1][:], t16[0][:], are[:], ALU.mult, [B("s_t0"), B("s_are")], [B("s_t1")])
        tt(t16[2][:], lamI[:, 1, :], aim[:], ALU.mult, [B("s_lamI"), B("s_aim")], [B("s_t2")])
        tt(t16[1][:], t16[1][:], t16[2][:], ALU.add, [B("s_t1"), B("s_t2")], [B("s_t1")])
        tt(fR[:], t16[1][:], t16[3][:], ALU.mult, [B("s_t1"), B("s_t3")], [B("s_fR")])
        tt(t16[1][:], lamI[:, 1, :], are[:], ALU.mult, [B("s_lamI"), B("s_are")], [B("s_t1")])
        tt(t16[2][:], t16[0][:], aim[:], ALU.mult, [B("s_t0"), B("s_aim")], [B("s_t2")])
        tt(t16[1][:], t16[1][:], t16[2][:], ALU.subtract, [B("s_t1"), B("s_t2")], [B("s_t1")])
        tt(fI[:], t16[1][:], t16[3][:], ALU.mult, [B("s_t1"), B("s_t3")], [B("s_fI")])
        LAMR = [B("s_lamR"), B("s_lamR0")]; LAMI = [B("s_lamI"), B("s_lamI0")]; MUR = [B("s_muR"), B("s_muR0")]; MUI = [B("s_muI"), B("s_muI0")]

        with arK.scope() as phk:
            ang = sb(phk, "s_ang", [128, NP, 128], F32); tu = sb(phk, "s_tu", [128, NP, 128], F32)
            tnf = sb(phk, "s_tnf", [128, NP, 128], F32); tint = sb(phk, "s_tint", [128, NP, 128], I32)
            with arena.scope() as phv:
                tu2 = sb(phv, "s_tu2", [128, NP, 128], F32)
                tt(ang[:], th[:].unsqueeze(2).to_broadcast([128, NP, 128]), kk[:].unsqueeze(1).to_broadcast([128, NP, 128]), ALU.mult,
                   [B("s_th"), B("kk")], [B("s_ang")])
                sincos(ang[:], "s_ang", Ctab[:], "Ctab", Stab[:], "Stab", tu[:], "s_tu", tnf[:], "s_tnf", tint[:], "s_tint", tu2[:], "s_tu2")
                fw.barrier(engs=CE, with_sp=True)

        with arK.scope() as phk:
            EER = sb(phk, "s_EER", [128, NP, 128], F32); EEIn = sb(phk, "s_EEIn", [128, NP, 128], F32)
            E3R = sb(phk, "s_E3R", [128, NP, 128], F32); E3I = sb(phk, "s_E3I", [128, NP, 128], F32)
            bc = lambda t, n=None: (t if n is None else t[:, n, :]).unsqueeze(2).to_broadcast([128, NP, 32])

            def cmul_b(oR, oI, oRb, oIb, sR, sI, sRb, sIb, xR, xI, xRb, xIb, neg_im=False):
                tt(p1[:], xR, sR, ALU.mult, xRb + sRb, [B("s_p1")])
                tt(p2[:], xI, sI, ALU.mult, xIb + sIb, [B("s_p2")])
                tt(oR, p1[:], p2[:], ALU.subtract, [B("s_p1"), B("s_p2")], oRb)
                tt(p1[:], xI, sR, ALU.mult, xIb + sRb, [B("s_p1")])
                tt(p2[:], xR, sI, ALU.mult, xRb + sIb, [B("s_p2")])
                if neg_im:
                    fw.op("dve", lambda e: e.scalar_tensor_tensor(out=oI, in0=p1[:], scalar=-1.0, in1=p2[:], op0=ALU.mult, op1=ALU.subtract),
                          reads=[B("s_p1"), B("s_p2")], writes=oIb)
                else:
                    tt(oI, p1[:], p2[:], ALU.add, [B("s_p1"), B("s_p2")], oIb)

            cmul_b(BbR[:], BbI[:], [B("s_BbR")], [B("s_BbI")], bc(fR[:]), bc(fI[:]), [B("s_fR")], [B("s_fI")], BrR[:], BrI[:], BRR, BRI)
            for (Cn, Cnb, Cb, Cbn) in ((CnR, CNR, CbR, "s_CbR"), (CnI, CNI, CbI, "s_CbI")):
                bi = nbk()
                for qq in range(NP):
                    fw.op("pe", lambda e, Cn=Cn, qq=qq, bi=bi: e.transpose(out=ps[bi][:, qq * 32:(qq + 1) * 32], in_=Cn[0:32, qq, :], identity=ident_f[0:32, 0:32]),
                          reads=Cnb + [B("ident_f")], writes=[PSB[bi]], signal=(qq == NP - 1))
                fw.op("act", lambda e, Cb=Cb, bi=bi: e.activation(out=Cb[:].rearrange("p a c -> p (a c)"), in_=ps[bi][:, 0:NP * 32], func=AF.Copy), reads=[PSB[bi]], writes=[B(Cbn)])
            for n in range(5):
                cmul_b(FFR[:, :, 32 * n:32 * n + 32], FFI[:, :, 32 * n:32 * n + 32], [B("s_FFR")], [B("s_FFI")], bc(lamR, n), bc(lamI, n), LAMR, LAMI,
                       CbR[:], CbI[:], [B("s_CbR")], [B("s_CbI")])
            fw.op("act", lambda e: e.activation(out=CmRe[:], in_=FFR[:, :, 32:160], func=AF.Copy), reads=[B("s_FFR")], writes=[B("CmRe")])
            fw.op("act", lambda e: e.activation(out=CmIm[:], in_=FFI[:, :, 32:160], func=AF.Copy, scale=-1.0), reads=[B("s_FFI")], writes=[B("CmIm")])
            for j in range(4):
                cmul_b(EER[:, :, 32 * j:32 * j + 32], EEIn[:, :, 32 * j:32 * j + 32], [B("s_EER")], [B("s_EEIn")], bc(muR, j), bc(muI, j), MUR, MUI,
                       BbR[:], BbI[:], [B("s_BbR")], [B("s_BbI")], neg_im=True)
                cmul_b(E3R[:, :, 32 * j:32 * j + 32], E3I[:, :, 32 * j:32 * j + 32], [B("s_E3R")], [B("s_E3I")], bc(lamR, 3 - j), bc(lamI, 3 - j), LAMR, LAMI,
                       BbR[:], BbI[:], [B("s_BbR")], [B("s_BbI")])
            for q in range(NP):
                bi = nbk()
                fw.op("pe", lambda e, q=q, bi=bi: e.matmul(ps[bi][:, 0:128], lhsT=EER[:, q, :], rhs=FFR[:, q, 0:128], start=True, stop=False),
                      reads=[B("s_EER"), B("s_FFR")], writes=[PSB[bi]], signal=False)
                fw.op("pe", lambda e, q=q, bi=bi: e.matmul(ps[bi][:, 0:128], lhsT=EEIn[:, q, :], rhs=FFI[:, q, 0:128], start=False, stop=True),
                      reads=[B("s_EEIn"), B("s_FFI")], writes=[PSB[bi]], signal=False)
                fw.op("pe", lambda e, q=q, bi=bi: e.transpose(out=ps[bi][:, 128:256], in_=E3R[:, q, :], identity=ident_f[:]),
                      reads=[B("s_E3R"), B("ident_f")], writes=[PSB[bi]], signal=False)
                fw.op("pe", lambda e, q=q, bi=bi: e.transpose(out=ps[bi][:, 256:384], in_=E3I[:, q, :], identity=ident_f[:]),
                      reads=[B("s_E3I"), B("ident_f")], writes=[PSB[bi]], signal=True)
                tt(Gm[:, q, :], ps[bi][:, 0:128], maskG[:], ALU.mult, [PSB[bi], B("maskG")], [B("Gm")])
                fw.op("act", lambda e, q=q, bi=bi: e.activation(out=BcRe[:, q, :], in_=ps[bi][:, 128:256], func=AF.Copy), reads=[PSB[bi]], writes=[B("BcRe")])
                fw.op("act", lambda e, q=q, bi=bi: e.activation(out=BcIm[:, q, :], in_=ps[bi][:, 256:384], func=AF.Copy), reads=[PSB[bi]], writes=[B("BcIm")])
            fw.barrier(engs=CE, with_sp=True)
    fw.barrier(engs=CE, with_sp=True)


_CACHE = {}

_PARAM_SHAPES = {
    "norm_mix_pre": (1, D), "norm_mix_post": (1, D), "w_in": (D, 4096), "b_gate": (1, 2048),
    "ssm_a_re": (32, 64), "ssm_a_im": (32, 64), "ssm_log_dt": (1, 32), "ssm_b_re": (32, 64, 16), "ssm_b_im": (32, 64, 16),
    "ssm_c_re": (32, 16, 64), "ssm_c_im": (32, 16, 64), "ssm_d": (1, 512), "ssm_w_glu": (512, 512), "ssm_b_glu": (1, 512),
    "w_branch_attn": (512, D), "w_branch_ssm": (512, D), "w_out": (D, D), "norm_xa_pre": (1, D), "norm_xa_post": (1, D),
    "norm_mem": (1, D), "xa_wq": (D, D), "xa_wk": (D, D), "xa_wv": (D, D), "xa_wo": (D, D), "norm_ffn_pre": (1, D),
    "norm_ffn_post": (1, D), "ffn_w_up": (D, 2 * DFF), "ffn_conv_w": (3, 2 * DFF), "ffn_conv_b": (1, 2 * DFF), "ffn_w_down": (DFF, D),
}


def make_in_maps(inputs, ncores=8):
    params = {k: np.ascontiguousarray(np.asarray(inputs[k], dtype=np.float32).reshape(shp)) for k, shp in _PARAM_SHAPES.items()}
    x = np.asarray(inputs["x"], dtype=np.float32)
    mem = np.asarray(inputs["mem"], dtype=np.float32)
    maps = []
    for b in range(ncores):
        m = dict(params)
        m["x"] = np.ascontiguousarray(x[b])
        m["mem"] = np.ascontiguousarray(mem[b])
        maps.append(m)
    return maps


def kernel(**inputs):
    if "nc" not in _CACHE:
        _CACHE["nc"] = build()
    nc = _CACHE["nc"]
    in_maps = make_in_maps(inputs, 8)
    res = run_bass_kernel_spmd(nc, in_maps, core_ids=list(range(8)))
    out = np.stack([np.asarray(res.results[b]["out"], dtype=np.float32) for b in range(8)], axis=0)
    return out
```
